# Optimizing a Trainium2 kernel written in Bass

```python
import math
import jax, jax.numpy as jnp
from jax import lax
import numpy as np

D_MODEL = 1024
BATCH = 2
SEQ = 8192
DEPTH = 4


N_EVEN = (DEPTH + 1) // 2
N_ODD = DEPTH // 2
N_MEM = 256
RMS_EPS = 1e-6

D_POOL = D_MODEL // 2
POOL_WINDOWS = (2, 4, 8, 16)
N_POOL_GROUPS = len(POOL_WINDOWS)
POOL_GROUP_DIM = D_POOL // N_POOL_GROUPS
D_CONV = D_MODEL // 2
CONV_HEADS = 8
CONV_WIDTH = 3
EVEN_IN = D_POOL + 3 * D_CONV
EVEN_MIX = D_POOL + D_CONV

D_S5 = D_MODEL // 2
S5_GROUP_DIM = 16
S5_GROUPS = D_S5 // S5_GROUP_DIM
S5_STATE = 64
S5_DT_MIN = 1e-3
S5_DT_MAX = 1e-1
D_HYENA = D_MODEL // 2
HYENA_HEADS = 8
HYENA_ORDER = 2
HYENA_BANDS = 16
HYENA_EMB = 2 * HYENA_BANDS + 1
HYENA_FFN = 64
HYENA_TARGET = 1e-2
HYENA_SHORT_DECAY_PCT = 0.3
HYENA_LONG_DECAY_PCT = 1.5
ODD_IN = D_S5 + (HYENA_ORDER + 1) * D_HYENA
ODD_MIX = D_S5 + D_HYENA

XA_HEADS = 4
XA_HEAD_DIM = D_MODEL // XA_HEADS
D_FF = 4 * D_MODEL

kernel_name = 'hybrid_pool_conv_s5_hyena_encoder'


def rms_norm(x, g):
    xf = x.astype(jnp.float32)
    y = xf * lax.rsqrt(jnp.mean(xf * xf, axis=-1, keepdims=True) + RMS_EPS)
    return (y * g.astype(jnp.float32)).astype(x.dtype)


def centred_conv3(u, w):
    up = jnp.pad(u, ((0, 0), (1, 1), (0, 0)))
    return w[0] * up[:, :-2] + w[1] * up[:, 1:-1] + w[2] * up[:, 2:]


def centred_window_mean(u, window):
    L = u.shape[1]
    h = window // 2
    uf = u.astype(jnp.float32)
    cs = jnp.cumsum(jnp.pad(uf, ((0, 0), (h + 1, h), (0, 0))), axis=1)
    s = cs[:, window:window + L] - cs[:, :L]
    t = jnp.arange(L)
    cnt = (jnp.minimum(t + h, L) - jnp.maximum(t - h, 0)).astype(jnp.float32)
    return (s / cnt[None, :, None]).astype(u.dtype)


def pool_mixer(u, w_group, scale):
    B, L, _ = u.shape
    ug = u.reshape(B, L, N_POOL_GROUPS, POOL_GROUP_DIM)
    pooled = jnp.stack([centred_window_mean(ug[:, :, g], win) for g, win in enumerate(POOL_WINDOWS)], axis=2) - ug
    y = jnp.einsum('blgc,gcd->blgd', pooled, w_group).reshape(B, L, D_POOL)
    return y * scale


def short_conv_mixer(b_gate, c_gate, h, conv_w):
    return b_gate * centred_conv3(c_gate * h, conv_w)


def _linear_recurrence(e1, e2):
    a1, b1 = e1
    a2, b2 = e2
    return a1 * a2, a2 * b1 + b2


def s5_mixer(u, lam_re, lam_im, log_dt, b_re, b_im, c_re, c_im, d_skip, w_glu):
    B, L, _ = u.shape
    f32 = jnp.float32
    uf = u.astype(f32)
    lam = lax.complex(jnp.minimum(lam_re.astype(f32), -1e-4), lam_im.astype(f32))
    dt = jnp.exp(log_dt.astype(f32))[..., None]
    lam_bar = jnp.exp(lam * dt)
    coef = (lam_bar - 1.0) / lam
    b_mat = lax.complex(b_re.astype(f32), b_im.astype(f32))
    c_mat = lax.complex(c_re.astype(f32), c_im.astype(f32))
    ug = uf.reshape(B, L, S5_GROUPS, S5_GROUP_DIM).astype(jnp.complex64)
    bu = jnp.einsum('blgh,gnh->blgn', ug, b_mat)
    ys = []
    for direction in range(2):
        a = jnp.broadcast_to(lam_bar[direction], bu.shape)
        _, states = lax.associative_scan(_linear_recurrence, (a, coef[direction] * bu), reverse=(direction == 1), axis=1)
        ys.append(jnp.einsum('blgn,ghn->blgh', states, c_mat[direction]).real)
    y = (ys[0] + ys[1]).reshape(B, L, D_S5) + d_skip.astype(f32) * uf
    g = jax.nn.gelu(y)
    out = g * jax.nn.sigmoid(g @ w_glu.astype(f32))
    return out.astype(u.dtype)


def hyena_filters(L, w1, b1, w2, b2, w3, freq):
    f32 = jnp.float32
    t_norm = jnp.linspace(0.0, 1.0, L, dtype=f32)[:, None]
    bands = jnp.linspace(1e-4, HYENA_BANDS - 1, HYENA_BANDS, dtype=f32)[None, :]
    ang = (2.0 * math.pi / L) * jnp.arange(L, dtype=f32)[:, None] * bands
    z = jnp.concatenate([t_norm, jnp.cos(ang), -jnp.sin(ang)], axis=-1)
    fr = freq.astype(f32)
    h = jnp.sin(fr * (z @ w1.astype(f32) + b1.astype(f32)))
    h = jnp.sin(fr * (h @ w2.astype(f32) + b2.astype(f32)))
    h = (h @ w3.astype(f32)).reshape(L, HYENA_ORDER, 2, D_HYENA)
    deltas = jnp.abs(jnp.linspace(math.log(HYENA_TARGET) / HYENA_LONG_DECAY_PCT, math.log(HYENA_TARGET) / HYENA_SHORT_DECAY_PCT, D_HYENA, dtype=f32))
    h = h * jnp.exp(-t_norm * deltas)[:, None, None, :]
    h = h / (jnp.sum(jnp.abs(h), axis=(0, 2), keepdims=True) + 1e-6)
    fwd = h[:, :, 0]
    bwd = h[:, :, 1]
    k = jnp.concatenate([fwd[:1] + bwd[:1], fwd[1:], jnp.zeros_like(fwd[:1]), bwd[:0:-1]], axis=0)
    return jnp.fft.rfft(k, n=2 * L, axis=0)


def fft_long_conv(u, k_f, bias):
    L = u.shape[1]
    uf = u.astype(jnp.float32)
    y = jnp.fft.irfft(jnp.fft.rfft(uf, n=2 * L, axis=1) * k_f[None], n=2 * L, axis=1)[:, :L]
    return (y + uf * bias.astype(jnp.float32)).astype(u.dtype)


def hyena_mixer(p, short_w, short_b, w1, b1, w2, b2, w3, freq, bias):
    L = p.shape[1]
    p = centred_conv3(p, short_w) + short_b
    g_out, g_mid, v = jnp.split(p, 3, axis=-1)
    k_f = hyena_filters(L, w1, b1, w2, b2, w3, freq)
    z = g_mid * fft_long_conv(v, k_f[:, 0], bias[0])
    return g_out * fft_long_conv(z, k_f[:, 1], bias[1])


def memory_cross_attention(xn, memn, wq, wk, wv, wo):
    B, L, _ = xn.shape
    q = (xn @ wq).reshape(B, L, XA_HEADS, XA_HEAD_DIM)
    k = (memn @ wk).reshape(B, -1, XA_HEADS, XA_HEAD_DIM)
    v = (memn @ wv).reshape(B, -1, XA_HEADS, XA_HEAD_DIM)
    s = jnp.einsum('blhd,bmhd->bhlm', q, k).astype(jnp.float32) * (XA_HEAD_DIM ** -0.5)
    pr = jax.nn.softmax(s, axis=-1).astype(v.dtype)
    o = jnp.einsum('bhlm,bmhd->blhd', pr, v).reshape(B, L, D_MODEL)
    return o @ wo


def squared_relu_mlp(xn, w1, w2):
    h = jax.nn.relu(xn @ w1)
    return (h * h) @ w2


def setup_inputs(seed: int = 0) -> dict:
    key = jax.random.key(seed)
    keys = iter(jax.random.split(key, 48))
    f32 = jnp.float32

    def normal(shape, scale):
        return scale * jax.random.normal(next(keys), shape, f32)

    def gain(shape):
        return 1.0 + normal(shape, 0.05)

    x = normal((BATCH, SEQ, D_MODEL), 1.0)
    mem = normal((BATCH, N_MEM, D_MODEL), 1.0)
    norm_mix = gain((DEPTH, 2, D_MODEL))
    norm_xattn = gain((DEPTH, 2, D_MODEL))
    norm_mem = gain((DEPTH, D_MODEL))
    norm_mlp = gain((DEPTH, 2, D_MODEL))
    xa_wq = normal((DEPTH, D_MODEL, D_MODEL), D_MODEL ** -0.5)
    xa_wk = normal((DEPTH, D_MODEL, D_MODEL), D_MODEL ** -0.5)
    xa_wv = normal((DEPTH, D_MODEL, D_MODEL), D_MODEL ** -0.5)
    xa_wo = normal((DEPTH, D_MODEL, D_MODEL), D_MODEL ** -0.5)
    mlp_w1 = normal((DEPTH, D_MODEL, D_FF), D_MODEL ** -0.5)
    mlp_w2 = normal((DEPTH, D_FF, D_MODEL), D_FF ** -0.5)
    ev_w_in = normal((N_EVEN, D_MODEL, EVEN_IN), D_MODEL ** -0.5)
    ev_pool_w = normal((N_EVEN, N_POOL_GROUPS, POOL_GROUP_DIM, POOL_GROUP_DIM), POOL_GROUP_DIM ** -0.5)
    ev_pool_scale = gain((N_EVEN, D_POOL))
    ev_conv_w = normal((N_EVEN, CONV_WIDTH, D_CONV), CONV_WIDTH ** -0.5)
    ev_w_out = normal((N_EVEN, EVEN_MIX, D_MODEL), EVEN_MIX ** -0.5)
    od_w_in = normal((N_ODD, D_MODEL, ODD_IN), D_MODEL ** -0.5)
    od_s5_lambda_re = -0.5 + normal((N_ODD, 2, S5_GROUPS, S5_STATE), 0.01)
    od_s5_lambda_im = math.pi * jnp.arange(S5_STATE, dtype=f32) + normal((N_ODD, 2, S5_GROUPS, S5_STATE), 0.01)
    od_s5_log_dt = jax.random.uniform(next(keys), (N_ODD, 2, S5_GROUPS), f32, math.log(S5_DT_MIN), math.log(S5_DT_MAX))
    od_s5_b_re = normal((N_ODD, S5_GROUPS, S5_STATE, S5_GROUP_DIM), (2 * S5_GROUP_DIM) ** -0.5)
    od_s5_b_im = normal((N_ODD, S5_GROUPS, S5_STATE, S5_GROUP_DIM), (2 * S5_GROUP_DIM) ** -0.5)
    od_s5_c_re = normal((N_ODD, 2, S5_GROUPS, S5_GROUP_DIM, S5_STATE), (2 * S5_STATE) ** -0.5)
    od_s5_c_im = normal((N_ODD, 2, S5_GROUPS, S5_GROUP_DIM, S5_STATE), (2 * S5_STATE) ** -0.5)
    od_s5_d = normal((N_ODD, D_S5), 1.0)
    od_s5_w_glu = normal((N_ODD, D_S5, D_S5), D_S5 ** -0.5)
    od_hy_short_w = normal((N_ODD, CONV_WIDTH, (HYENA_ORDER + 1) * D_HYENA), CONV_WIDTH ** -0.5)
    od_hy_short_b = normal((N_ODD, (HYENA_ORDER + 1) * D_HYENA), 0.02)
    od_hy_w1 = normal((N_ODD, HYENA_EMB, HYENA_FFN), HYENA_EMB ** -0.5)
    od_hy_b1 = normal((N_ODD, HYENA_FFN), 0.1)
    od_hy_w2 = normal((N_ODD, HYENA_FFN, HYENA_FFN), HYENA_FFN ** -0.5)
    od_hy_b2 = normal((N_ODD, HYENA_FFN), 0.1)
    od_hy_w3 = normal((N_ODD, HYENA_FFN, HYENA_ORDER * 2 * D_HYENA), HYENA_FFN ** -0.5)
    od_hy_freq = gain((N_ODD, HYENA_FFN))
    od_hy_bias = normal((N_ODD, HYENA_ORDER, D_HYENA), 1.0)
    od_w_out = normal((N_ODD, ODD_MIX, D_MODEL), ODD_MIX ** -0.5)
    return {
        'x': x, 'mem': mem,
        'norm_mix': norm_mix, 'norm_xattn': norm_xattn, 'norm_mem': norm_mem, 'norm_mlp': norm_mlp,
        'xa_wq': xa_wq, 'xa_wk': xa_wk, 'xa_wv': xa_wv, 'xa_wo': xa_wo,
        'mlp_w1': mlp_w1, 'mlp_w2': mlp_w2,
        'ev_w_in': ev_w_in, 'ev_pool_w': ev_pool_w, 'ev_pool_scale': ev_pool_scale,
        'ev_conv_w': ev_conv_w, 'ev_w_out': ev_w_out,
        'od_w_in': od_w_in, 'od_s5_lambda_re': od_s5_lambda_re, 'od_s5_lambda_im': od_s5_lambda_im,
        'od_s5_log_dt': od_s5_log_dt, 'od_s5_b_re': od_s5_b_re, 'od_s5_b_im': od_s5_b_im,
        'od_s5_c_re': od_s5_c_re, 'od_s5_c_im': od_s5_c_im, 'od_s5_d': od_s5_d, 'od_s5_w_glu': od_s5_w_glu,
        'od_hy_short_w': od_hy_short_w, 'od_hy_short_b': od_hy_short_b,
        'od_hy_w1': od_hy_w1, 'od_hy_b1': od_hy_b1, 'od_hy_w2': od_hy_w2, 'od_hy_b2': od_hy_b2,
        'od_hy_w3': od_hy_w3, 'od_hy_freq': od_hy_freq, 'od_hy_bias': od_hy_bias,
        'od_w_out': od_w_out,
    }


def reference(x, mem, norm_mix, norm_xattn, norm_mem, norm_mlp, xa_wq, xa_wk, xa_wv, xa_wo, mlp_w1, mlp_w2, ev_w_in, ev_pool_w, ev_pool_scale, ev_conv_w, ev_w_out, od_w_in, od_s5_lambda_re, od_s5_lambda_im, od_s5_log_dt, od_s5_b_re, od_s5_b_im, od_s5_c_re, od_s5_c_im, od_s5_d, od_s5_w_glu, od_hy_short_w, od_hy_short_b, od_hy_w1, od_hy_b1, od_hy_w2, od_hy_b2, od_hy_w3, od_hy_freq, od_hy_bias, od_w_out):
    for i in range(DEPTH):
        j = i // 2
        h = rms_norm(x, norm_mix[i, 0])
        if i % 2 == 0:
            p = h @ ev_w_in[j]
            b_gate, c_gate, hv = jnp.split(p[..., D_POOL:], 3, axis=-1)
            y_pool = pool_mixer(p[..., :D_POOL], ev_pool_w[j], ev_pool_scale[j])
            y_conv = short_conv_mixer(b_gate, c_gate, hv, ev_conv_w[j])
            mix = jnp.concatenate([y_pool, y_conv], axis=-1) @ ev_w_out[j]
        else:
            p = h @ od_w_in[j]
            y_s5 = s5_mixer(p[..., :D_S5], od_s5_lambda_re[j], od_s5_lambda_im[j], od_s5_log_dt[j], od_s5_b_re[j], od_s5_b_im[j], od_s5_c_re[j], od_s5_c_im[j], od_s5_d[j], od_s5_w_glu[j])
            y_hy = hyena_mixer(p[..., D_S5:], od_hy_short_w[j], od_hy_short_b[j], od_hy_w1[j], od_hy_b1[j], od_hy_w2[j], od_hy_b2[j], od_hy_w3[j], od_hy_freq[j], od_hy_bias[j])
            mix = jnp.concatenate([y_s5, y_hy], axis=-1) @ od_w_out[j]
        x = x + rms_norm(mix, norm_mix[i, 1])
        h = rms_norm(x, norm_xattn[i, 0])
        m = rms_norm(mem, norm_mem[i])
        x = x + rms_norm(memory_cross_attention(h, m, xa_wq[i], xa_wk[i], xa_wv[i], xa_wo[i]), norm_xattn[i, 1])
        h = rms_norm(x, norm_mlp[i, 0])
        x = x + rms_norm(squared_relu_mlp(h, mlp_w1[i], mlp_w2[i]), norm_mlp[i, 1])
    return x
```

```python
import math
from contextlib import ExitStack

import numpy as np
import ml_dtypes

import concourse.bass as bass
import concourse.mybir as mybir
from concourse.ap import AP
from concourse.bass_utils import run_bass_kernel_spmd

F32 = mybir.dt.float32
BF16 = mybir.dt.bfloat16
ALU = mybir.AluOpType
AF = mybir.ActivationFunctionType

NCORES = 8
D = 1024
B = 2
L = 8192
NTOK = B * L
NT = NTOK // NCORES
DEPTH = 4
NMEM = 256
EPS = 1e-6
MAGIC = 12582912.0
TWO_PI = 2.0 * math.pi


class Prog:
    def __init__(self, nc, st):
        self.nc = nc
        self.st = st
        self.eng = {"pe": nc.tensor, "dve": nc.vector, "act": nc.scalar, "pool": nc.gpsimd, "sp": nc.sync}
        self.cnt = {e: 0 for e in self.eng}
        self.csem = {e: st.enter_context(nc.semaphore("cs_" + e)) for e in self.eng}
        self.seen = {e: {} for e in self.eng}
        self.lastw = {}
        self.readers = {}
        self.dsem = {}
        self.final = []
        self.nsb = 0
        self.cur = st
        self.ncc = 0

    def sb(self, shape, dt, name=None, st=None):
        self.nsb += 1
        self.nsb += 1
        return (st or self.cur).enter_context(self.nc.sbuf_tensor("s%d_" % self.nsb + (name or "t"), list(shape), dt))

    def ps(self, shape, dt=F32, name=None, st=None):
        self.nsb += 1
        self.nsb += 1
        return (st or self.cur).enter_context(self.nc.psum_tensor("p%d_" % self.nsb + (name or "t"), list(shape), dt))

    def op(self, e, fn, r=(), w=(), dma=None, final=False):
        deps = {}
        mysem = None
        if dma is not None:
            if dma not in self.dsem:
                self.dsem[dma] = [self.st.enter_context(self.nc.semaphore("ds%d" % len(self.dsem))), 0]
            mysem = self.dsem[dma][0]

        def need(tok):
            if tok is None:
                return
            sem, val, src = tok
            if src == "pe" and e == "pe":
                return
            if sem is mysem:
                return
            k = id(sem)
            if k not in deps or deps[k][1] < val:
                deps[k] = (sem, val)

        for k in r:
            need(self.lastw.get(k))
        for k in w:
            need(self.lastw.get(k))
            for t in self.readers.get(k, ()):
                need(t)
        eng = self.eng[e]
        for k, (sem, val) in deps.items():
            if self.seen[e].get(k, 0) < val:
                eng.wait_ge(sem, val)
                self.seen[e][k] = val
        if dma is None:
            self.cnt[e] += 1
            tok = (self.csem[e], self.cnt[e], e)
            fn(eng).then_inc(self.csem[e], 1)
        else:
            d = self.dsem[dma]
            d[1] += 16
            tok = (d[0], d[1], "dma")
            fn(eng).then_inc(d[0], 16)
        for k in r:
            self.readers.setdefault(k, []).append(tok)
        for k in w:
            self.lastw[k] = tok
            self.readers[k] = []
        if final:
            self.final.append(tok)
        return tok

    def dma(self, e, out, in_, r, w, key, final=False, **kw):
        return self.op(e, lambda g: g.dma_start(out=out, in_=in_, **kw), r=r, w=w, dma=key, final=final)

    def allgather(self, src, dst, r, w):
        sem = self.st.enter_context(self.nc.semaphore("cc%d" % self.ncc))
        self.ncc += 1
        if not hasattr(self, "ccd"):
            self.ccd = self.st.enter_context(self.nc.sbuf_tensor("s_ccdummy", [128, 4], F32))

        def fn(g):
            g.collective_compute("AllGather", ALU.bypass, replica_groups=[list(range(NCORES))], ins=[src], outs=[dst]).then_inc(sem)
            g.wait_ge(sem, 1)
            return g.memset(self.ccd[:], 0.0)

        return self.op("pool", fn, r=r, w=list(w) + ["ccdummy"])

    def finish(self):
        eng = self.eng["sp"]
        for sem, val, _ in self.final:
            eng.wait_ge(sem, val)
        for e in ("pe", "dve", "act", "pool"):
            if self.cnt[e]:
                eng.wait_ge(self.csem[e], self.cnt[e])


def keys(name, n):
    return [(name, i) for i in range(n)]


def load_weight(P, dram, sbt, name, nk):
    for k in range(nk):
        P.dma("pool", sbt[:, k, :], dram[:, k, :], r=[], w=[name], key=name)


class DenseCtx:
    def __init__(self, P, T):
        self.P = P
        self.T = T
        self.ones = P.sb([128, 128], BF16, "ones")
        P.op("dve", lambda e: e.memset(self.ones[:], 1.0), w=["ones"])
        self.eps = P.sb([128, 1], F32, "epsc")
        P.op("dve", lambda e: e.memset(self.eps[:], EPS), w=["eps"])
        self.ps_stat = [P.ps([128, T], F32, "ps_stat%d" % i) for i in range(1)]
        self.nstat = 0
        self.lnt = P.sb([128, T], F32, "lnt")

    def rstd_from_sq(self, sq_fn, sq_keys, nchunks, rstd, rstd_key):
        P = self.P
        ps = self.ps_stat[0]
        pk = "ps_stat0"
        for k in range(nchunks):
            P.op("pe", lambda e, k=k: e.matmul(ps[:], self.ones[:], sq_fn(k), start=(k == 0), stop=(k == nchunks - 1)),
                 r=["ones", sq_keys[k]], w=[pk])
        P.op("act", lambda e: e.activation(out=self.lnt[:], in_=ps[:], func=AF.Ln, bias=self.eps[:, 0:1], scale=1.0 / D),
             r=[pk, "eps"], w=["lnt"])
        P.op("act", lambda e: e.activation(out=rstd, in_=self.lnt[:], func=AF.Exp, scale=-0.5),
             r=["lnt"], w=[rstd_key])


def emit_prenorm(C, xt, xkeys, gT, gkey, sq, sqname, h, hname, rstd, rstdkey):
    P = C.P
    for k in range(8):
        P.op("act", lambda e, k=k: e.activation(out=sq[:, k, :], in_=xt[:, k, :], func=AF.Square),
             r=[xkeys[k]], w=[(sqname, k)])
    C.rstd_from_sq(lambda k: sq[:, k, :], keys(sqname, 8), 8, rstd[:], rstdkey)
    for k in range(8):
        P.op("dve", lambda e, k=k: e.scalar_tensor_tensor(out=h[:, k, :], in0=xt[:, k, :], scalar=gT[:, k:k + 1], in1=rstd[:],
                                                          op0=ALU.mult, op1=ALU.mult),
             r=[xkeys[k], gkey, rstdkey], w=[(hname, k)])


def emit_postnorm_residual(C, mo, moname, sq, sqname, gT, gkey, xt, xkeys, rstd, rstdkey, tmp, tmpname):
    P = C.P
    C.rstd_from_sq(lambda k: sq[:, k, :], keys(sqname, 8), 8, rstd[:], rstdkey)
    for k in range(8):
        P.op("dve", lambda e, k=k: e.scalar_tensor_tensor(out=tmp[:, k % 2, :], in0=mo[:, k, :], scalar=gT[:, k:k + 1], in1=rstd[:],
                                                          op0=ALU.mult, op1=ALU.mult),
             r=[(moname, k), gkey, rstdkey], w=[(tmpname, k % 2)])
        P.op("pool", lambda e, k=k: e.tensor_tensor(out=xt[:, k, :], in0=xt[:, k, :], in1=tmp[:, k % 2, :], op=ALU.add),
             r=[xkeys[k], (tmpname, k % 2)], w=[xkeys[k]])


def emit_proj(P, psb, psname, w_sb, wname, nk, m, rhs_fn, rhs_keys, T):
    for k in range(nk):
        P.op("pe", lambda e, k=k: e.matmul(psb, w_sb[:, k, 128 * m:128 * m + 128], rhs_fn(k), start=(k == 0), stop=(k == nk - 1)),
             r=[wname, rhs_keys[k]], w=[psname])


def emit_PA(P, io, T=512):
    nc = P.nc
    xT = io["xT"]
    w = io["w"]
    g = io["g"]
    pT = io["pT"]
    with ExitStack() as st:
        P.cur = st
        C = DenseCtx(P, T)
        wsb = P.sb([128, 8, 2048], BF16, "w_in")
        load_weight(P, w, wsb, "w_in", 8)
        gT = P.sb([128, 8], F32, "gT")
        P.dma("sp", gT[:], g, r=[], w=["gT"], key="gT")
        xts = [P.sb([128, 8, T], F32, "xt%d" % i) for i in range(2)]
        sq = P.sb([128, 8, T], BF16, "sq")
        hs = [P.sb([128, 8, T], BF16, "h%d" % i) for i in range(2)]
        rstd = P.sb([128, T], F32, "rstd")
        stage = [P.sb([128, 4, T], BF16, "stage%d" % i) for i in range(2)]
        pso = [P.ps([128, T], F32, "pso%d" % i) for i in range(4)]
        xv = xT.rearrange("(k p) n -> p k n", p=128)
        pv = pT.rearrange("(m p) n -> p m n", p=128)
        nev = 0
        for t in range(NT // T):
            j = t % 2
            xt = xts[j]
            xk = keys("xt%d" % j, 8)
            P.dma("sp", xt[:], xv[:, :, t * T:(t + 1) * T], r=[], w=xk, key="xt%d" % j)
            emit_prenorm(C, xt, xk, gT, "gT", sq, "sq", hs[j], "h%d" % j, rstd, "rstd")
            hk = keys("h%d" % j, 8)
            for mg in range(4):
                sj = (t * 4 + mg) % 2
                for mi in range(4):
                    m = mg * 4 + mi
                    pb = pso[m % 4]
                    emit_proj(P, pb[:], "pso%d" % (m % 4), wsb, "w_in", 8, m, lambda k: hs[j][:, k, :], hk, T)
                    eng = "act" if nev % 2 == 0 else "dve"
                    nev += 1
                    if eng == "act":
                        P.op("act", lambda e, pb=pb, mi=mi: e.activation(out=stage[sj][:, mi, :], in_=pb[:], func=AF.Copy),
                             r=["pso%d" % (m % 4)], w=[("stage%d" % sj, mi)])
                    else:
                        P.op("dve", lambda e, pb=pb, mi=mi: e.tensor_copy(out=stage[sj][:, mi, :], in_=pb[:]),
                             r=["pso%d" % (m % 4)], w=[("stage%d" % sj, mi)])
                P.dma("sp", pv[:, mg * 4:mg * 4 + 4, t * T:(t + 1) * T], stage[sj][:], r=keys("stage%d" % sj, 4), w=[],
                      key="stage%d" % sj, final=True)
        P.barrier()
        P.cur = P.st


def emit_PC2(P, io, T=256):
    nc = P.nc
    xT = io["xT"]
    w1 = io["w1"]
    w2 = io["w2"]
    g0 = io["g0"]
    g1 = io["g1"]
    xoT = io["xoT"]
    with ExitStack() as st:
        P.cur = st
        C = DenseCtx(P, T)
        w1s = P.sb([128, 8, 4096], BF16, "w1")
        w2s = P.sb([128, 32, 1024], BF16, "w2")
        load_weight(P, w1, w1s, "w1", 8)
        load_weight(P, w2, w2s, "w2", 32)
        g0T = P.sb([128, 8], F32, "g0T")
        g1T = P.sb([128, 8], F32, "g1T")
        P.dma("sp", g0T[:], g0, r=[], w=["g0T"], key="g0T")
        P.dma("sp", g1T[:], g1, r=[], w=["g1T"], key="g1T")
        xts = [P.sb([128, 8, T], F32, "xt%d" % i) for i in range(2)]
        sq = P.sb([128, 8, T], BF16, "sq")
        h = P.sb([128, 8, T], BF16, "h")
        rstd = P.sb([128, T], F32, "rstd")
        a = P.sb([128, 32, T], BF16, "a")
        rl = [P.sb([128, T], F32, "rl%d" % i) for i in range(2)]
        mo = P.sb([128, 8, T], F32, "mo")
        tmp = P.sb([128, 2, T], F32, "tmp")
        psu = [P.ps([128, T], F32, "psu%d" % i) for i in range(3)]
        psd = [P.ps([128, T], F32, "psd%d" % i) for i in range(2)]
        xv = xT.rearrange("(k p) n -> p k n", p=128)
        ov = xoT.rearrange("(k p) n -> p k n", p=128)
        for t in range(NT // T):
            j = t % 2
            xt = xts[j]
            xk = keys("xt%d" % j, 8)
            P.dma("sp", xt[:], xv[:, :, t * T:(t + 1) * T], r=[], w=xk, key="xt%d" % j)
            emit_prenorm(C, xt, xk, g0T, "g0T", sq, "sq", h, "h", rstd, "rstd")
            hk = keys("h", 8)
            for f in range(32):
                pb = psu[f % 3]
                pn = "psu%d" % (f % 3)
                emit_proj(P, pb[:], pn, w1s, "w1", 8, f, lambda k: h[:, k, :], hk, T)
                rj = f % 2
                P.op("act", lambda e, pb=pb, rj=rj: e.activation(out=rl[rj][:], in_=pb[:], func=AF.Relu), r=[pn], w=["rl%d" % rj])
                veng = "dve" if f % 2 == 0 else "pool"
                P.op(veng, lambda e, rj=rj, f=f: e.tensor_tensor(out=a[:, f, :], in0=rl[rj][:], in1=rl[rj][:], op=ALU.mult),
                     r=["rl%d" % rj], w=[("a", f)])
            ak = keys("a", 32)
            for m in range(8):
                pb = psd[m % 2]
                pn = "psd%d" % (m % 2)
                emit_proj(P, pb[:], pn, w2s, "w2", 32, m, lambda k: a[:, k, :], ak, T)
                P.op("act", lambda e, pb=pb, m=m: e.activation(out=mo[:, m, :], in_=pb[:], func=AF.Copy), r=[pn], w=[("mo", m)])
                P.op("act", lambda e, pb=pb, m=m: e.activation(out=sq[:, m, :], in_=pb[:], func=AF.Square), r=[pn], w=[("sq", m)])
            emit_postnorm_residual(C, mo, "mo", sq, "sq", g1T, "g1T", xt, xk, rstd, "rstd", tmp, "tmp")
            P.dma("sp", ov[:, :, t * T:(t + 1) * T], xt[:], r=xk, w=[], key="xo%d" % j, final=True)
        P.barrier()
        P.cur = P.st


def wlay(w):
    kk = w.shape[0] // 128
    return np.ascontiguousarray(w.reshape(kk, 128, w.shape[1]).transpose(1, 0, 2))


def glay(g):
    return np.ascontiguousarray(g.reshape(-1, 128).T)


_PROGS = {}


def prog(name, builder):
    if name not in _PROGS:
        _PROGS[name] = builder()
    return _PROGS[name]


def run(nc, in_maps):
    res = run_bass_kernel_spmd(nc, in_maps, core_ids=list(range(NCORES)))
    return res.results


def barrier(P):
    engs = ("pe", "dve", "act", "pool", "sp")
    for e in engs:
        eng = P.eng[e]
        for e2 in engs:
            if e2 != e and P.cnt[e2] > P.seen[e].get(id(P.csem[e2]), 0):
                eng.wait_ge(P.csem[e2], P.cnt[e2])
                P.seen[e][id(P.csem[e2])] = P.cnt[e2]
        for sem, cntv in P.dsem.values():
            if cntv > P.seen[e].get(id(sem), 0):
                eng.wait_ge(sem, cntv)
                P.seen[e][id(sem)] = cntv


Prog.barrier = barrier


def emit_PC1(P, io, odd, T=512):
    nc = P.nc
    din = lambda name, shape, dt=F32: io[name]
    xT = din("xT", [D, NT])
    y_all = io["y_all"]
    idxy_d = io["idx_y"]
    if odd:
        wglu_d = din("wglu", [128, 4, 512])
    memT = din("memT", [D, NMEM])
    wout_d = din("wout", [128, 8, 1024])
    wq_d = din("wq", [128, 8, 1024])
    wk_d = din("wk", [128, 8, 1024])
    wv_d = din("wv", [128, 8, 1024])
    wo_d = din("wo", [128, 8, 1024])
    gains_d = din("gains", [128, 4, 8])
    xoT = io["xoT"]
    with ExitStack() as st:
        P.cur = st
        C = DenseCtx(P, T)
        idxy = P.sb([128, 32], mybir.dt.int32, "idxy")
        P.dma("sp", idxy[:], idxy_d, r=[], w=["idxy"], key="idxy")
        gains = P.sb([128, 4, 8], F32, "gains")
        P.dma("sp", gains[:], gains_d, r=[], w=["gains"], key="gains")
        wout = P.sb([128, 8, 1024], BF16, "wout")
        wq = P.sb([128, 8, 1024], BF16, "wq")
        wo = P.sb([128, 8, 1024], BF16, "wo")
        kT = P.sb([128, 8, NMEM], BF16, "kT")
        vS = P.sb([128, 2, 1024], BF16, "vS")
        psg = [P.ps([128, T], F32, "psg%d" % i) for i in range(3)]
        pss = [P.ps([128, T], F32, "pss%d" % i) for i in range(2)]
        psd = P.ps([128, T], F32, "psden")
        pso = P.ps([128, T], F32, "pspv")
        with ExitStack() as st2:
            wk = P.sb([128, 8, 1024], BF16, "wk", st2)
            wv = P.sb([128, 8, 1024], BF16, "wv", st2)
            mt = P.sb([128, 8, NMEM], F32, "mt", st2)
            msq = P.sb([128, 8, NMEM], BF16, "msq", st2)
            mn = P.sb([128, 8, NMEM], BF16, "mn", st2)
            mrstd = P.sb([128, T], F32, "mrstd", st2)
            load_weight(P, wk_d, wk, "wk", 8)
            load_weight(P, wv_d, wv, "wv", 8)
            P.dma("sp", mt[:], memT.rearrange("(k p) n -> p k n", p=128), r=[], w=keys("mt", 8), key="mt")
            for k in range(8):
                P.op("act", lambda e, k=k: e.activation(out=msq[:, k, :], in_=mt[:, k, :], func=AF.Square), r=[("mt", k)], w=[("msq", k)])
            ps = C.ps_stat[0]
            for k in range(8):
                P.op("pe", lambda e, k=k: e.matmul(ps[:, 0:NMEM], C.ones[:], msq[:, k, :], start=(k == 0), stop=(k == 7)),
                     r=["ones", ("msq", k)], w=["ps_stat0"])
            P.op("act", lambda e: e.activation(out=C.lnt[:, 0:NMEM], in_=ps[:, 0:NMEM], func=AF.Ln, bias=C.eps[:, 0:1], scale=1.0 / D),
                 r=["ps_stat0", "eps"], w=["lnt"])
            P.op("act", lambda e: e.activation(out=mrstd[:, 0:NMEM], in_=C.lnt[:, 0:NMEM], func=AF.Exp, scale=-0.5), r=["lnt"], w=["mrstd"])
            for k in range(8):
                P.op("dve", lambda e, k=k: e.scalar_tensor_tensor(out=mn[:, k, :], in0=mt[:, k, :], scalar=gains[:, 3, k:k + 1],
                                                                  in1=mrstd[:, 0:NMEM], op0=ALU.mult, op1=ALU.mult),
                     r=[("mt", k), "gains", "mrstd"], w=[("mn", k)])
            for m in range(8):
                pb = psg[m % 3]
                pn = "psg%d" % (m % 3)
                for k in range(8):
                    P.op("pe", lambda e, k=k, m=m, pb=pb: e.matmul(pb[:, 0:NMEM], wk[:, k, 128 * m:128 * m + 128], mn[:, k, :],
                                                                   start=(k == 0), stop=(k == 7)),
                         r=["wk", ("mn", k)], w=[pn])
                P.op("act", lambda e, m=m, pb=pb: e.activation(out=kT[:, m, :], in_=pb[:, 0:NMEM], func=AF.Copy), r=[pn], w=[("kT", m)])
            for j in range(2):
                for dh in range(2):
                    i = j * 2 + dh
                    pb = psg[i % 3]
                    pn = "psg%d" % (i % 3)
                    for k in range(8):
                        P.op("pe", lambda e, k=k, j=j, dh=dh, pb=pb: e.matmul(pb[:, 0:512], mn[:, k, 128 * j:128 * j + 128],
                                                                            wv[:, k, 512 * dh:512 * dh + 512], start=(k == 0), stop=(k == 7)),
                             r=["wv", ("mn", k)], w=[pn])
                    P.op("act", lambda e, j=j, dh=dh, pb=pb: e.activation(out=vS[:, j, 512 * dh:512 * dh + 512], in_=pb[:, 0:512], func=AF.Copy),
                         r=[pn], w=[("vS", j, dh)])
            P.barrier()
        vkeys = lambda j: [("vS", j, 0), ("vS", j, 1)]
        load_weight(P, wout_d, wout, "wout", 8)
        load_weight(P, wq_d, wq, "wq", 8)
        load_weight(P, wo_d, wo, "wo", 8)
        if odd:
            wglu = P.sb([128, 4, 512], BF16, "wglu")
            load_weight(P, wglu_d, wglu, "wglu", 4)
            gt = [P.sb([128, T], F32, "gt%d" % i) for i in range(2)]
            gg = P.sb([128, 4, T], BF16, "gg")
            glu = P.sb([128, 4, T], BF16, "glu")
        xt = P.sb([128, 8, T], F32, "xt")
        yt = P.sb([128, 8, T], BF16, "yt")
        sq = P.sb([128, 8, T], BF16, "sq")
        h = P.sb([128, 8, T], BF16, "h")
        mo = P.sb([128, 8, T], F32, "mo")
        tmp = P.sb([128, 2, T], F32, "tmp")
        rstd = P.sb([128, T], F32, "rstd")
        qT = P.sb([128, 8, T], BF16, "qT")
        ee = [P.sb([128, 2, T], BF16, "ee%d" % i) for i in range(2)]
        rden = [P.sb([128, T], F32, "rden%d" % i) for i in range(2)]
        lnd = P.sb([128, T], F32, "lnd")
        oT = P.sb([128, 8, T], BF16, "oT")
        xv = xT.rearrange("(k p) n -> p k n", p=128)
        ov = xoT.rearrange("(k p) n -> p k n", p=128)
        xk = keys("xt", 8)
        for t in range(NT // T):
            cs = slice(t * T, (t + 1) * T)
            P.dma("sp", xt[:], xv[:, :, cs], r=[], w=xk, key="xt")
            for k in range(8):
                tok = P.op("pool", lambda g, k=k, t=t: g.indirect_dma_start(out=yt[:, k, :], out_offset=None, in_=y_all,
                                                                           in_offset=bass.IndirectOffsetOnAxis(ap=idxy[:, 4 * k + t:4 * k + t + 1], axis=0)),
                           r=["idxy", "y_all"], w=[("yt", k)], dma="yt")
            for k in range(8):
                P.lastw[("yt", k)] = tok
            rhs_keys = keys("yt", 8)
            rhs_fn = lambda k: yt[:, k, :]
            if odd:
                for c in range(4):
                    g0, g1 = gt[0], gt[1]
                    P.op("dve", lambda e, c=c: e.tensor_copy(out=g0[:], in_=yt[:, c, :]), r=[("yt", c)], w=["gt0"])
                    P.op("act", lambda e: e.activation(out=g1[:], in_=g0[:], func=AF.Square), r=["gt0"], w=["gt1"])
                    P.op("dve", lambda e: e.tensor_scalar(out=g1[:], in0=g1[:], scalar1=0.044715, scalar2=1.0, op0=ALU.mult, op1=ALU.add),
                         r=["gt1"], w=["gt1"])
                    P.op("dve", lambda e: e.tensor_tensor(out=g1[:], in0=g1[:], in1=g0[:], op=ALU.mult), r=["gt1", "gt0"], w=["gt1"])
                    P.op("act", lambda e: e.activation(out=g1[:], in_=g1[:], func=AF.Sigmoid, scale=1.5957691216057308), r=["gt1"], w=["gt1"])
                    P.op("dve", lambda e, c=c: e.tensor_tensor(out=gg[:, c, :], in0=g0[:], in1=g1[:], op=ALU.mult),
                         r=["gt0", "gt1"], w=[("gg", c)])
                for m in range(4):
                    pb = psg[m % 3]
                    pn = "psg%d" % (m % 3)
                    for c in range(4):
                        P.op("pe", lambda e, c=c, m=m, pb=pb: e.matmul(pb[:], wglu[:, c, 128 * m:128 * m + 128], gg[:, c, :],
                                                                       start=(c == 0), stop=(c == 3)),
                             r=["wglu", ("gg", c)], w=[pn])
                    P.op("act", lambda e, pb=pb: e.activation(out=gt[1][:], in_=pb[:], func=AF.Sigmoid), r=[pn], w=["gt1"])
                    P.op("dve", lambda e, m=m: e.tensor_tensor(out=glu[:, m, :], in0=gg[:, m, :], in1=gt[1][:], op=ALU.mult),
                         r=[("gg", m), "gt1"], w=[("glu", m)])
                rhs_keys = keys("glu", 4) + keys("yt", 8)[4:]
                rhs_fn = lambda k: (glu[:, k, :] if k < 4 else yt[:, k, :])

            def proj_postnorm(wsb, wname, rfn, rkeys, gidx):
                for m in range(8):
                    pb = psg[m % 3]
                    pn = "psg%d" % (m % 3)
                    emit_proj(P, pb[:], pn, wsb, wname, 8, m, rfn, rkeys, T)
                    P.op("act", lambda e, pb=pb, m=m: e.activation(out=mo[:, m, :], in_=pb[:], func=AF.Copy), r=[pn], w=[("mo", m)])
                    P.op("act", lambda e, pb=pb, m=m: e.activation(out=sq[:, m, :], in_=pb[:], func=AF.Square), r=[pn], w=[("sq", m)])
                emit_postnorm_residual(C, mo, "mo", sq, "sq", gains[:, gidx, :], "gains", xt, xk, rstd, "rstd", tmp, "tmp")

            proj_postnorm(wout, "wout", rhs_fn, rhs_keys, 0)
            emit_prenorm(C, xt, xk, gains[:, 1, :], "gains", sq, "sq", h, "h", rstd, "rstd")
            for m in range(8):
                pb = psg[m % 3]
                pn = "psg%d" % (m % 3)
                emit_proj(P, pb[:], pn, wq, "wq", 8, m, lambda k: h[:, k, :], keys("h", 8), T)
                P.op("act", lambda e, pb=pb, m=m: e.activation(out=qT[:, m, :], in_=pb[:], func=AF.Copy, scale=1.0 / 16.0), r=[pn], w=[("qT", m)])
            for hd in range(4):
                e2 = ee[hd % 2]
                en = "ee%d" % (hd % 2)
                for j in range(2):
                    pb = pss[j]
                    pn = "pss%d" % j
                    for dd in range(2):
                        dch = 2 * hd + dd
                        P.op("pe", lambda e, pb=pb, dch=dch, j=j, dd=dd: e.matmul(pb[:], kT[:, dch, 128 * j:128 * j + 128], qT[:, dch, :],
                                                                                 start=(dd == 0), stop=(dd == 1)),
                             r=[("kT", dch), ("qT", dch)], w=[pn])
                    P.op("act", lambda e, pb=pb, j=j, e2=e2: e.activation(out=e2[:, j, :], in_=pb[:], func=AF.Exp), r=[pn], w=[(en, j)])
                for j in range(2):
                    P.op("pe", lambda e, j=j, e2=e2: e.matmul(psd[:], C.ones[:], e2[:, j, :], start=(j == 0), stop=(j == 1)),
                         r=["ones", (en, j)], w=["psden"])
                rd = rden[hd % 2]
                rn = "rden%d" % (hd % 2)
                P.op("act", lambda e: e.activation(out=lnd[:], in_=psd[:], func=AF.Ln), r=["psden"], w=["lnd"])
                P.op("act", lambda e, rd=rd: e.activation(out=rd[:], in_=lnd[:], func=AF.Exp, scale=-1.0), r=["lnd"], w=[rn])
                for dd in range(2):
                    dch = 2 * hd + dd
                    for j in range(2):
                        P.op("pe", lambda e, j=j, dch=dch, e2=e2: e.matmul(pso[:], vS[:, j, 128 * dch:128 * dch + 128], e2[:, j, :],
                                                                          start=(j == 0), stop=(j == 1)),
                             r=vkeys(j) + [(en, j)], w=["pspv"])
                    P.op("dve", lambda e, dch=dch, rd=rd: e.tensor_tensor(out=oT[:, dch, :], in0=pso[:], in1=rd[:], op=ALU.mult),
                         r=["pspv", rn], w=[("oT", dch)])
            proj_postnorm(wo, "wo", lambda k: oT[:, k, :], keys("oT", 8), 2)
            P.dma("sp", ov[:, :, cs], xt[:], r=xk, w=[], key="xo", final=True)
        P.barrier()
        P.cur = P.st


def emit_PBE(P, io):
    nc = P.nc
    din = lambda name, shape, dt=F32: io[name]
    pu_d = din("pu", [128, L], BF16)
    poolw_d = din("poolw", [128, 1, 128])
    pscale_d = din("pscale", [128, 1])
    msel_d = din("msel", [4, L])
    cb_d = din("cb", [128, L], BF16)
    cc_d = din("cc", [128, L], BF16)
    ch_d = din("ch", [128, L], BF16)
    convw_d = din("convw", [128, 3])
    ypool_d = io["ypool"]
    yconv_d = io["yconv"]
    H = 8
    Wd = L + 2 * H
    BL = 2048
    with ExitStack() as st:
        P.cur = st
        psb = [P.ps([128, 512], F32, "psb%d" % i) for i in range(2)]
        with ExitStack() as st2:
            u = P.sb([128, Wd], BF16, "u", st2)
            A = P.sb([128, Wd], F32, "A", st2)
            Bf = P.sb([128, Wd], F32, "Bf", st2)
            acc = P.sb([128, L], F32, "acc", st2)
            mt = [P.sb([128, BL], F32, "mt%d" % i, st2) for i in range(2)]
            tmpb = P.sb([128, BL], F32, "tmpb", st2)
            pooled = P.sb([128, L], BF16, "pooled", st2)
            pw = P.sb([128, 1, 128], BF16, "pw", st2)
            psc = P.sb([128, 1], F32, "psc", st2)
            ost = [P.sb([128, 2048], BF16, "ost%d" % i, st2) for i in range(2)]
            load_weight(P, poolw_d, pw, "pw", 1)
            P.dma("sp", psc[:], pscale_d, r=[], w=["psc"], key="psc")
            P.op("dve", lambda e: e.memset(u[:, 0:H], 0.0), w=["u"])
            P.op("dve", lambda e: e.memset(u[:, Wd - H:Wd], 0.0), w=["u"])
            P.op("pool", lambda e: e.memset(A[:], 0.0), w=["A"])
            P.op("pool", lambda e: e.memset(Bf[:], 0.0), w=["Bf"])
            P.dma("sp", u[:, H:H + L], pu_d, r=[], w=["u"], key="u")
            nm = 0

            def accumulate(src, srcname, i):
                nonlocal nm
                for blk in range(L // BL):
                    m = mt[nm % 2]
                    mn_ = "mt%d" % (nm % 2)
                    nm += 1
                    P.dma("sp", m[:], msel_d[i:i + 1, blk * BL:(blk + 1) * BL].partition_broadcast(128), r=[], w=[mn_], key=mn_)
                    s_ = src[:, H + blk * BL:H + (blk + 1) * BL]
                    a_ = acc[:, blk * BL:(blk + 1) * BL]
                    if i == 0:
                        P.op("dve", lambda e, s_=s_, a_=a_, m=m: e.tensor_tensor(out=a_, in0=s_, in1=m[:], op=ALU.mult),
                             r=[srcname, mn_], w=[("acc", blk)])
                    else:
                        P.op("dve", lambda e, s_=s_, m=m: e.tensor_tensor(out=tmpb[:], in0=s_, in1=m[:], op=ALU.mult),
                             r=[srcname, mn_], w=["tmpb"])
                        P.op("pool", lambda e, a_=a_: e.tensor_tensor(out=a_, in0=a_, in1=tmpb[:], op=ALU.add),
                             r=["tmpb", ("acc", blk)], w=[("acc", blk)])

            P.op("dve", lambda e: e.tensor_tensor(out=A[:, 1:Wd], in0=u[:, 0:Wd - 1], in1=u[:, 1:Wd], op=ALU.add), r=["u"], w=["A"])
            accumulate(A, "A", 0)
            P.op("dve", lambda e: e.tensor_tensor(out=Bf[:, 2:Wd - 2], in0=A[:, 1:Wd - 3], in1=A[:, 3:Wd - 1], op=ALU.add), r=["A"], w=["Bf"])
            accumulate(Bf, "Bf", 1)
            P.op("dve", lambda e: e.tensor_tensor(out=A[:, 4:Wd - 4], in0=Bf[:, 2:Wd - 6], in1=Bf[:, 6:Wd - 2], op=ALU.add), r=["Bf"], w=["A"])
            accumulate(A, "A", 2)
            P.op("dve", lambda e: e.tensor_tensor(out=Bf[:, 8:Wd - 8], in0=A[:, 4:Wd - 12], in1=A[:, 12:Wd - 4], op=ALU.add), r=["A"], w=["Bf"])
            accumulate(Bf, "Bf", 3)
            for blk in range(L // BL):
                sl = slice(blk * BL, (blk + 1) * BL)
                P.op("dve", lambda e, sl=sl, blk=blk: e.tensor_tensor(out=pooled[:, sl], in0=acc[:, sl], in1=u[:, H + blk * BL:H + (blk + 1) * BL],
                                                                      op=ALU.subtract),
                     r=[("acc", blk), "u"], w=[("pooled", blk)])
            for blk in range(L // BL):
                o_ = ost[blk % 2]
                on = "ost%d" % (blk % 2)
                for s in range(BL // 512):
                    i = blk * 4 + s
                    pb = psb[i % 2]
                    pn = "psb%d" % (i % 2)
                    P.op("pe", lambda e, pb=pb, i=i: e.matmul(pb[:], pw[:, 0, :], pooled[:, i * 512:(i + 1) * 512], start=True, stop=True),
                         r=["pw", ("pooled", blk)], w=[pn])
                    P.op("act", lambda e, pb=pb, o_=o_, s=s: e.activation(out=o_[:, s * 512:(s + 1) * 512], in_=pb[:], func=AF.Copy, scale=psc[:, 0:1]),
                         r=[pn, "psc"], w=[(on, s)])
                P.dma("sp", ypool_d[:, blk * BL:(blk + 1) * BL], o_[:], r=keys(on, 4), w=[], key=on, final=True)
            P.barrier()
        with ExitStack() as st3:
            cb = P.sb([128, L], BF16, "cb", st3)
            cc = P.sb([128, L], BF16, "cc", st3)
            chh = P.sb([128, L], BF16, "chh", st3)
            cw = P.sb([128, 3], F32, "cw", st3)
            mm_ = P.sb([128, L + 2], F32, "mm_", st3)
            aa = P.sb([128, L], F32, "aa", st3)
            yo = P.sb([128, L], BF16, "yo", st3)
            P.dma("sp", cb[:], cb_d, r=[], w=["cb"], key="cb")
            P.dma("sp", cc[:], cc_d, r=[], w=["cc"], key="cc")
            P.dma("sp", chh[:], ch_d, r=[], w=["chh"], key="chh")
            P.dma("sp", cw[:], convw_d, r=[], w=["cw"], key="cw")
            P.op("pool", lambda e: e.memset(mm_[:, 0:1], 0.0), w=["mm_"])
            P.op("pool", lambda e: e.memset(mm_[:, L + 1:L + 2], 0.0), w=["mm_"])
            P.op("dve", lambda e: e.tensor_tensor(out=mm_[:, 1:L + 1], in0=cc[:], in1=chh[:], op=ALU.mult), r=["cc", "chh"], w=["mm_"])
            P.op("dve", lambda e: e.tensor_scalar(out=aa[:], in0=mm_[:, 1:L + 1], scalar1=cw[:, 1:2], scalar2=None, op0=ALU.mult),
                 r=["mm_", "cw"], w=["aa"])
            P.op("dve", lambda e: e.scalar_tensor_tensor(out=aa[:], in0=mm_[:, 0:L], scalar=cw[:, 0:1], in1=aa[:], op0=ALU.mult, op1=ALU.add),
                 r=["mm_", "cw", "aa"], w=["aa"])
            P.op("dve", lambda e: e.scalar_tensor_tensor(out=aa[:], in0=mm_[:, 2:L + 2], scalar=cw[:, 2:3], in1=aa[:], op0=ALU.mult, op1=ALU.add),
                 r=["mm_", "cw", "aa"], w=["aa"])
            P.op("pool", lambda e: e.tensor_tensor(out=yo[:], in0=aa[:], in1=cb[:], op=ALU.mult), r=["aa", "cb"], w=["yo"])
            P.dma("sp", yconv_d, yo[:], r=["yo"], w=[], key="yo", final=True)
            P.barrier()
        P.cur = P.st


def rev_ap(ap, n):
    a = ap
    return AP(a.tensor, a.offset + (n - 1), [list(a.ap[0]), [-1, n]])


def emit_PBS5(P, io):
    nc = P.nc
    din = lambda name, shape, dt=F32: io[name]
    u_d = din("u", [2, 32, 2, L], BF16)
    par_d = din("par", [2, 2, 128, 4])
    bmat_d = din("bmat", [2, 128, 2, 16])
    cmat_d = din("cmat", [2, 2, 128, 2, 16])
    dsk_d = din("dsk", [2, 32, 1])
    ident_d = din("ident", [128, 128])
    ys_d = io["ys"]
    ys0_d = io["ys0"]
    BLK = 2048
    NB = L // BLK
    with ExitStack() as st:
        P.cur = st
        ident = P.sb([128, 128], F32, "ident")
        P.dma("sp", ident[:], ident_d, r=[], w=["ident"], key="ident")
        tv = P.sb([128, BLK], F32, "tv")
        P.op("pool", lambda e: e.iota(tv[:], pattern=[[1, BLK]], base=0, channel_multiplier=0, allow_small_or_imprecise_dtypes=True), w=["tv"])
        trn = P.sb([128, BLK], F32, "trn")
        tq = P.sb([128, BLK], F32, "tq")
        sn = P.sb([128, BLK], F32, "sn")
        cs = P.sb([128, BLK], F32, "cs")
        wre = P.sb([128, BLK], F32, "wre")
        wim = P.sb([128, BLK], F32, "wim")
        xr = P.sb([128, BLK], F32, "xr")
        xi = P.sb([128, BLK], F32, "xi")
        mtmp = [P.sb([128, 512], F32, "mtmp%d" % i) for i in range(4)]
        d1 = P.sb([128, BLK], F32, "d1")
        d2 = P.sb([128, BLK], F32, "d2")
        xre16 = P.sb([128, BLK], BF16, "xre16")
        xim16 = P.sb([128, BLK], BF16, "xim16")
        ub = [P.sb([32, BLK], BF16, "ub%d" % i) for i in range(2)]
        ystage = [P.sb([32, BLK], BF16, "ystage%d" % i) for i in range(2)]
        carry = P.sb([128, 4], F32, "carry")
        yprev = P.sb([32, BLK], BF16, "yprev")
        par = P.sb([128, 4], F32, "par")
        bm = P.sb([128, 2, 16], F32, "bm")
        cm = P.sb([128, 2, 16], F32, "cm")
        sc = P.sb([128, 32], F32, "sc")
        cb16 = P.sb([128, 2, 16], F32, "cb16")
        t16 = P.sb([128, 2, 16], F32, "t16")
        bd = P.sb([128, 2, 32], F32, "bd")
        lb16 = P.sb([32, 2, 128], BF16, "lb16")
        cbd = P.sb([128, 2, 32], BF16, "cbd")
        dsk = P.sb([32, 1], F32, "dsk")
        ddiag = P.sb([32, 32], BF16, "ddiag")
        ps_ri = [[P.ps([128, 512], F32, "ps_ri%d%d" % (i, j)) for j in range(2)] for i in range(2)]
        ps_y = [P.ps([32, 512], F32, "ps_y%d" % i) for i in range(2)]
        ps_t = P.ps([32, 128], F32, "ps_t")
        col = lambda i: sc[:, i:i + 1]
        (LRE, DT, AA, RHO, TH, THT, RN, FR, SINT, COST, LBR, LBI, NUMR, MAG, INV, CR, CI, NCI, T1, T2, TC, LIM) = range(22)
        V = lambda fn, r, w: P.op("dve", fn, r=r, w=w)
        nys = 0
        nub = 0
        for gp in range(2):
            P.dma("sp", bm[:], bmat_d[gp], r=[], w=["bm"], key="bm")
            P.dma("sp", dsk[:], dsk_d[gp], r=[], w=["dsk"], key="dsk")
            V(lambda e: e.tensor_scalar(out=ddiag[:], in0=ident[0:32, 0:32], scalar1=dsk[:, 0:1], scalar2=None, op0=ALU.mult),
              ["ident", "dsk"], ["ddiag"])
            for d in range(2):
                P.dma("sp", par[:], par_d[gp, d], r=[], w=["par"], key="par")
                P.dma("sp", cm[:], cmat_d[gp, d], r=[], w=["cm"], key="cm")
                S = ["sc"]
                V(lambda e: e.tensor_scalar(out=col(LRE), in0=par[:, 0:1], scalar1=-1e-4, scalar2=None, op0=ALU.min), ["par"], S)
                V(lambda e: e.tensor_copy(out=col(LIM), in_=par[:, 1:2]), ["par"], S)
                P.op("act", lambda e: e.activation(out=col(DT), in_=par[:, 2:3], func=AF.Exp), r=["par"], w=S)
                V(lambda e: e.tensor_tensor(out=col(AA), in0=col(LRE), in1=col(DT), op=ALU.mult), S, S)
                P.op("act", lambda e: e.activation(out=col(RHO), in_=col(AA), func=AF.Exp), r=S, w=S)
                V(lambda e: e.tensor_tensor(out=col(TH), in0=col(LIM), in1=col(DT), op=ALU.mult), S, S)
                V(lambda e: e.tensor_scalar(out=col(THT), in0=col(TH), scalar1=1.0 / TWO_PI, scalar2=None, op0=ALU.mult), S, S)
                V(lambda e: e.tensor_scalar(out=col(RN), in0=col(THT), scalar1=MAGIC, scalar2=MAGIC, op0=ALU.add, op1=ALU.subtract), S, S)
                V(lambda e: e.tensor_tensor(out=col(FR), in0=col(THT), in1=col(RN), op=ALU.subtract), S, S)
                P.op("act", lambda e: e.activation(out=col(SINT), in_=col(FR), func=AF.Sin, scale=TWO_PI), r=S, w=S)
                V(lambda e: e.tensor_scalar(out=col(TC), in0=col(THT), scalar1=0.25, scalar2=None, op0=ALU.add), S, S)
                V(lambda e: e.tensor_scalar(out=col(RN), in0=col(TC), scalar1=MAGIC, scalar2=MAGIC, op0=ALU.add, op1=ALU.subtract), S, S)
                V(lambda e: e.tensor_tensor(out=col(FR), in0=col(TC), in1=col(RN), op=ALU.subtract), S, S)
                P.op("act", lambda e: e.activation(out=col(COST), in_=col(FR), func=AF.Sin, scale=TWO_PI), r=S, w=S)
                V(lambda e: e.tensor_tensor(out=col(LBR), in0=col(RHO), in1=col(COST), op=ALU.mult), S, S)
                V(lambda e: e.tensor_tensor(out=col(LBI), in0=col(RHO), in1=col(SINT), op=ALU.mult), S, S)
                V(lambda e: e.tensor_scalar(out=col(NUMR), in0=col(LBR), scalar1=-1.0, scalar2=None, op0=ALU.add), S, S)
                V(lambda e: e.tensor_tensor(out=col(MAG), in0=col(LRE), in1=col(LRE), op=ALU.mult), S, S)
                V(lambda e: e.tensor_tensor(out=col(T1), in0=col(LIM), in1=col(LIM), op=ALU.mult), S, S)
                V(lambda e: e.tensor_tensor(out=col(MAG), in0=col(MAG), in1=col(T1), op=ALU.add), S, S)
                V(lambda e: e.reciprocal(out=col(INV), in_=col(MAG)), S, S)
                V(lambda e: e.tensor_tensor(out=col(T1), in0=col(NUMR), in1=col(LRE), op=ALU.mult), S, S)
                V(lambda e: e.tensor_tensor(out=col(T2), in0=col(LBI), in1=col(LIM), op=ALU.mult), S, S)
                V(lambda e: e.tensor_tensor(out=col(T1), in0=col(T1), in1=col(T2), op=ALU.add), S, S)
                V(lambda e: e.tensor_tensor(out=col(CR), in0=col(T1), in1=col(INV), op=ALU.mult), S, S)
                V(lambda e: e.tensor_tensor(out=col(T1), in0=col(LBI), in1=col(LRE), op=ALU.mult), S, S)
                V(lambda e: e.tensor_tensor(out=col(T2), in0=col(NUMR), in1=col(LIM), op=ALU.mult), S, S)
                V(lambda e: e.tensor_tensor(out=col(T1), in0=col(T1), in1=col(T2), op=ALU.subtract), S, S)
                V(lambda e: e.tensor_tensor(out=col(CI), in0=col(T1), in1=col(INV), op=ALU.mult), S, S)
                V(lambda e: e.tensor_scalar(out=col(NCI), in0=col(CI), scalar1=-1.0, scalar2=None, op0=ALU.mult), S, S)
                V(lambda e: e.tensor_scalar(out=t16[:, 0, :], in0=bm[:, 0, :], scalar1=col(CR), scalar2=None, op0=ALU.mult), ["bm"] + S, ["t16"])
                V(lambda e: e.scalar_tensor_tensor(out=cb16[:, 0, :], in0=bm[:, 1, :], scalar=col(NCI), in1=t16[:, 0, :], op0=ALU.mult, op1=ALU.add),
                  ["bm", "t16"] + S, ["cb16"])
                V(lambda e: e.tensor_scalar(out=t16[:, 1, :], in0=bm[:, 1, :], scalar1=col(CR), scalar2=None, op0=ALU.mult), ["bm"] + S, ["t16"])
                V(lambda e: e.scalar_tensor_tensor(out=cb16[:, 1, :], in0=bm[:, 0, :], scalar=col(CI), in1=t16[:, 1, :], op0=ALU.mult, op1=ALU.add),
                  ["bm", "t16"] + S, ["cb16"])
                V(lambda e: e.memset(bd[:], 0.0), [], ["bd"])
                for ri in range(2):
                    V(lambda e, ri=ri: e.tensor_copy(out=bd[0:64, ri, 0:16], in_=cb16[0:64, ri, :]), ["cb16"], ["bd"])
                    V(lambda e, ri=ri: e.tensor_copy(out=bd[64:128, ri, 16:32], in_=cb16[64:128, ri, :]), ["cb16"], ["bd"])
                for ri in range(2):
                    P.op("pe", lambda e, ri=ri: e.transpose(out=ps_t[:], in_=bd[:, ri, :], identity=ident[:]), r=["bd", "ident"], w=["ps_t"])
                    V(lambda e, ri=ri: e.tensor_copy(out=lb16[:, ri, :], in_=ps_t[:]), ["ps_t"], [("lb16", ri)])
                V(lambda e: e.memset(cbd[:], 0.0), [], ["cbd"])
                V(lambda e: e.tensor_copy(out=cbd[0:64, 0, 0:16], in_=cm[0:64, 0, :]), ["cm"], ["cbd"])
                V(lambda e: e.tensor_copy(out=cbd[64:128, 0, 16:32], in_=cm[64:128, 0, :]), ["cm"], ["cbd"])
                V(lambda e: e.tensor_scalar(out=cbd[0:64, 1, 0:16], in0=cm[0:64, 1, :], scalar1=-1.0, scalar2=None, op0=ALU.mult), ["cm"], ["cbd"])
                V(lambda e: e.tensor_scalar(out=cbd[64:128, 1, 16:32], in0=cm[64:128, 1, :], scalar1=-1.0, scalar2=None, op0=ALU.mult), ["cm"], ["cbd"])
                sgn = 1.0 if d == 0 else -1.0
                order = list(range(NB)) if d == 0 else list(range(NB - 1, -1, -1))
                for oi, bi in enumerate(order):
                    t0 = bi * BLK
                    V(lambda e, t0=t0: e.tensor_scalar(out=trn[:], in0=tv[:], scalar1=float(t0), scalar2=col(THT), op0=ALU.add, op1=ALU.mult),
                      ["tv"] + S, ["trn"])
                    V(lambda e: e.tensor_scalar(out=tq[:], in0=trn[:], scalar1=MAGIC, scalar2=MAGIC, op0=ALU.add, op1=ALU.subtract), ["trn"], ["tq"])
                    V(lambda e: e.tensor_tensor(out=tq[:], in0=trn[:], in1=tq[:], op=ALU.subtract), ["trn", "tq"], ["tq"])
                    P.op("act", lambda e, sgn=sgn: e.activation(out=sn[:], in_=tq[:], func=AF.Sin, scale=sgn * TWO_PI), r=["tq"], w=["sn"])
                    V(lambda e: e.tensor_scalar(out=trn[:], in0=trn[:], scalar1=0.25, scalar2=None, op0=ALU.add), ["trn"], ["trn"])
                    V(lambda e: e.tensor_scalar(out=tq[:], in0=trn[:], scalar1=MAGIC, scalar2=MAGIC, op0=ALU.add, op1=ALU.subtract), ["trn"], ["tq"])
                    V(lambda e: e.tensor_tensor(out=tq[:], in0=trn[:], in1=tq[:], op=ALU.subtract), ["trn", "tq"], ["tq"])
                    P.op("act", lambda e: e.activation(out=cs[:], in_=tq[:], func=AF.Sin, scale=TWO_PI), r=["tq"], w=["cs"])
                    for b in range(2):
                        uj = nub % 2
                        nub += 1
                        ubt = ub[uj]
                        un = "ub%d" % uj
                        P.dma("sp", ubt[:], u_d[gp, :, b, t0:t0 + BLK], r=[], w=[un], key=un)
                        for s4 in range(4):
                            c_ = slice(s4 * 512, (s4 + 1) * 512)
                            pr, pi = ps_ri[s4 % 2]
                            prn, pin = "ps_r%d" % (s4 % 2), "ps_i%d" % (s4 % 2)
                            P.op("pe", lambda e, pr=pr, c_=c_, ubt=ubt: e.matmul(pr[:], lb16[:, 0, :], ubt[:, c_], start=True, stop=True),
                                 r=[("lb16", 0), un], w=[prn])
                            P.op("pe", lambda e, pi=pi, c_=c_, ubt=ubt: e.matmul(pi[:], lb16[:, 1, :], ubt[:, c_], start=True, stop=True),
                                 r=[("lb16", 1), un], w=[pin])
                            V(lambda e, pr=pr, c_=c_: e.tensor_tensor(out=mtmp[0][:], in0=pr[:], in1=cs[:, c_], op=ALU.mult), [prn, "cs"], ["mtmp0"])
                            V(lambda e, pi=pi, c_=c_: e.tensor_tensor(out=mtmp[1][:], in0=pi[:], in1=sn[:, c_], op=ALU.mult), [pin, "sn"], ["mtmp1"])
                            P.op("pool", lambda e, c_=c_: e.tensor_tensor(out=wre[:, c_], in0=mtmp[0][:], in1=mtmp[1][:], op=ALU.add),
                                 r=["mtmp0", "mtmp1"], w=[("wre", s4)])
                            V(lambda e, pi=pi, c_=c_: e.tensor_tensor(out=mtmp[2][:], in0=pi[:], in1=cs[:, c_], op=ALU.mult), [pin, "cs"], ["mtmp2"])
                            V(lambda e, pr=pr, c_=c_: e.tensor_tensor(out=mtmp[3][:], in0=pr[:], in1=sn[:, c_], op=ALU.mult), [prn, "sn"], ["mtmp3"])
                            P.op("pool", lambda e, c_=c_: e.tensor_tensor(out=wim[:, c_], in0=mtmp[2][:], in1=mtmp[3][:], op=ALU.subtract),
                                 r=["mtmp2", "mtmp3"], w=[("wim", s4)])
                        rho_bc = col(RHO).to_broadcast([128, BLK])
                        for (wt, wn, xt_, xn, ci_) in ((wre, "wre", xr, "xr", 2 * b), (wim, "wim", xi, "xi", 2 * b + 1)):
                            init = 0.0 if oi == 0 else carry[:, ci_:ci_ + 1]
                            if d == 0:
                                dat, out_ = wt[:], xt_[:]
                                last = xt_[:, BLK - 1:BLK]
                            else:
                                dat, out_ = rev_ap(wt[:], BLK), rev_ap(xt_[:], BLK)
                                last = xt_[:, 0:1]
                            V(lambda e, dat=dat, out_=out_, init=init: e.tensor_tensor_scan(out=out_, data0=rho_bc, data1=dat, initial=init,
                                                                                         op0=ALU.mult, op1=ALU.add),
                              keys(wn, 4) + S + ["carry"], [xn])
                            V(lambda e, last=last, ci_=ci_: e.tensor_copy(out=carry[:, ci_:ci_ + 1], in_=last), [xn], ["carry"])
                        P.op("pool", lambda e: e.tensor_tensor(out=d1[:], in0=xr[:], in1=cs[:], op=ALU.mult), r=["xr", "cs"], w=["d1"])
                        V(lambda e: e.tensor_tensor(out=d2[:], in0=xi[:], in1=sn[:], op=ALU.mult), ["xi", "sn"], ["d2"])
                        P.op("pool", lambda e: e.tensor_tensor(out=xre16[:], in0=d1[:], in1=d2[:], op=ALU.subtract), r=["d1", "d2"], w=["xre16"])
                        P.op("pool", lambda e: e.tensor_tensor(out=d1[:], in0=xr[:], in1=sn[:], op=ALU.mult), r=["xr", "sn", "xre16"], w=["d1"])
                        V(lambda e: e.tensor_tensor(out=d2[:], in0=xi[:], in1=cs[:], op=ALU.mult), ["xi", "cs", "xre16"], ["d2"])
                        P.op("pool", lambda e: e.tensor_tensor(out=xim16[:], in0=d1[:], in1=d2[:], op=ALU.add), r=["d1", "d2"], w=["xim16"])
                        yj = nys % 2
                        nys += 1
                        yst = ystage[yj]
                        yn = "ystage%d" % yj
                        for s4 in range(4):
                            c_ = slice(s4 * 512, (s4 + 1) * 512)
                            py = ps_y[s4 % 2]
                            pyn = "ps_y%d" % (s4 % 2)
                            P.op("pe", lambda e, py=py, c_=c_: e.matmul(py[:], cbd[:, 0, :], xre16[:, c_], start=True, stop=False),
                                 r=["cbd", "xre16"], w=[pyn])
                            P.op("pe", lambda e, py=py, c_=c_: e.matmul(py[:], cbd[:, 1, :], xim16[:, c_], start=False, stop=(d == 1)),
                                 r=["cbd", "xim16"], w=[pyn])
                            if d == 0:
                                P.op("pe", lambda e, py=py, c_=c_, ubt=ubt: e.matmul(py[:], ddiag[:], ubt[:, c_], start=False, stop=True),
                                     r=["ddiag", un], w=[pyn])
                            P.op("act", lambda e, py=py, c_=c_, yst=yst: e.activation(out=yst[:, c_], in_=py[:], func=AF.Copy), r=[pyn], w=[(yn, s4)])
                        if d == 0:
                            P.dma("sp", ys0_d[gp, :, b, t0:t0 + BLK], yst[:], r=keys(yn, 4), w=[("ys0", gp, b, bi)], key=yn)
                        else:
                            P.dma("sp", yprev[:], ys0_d[gp, :, b, t0:t0 + BLK], r=[("ys0", gp, b, bi)], w=["yprev"], key="yprev")
                            V(lambda e, yst=yst: e.tensor_tensor(out=yst[:], in0=yst[:], in1=yprev[:], op=ALU.add), keys(yn, 4) + ["yprev"], keys(yn, 4))
                            P.dma("sp", ys_d[gp, :, b, t0:t0 + BLK], yst[:], r=keys(yn, 4), w=[], key=yn, final=True)
        P.barrier()
        P.cur = P.st


def s5_inputs(u, lam_re, lam_im, log_dt, b_re, b_im, c_re, c_im, dsk):
    ident = np.eye(128, dtype=np.float32)
    ims = []
    for c in range(NCORES):
        g0 = 4 * c
        uu = None if u is None else np.ascontiguousarray(u[:, :, 64 * c:64 * c + 64].reshape(B, L, 2, 32).transpose(2, 3, 0, 1))
        par = np.zeros((2, 2, 128, 4), np.float32)
        bmat = np.zeros((2, 128, 2, 16), np.float32)
        cmat = np.zeros((2, 2, 128, 2, 16), np.float32)
        for gp in range(2):
            for g2 in range(2):
                g = g0 + 2 * gp + g2
                ps = slice(64 * g2, 64 * g2 + 64)
                bmat[gp, ps, 0] = b_re[g]
                bmat[gp, ps, 1] = b_im[g]
                for d in range(2):
                    par[gp, d, ps, 0] = lam_re[d, g]
                    par[gp, d, ps, 1] = lam_im[d, g]
                    par[gp, d, ps, 2] = log_dt[d, g]
                    cmat[gp, d, ps, 0] = c_re[d, g].T
                    cmat[gp, d, ps, 1] = c_im[d, g].T
        ims.append({"u": uu, "par": par, "bmat": bmat, "cmat": cmat,
                    "dsk": np.ascontiguousarray(dsk[64 * c:64 * c + 64].reshape(2, 32, 1)), "ident": ident})
    return ims


def s5_outputs(res):
    outs = []
    for c in range(NCORES):
        ys = np.asarray(res[c]["ys"])
        outs.append(ys.transpose(0, 3, 4, 1, 2).reshape(2, B, L, 64))
    return np.concatenate(outs, axis=-1)


def fview(ap, dims):
    return AP(ap.tensor, ap.offset, [list(ap.ap[0])] + [list(d) for d in dims])


NFFT = 2 * L


def hyena_tables():
    j = np.arange(128)
    ang = 2.0 * np.pi * np.outer(j, j) / 128.0
    Wr, Wi = np.cos(ang), -np.sin(ang)
    W1s = np.concatenate([np.concatenate([Wr[:64], Wi[:64]], 1), np.concatenate([-Wi[:64], Wr[:64]], 1)], 0)
    W1k = np.concatenate([Wr, Wi], 1)
    Wcr, Wci = np.cos(ang), np.sin(ang)
    W1i = np.stack([np.concatenate([Wcr, Wci], 1), np.concatenate([-Wci, Wcr], 1)], 1)
    tl = np.arange(128)[:, None, None]
    fl = np.arange(128)[None, :, None]
    fh = np.arange(128)[None, None, :]
    a = 2.0 * np.pi * ((tl * (fl + 128 * fh)) % NFFT) / NFFT
    Ef = np.stack([np.cos(a), -np.sin(a)], 2)
    ta = np.arange(64)[None, None, :]
    g = 2.0 * np.pi * ((tl * (fl + 128 * ta)) % NFFT) / NFFT
    Gr, Gi = np.cos(g), np.sin(g)
    Ei = np.stack([np.concatenate([Gr, Gi], 2), np.concatenate([-Gi, Gr], 2)], 2)
    bf = lambda x: np.ascontiguousarray(x.astype(np.float32).astype(ml_dtypes.bfloat16))
    return {"W1s": bf(W1s), "W1k": bf(W1k), "W1i": bf(W1i), "Ef": bf(Ef), "Ei": bf(Ei)}


def emit_PBHY(P, io):
    nc = P.nc
    din = lambda name, shape, dt=F32: io[name]
    hyp_t = io["hyp_t"]
    HB = io["hyp_base"]
    shw_d = din("shw", [1, 3 * 4 * 64])
    w1_d = din("hw1", [33, 64])
    w2_d = din("hw2", [64, 64])
    w3_d = din("hw3", [64, 2, 128])
    mlpv_d = din("mlpv", [64, 3])
    hbias_d = din("hbias", [128, 1])
    ndelta_d = din("ndelta", [128, 1])
    zT_d = din("zT", [2, 33, L])
    tn_d = din("tn", [2, 1, L])
    W1s_d = din("W1s", [128, 256], BF16)
    W1k_d = din("W1k", [128, 256], BF16)
    W1i_d = din("W1i", [128, 2, 256], BF16)
    Ef_d = din("Ef", [128, 128, 2, 128], BF16)
    Ei_d = din("Ei", [128, 128, 2, 128], BF16)
    yhy_t = io["yhy_t"]
    YB = io["yhy_base"]
    kscr_d = io["kscr"]
    bscr_d = io["bscr"]
    kscr_t = kscr_d.tensor
    with ExitStack() as st:
        P.cur = st
        V = lambda fn, r, w: P.op("dve", fn, r=r, w=w)
        G = lambda fn, r, w: P.op("pool", fn, r=r, w=w)
        A_ = lambda fn, r, w: P.op("act", fn, r=r, w=w)
        gates = [P.sb([128, 64, 128], BF16, "gate%d" % i) for i in range(3)]
        W1s = P.sb([128, 256], BF16, "W1s")
        W1k = P.sb([128, 256], BF16, "W1k")
        W1i = P.sb([128, 2, 256], BF16, "W1i")
        P.dma("sp", W1s[:], W1s_d, r=[], w=["W1s"], key="W1s")
        P.dma("sp", W1k[:], W1k_d, r=[], w=["W1k"], key="W1k")
        P.dma("sp", W1i[:], W1i_d, r=[], w=["W1i"], key="W1i")
        biasb = P.sb([128, 128], F32, "biasb")
        with ExitStack() as st2:
            w1 = P.sb([33, 64], F32, "w1", st2)
            w2 = P.sb([64, 64], F32, "w2", st2)
            w3 = P.sb([64, 2, 128], F32, "w3", st2)
            mlpv = P.sb([64, 3], F32, "mlpv", st2)
            fsc = P.sb([64, 4], F32, "fsc", st2)
            hbias = P.sb([128, 1], F32, "hbias", st2)
            ndelta = P.sb([128, 1], F32, "ndelta", st2)
            for (t_, d_, n_) in ((w1, w1_d, "w1"), (w2, w2_d, "w2"), (w3, w3_d, "w3"), (mlpv, mlpv_d, "mlpv"), (hbias, hbias_d, "hbias"),
                                 (ndelta, ndelta_d, "ndelta")):
                P.dma("sp", t_[:], d_, r=[], w=[n_], key=n_)
            V(lambda e: e.tensor_scalar(out=fsc[:, 0:1], in0=mlpv[:, 2:3], scalar1=1.0 / TWO_PI, scalar2=None, op0=ALU.mult), ["mlpv"], ["fsc"])
            V(lambda e: e.tensor_tensor(out=fsc[:, 1:2], in0=fsc[:, 0:1], in1=mlpv[:, 0:1], op=ALU.mult), ["mlpv", "fsc"], ["fsc"])
            V(lambda e: e.tensor_tensor(out=fsc[:, 2:3], in0=fsc[:, 0:1], in1=mlpv[:, 1:2], op=ALU.mult), ["mlpv", "fsc"], ["fsc"])
            hk32 = [P.sb([128, L], F32, "hk32_%d" % i, st2) for i in range(2)]
            hk16 = P.sb([128, L], BF16, "hk16", st2)
            zt = [P.sb([33, 512], F32, "zt%d" % i, st2) for i in range(2)]
            tnb = [P.sb([128, 512], F32, "tnb%d" % i, st2) for i in range(2)]
            tq = P.sb([64, 512], F32, "tqm", st2)
            tr = P.sb([64, 512], F32, "trm", st2)
            h1 = P.sb([64, 512], F32, "h1", st2)
            h2 = P.sb([64, 512], F32, "h2", st2)
            dec = P.sb([128, 512], F32, "dec", st2)
            asum = P.sb([128, 32], F32, "asum", st2)
            nrm = P.sb([128, 4], F32, "nrm", st2)
            psm1 = P.ps([64, 512], F32, "psm1", st2)
            psm2 = P.ps([64, 512], F32, "psm2", st2)
            psm3 = P.ps([128, 512], F32, "psm3", st2)

            def sin_layer(ps, psn, bcol, hout, hn):
                V(lambda e: e.tensor_scalar(out=tq[:], in0=ps[:], scalar1=fsc[:, 0:1], scalar2=fsc[:, bcol:bcol + 1], op0=ALU.mult, op1=ALU.add),
                  [psn, "fsc"], ["tqm"])
                V(lambda e: e.tensor_scalar(out=tr[:], in0=tq[:], scalar1=MAGIC, scalar2=MAGIC, op0=ALU.add, op1=ALU.subtract), ["tqm"], ["trm"])
                V(lambda e: e.tensor_tensor(out=tr[:], in0=tq[:], in1=tr[:], op=ALU.subtract), ["tqm", "trm"], ["trm"])
                A_(lambda e: e.activation(out=hout[:], in_=tr[:], func=AF.Sin, scale=TWO_PI), ["trm"], [hn])

            for d in range(2):
                for tt in range(L // 512):
                    i = d * 16 + tt
                    z_ = zt[i % 2]
                    zn = "zt%d" % (i % 2)
                    tb_ = tnb[i % 2]
                    tbn = "tnb%d" % (i % 2)
                    cs_ = slice(tt * 512, (tt + 1) * 512)
                    P.dma("sp", z_[:], zT_d[d, :, cs_], r=[], w=[zn], key=zn)
                    P.dma("sp", tb_[:], tn_d[d, :, cs_].partition_broadcast(128), r=[], w=[tbn], key=tbn)
                    P.op("pe", lambda e, z_=z_: e.matmul(psm1[:], w1[:], z_[:], start=True, stop=True), r=["w1", zn], w=["psm1"])
                    sin_layer(psm1, "psm1", 1, h1, "h1")
                    P.op("pe", lambda e: e.matmul(psm2[:], w2[:], h1[:], start=True, stop=True), r=["w2", "h1"], w=["psm2"])
                    sin_layer(psm2, "psm2", 2, h2, "h2")
                    P.op("pe", lambda e, d=d: e.matmul(psm3[:], w3[:, d, :], h2[:], start=True, stop=True), r=["w3", "h2"], w=["psm3"])
                    A_(lambda e, tb_=tb_: e.activation(out=dec[:], in_=tb_[:], func=AF.Exp, scale=ndelta[:, 0:1]), [tbn, "ndelta"], ["dec"])
                    V(lambda e, d=d, cs_=cs_: e.tensor_tensor(out=hk32[d][:, cs_], in0=psm3[:], in1=dec[:], op=ALU.mult),
                      ["psm3", "dec"], [("hk32", d, tt)])
                    V(lambda e, d=d, cs_=cs_, i=i: e.tensor_reduce(out=asum[:, i:i + 1], in_=hk32[d][:, cs_], axis=mybir.AxisListType.X, op=ALU.add,
                                                                   apply_absolute_value=True),
                      [("hk32", d, tt)], ["asum"])
            V(lambda e: e.tensor_reduce(out=nrm[:, 0:1], in_=asum[:], axis=mybir.AxisListType.X, op=ALU.add), ["asum"], ["nrm"])
            V(lambda e: e.tensor_scalar(out=nrm[:, 0:1], in0=nrm[:, 0:1], scalar1=1e-6, scalar2=None, op0=ALU.add), ["nrm"], ["nrm"])
            V(lambda e: e.reciprocal(out=nrm[:, 1:2], in_=nrm[:, 0:1]), ["nrm"], ["nrm"])
            V(lambda e: e.scalar_tensor_tensor(out=nrm[:, 2:3], in0=hk32[1][:, 0:1], scalar=nrm[:, 1:2], in1=hbias[:], op0=ALU.mult, op1=ALU.add),
              ["nrm", "hbias", ("hk32", 1, 0)], ["nrm"])
            P.dma("sp", bscr_d.rearrange("o p -> p o"), nrm[:, 2:3], r=["nrm"], w=["bscr"], key="bscr")
            for d in range(2):
                for q in range(4):
                    cs_ = slice(q * 2048, (q + 1) * 2048)
                    eng = V if (q % 2 == 0) else G
                    eng(lambda e, d=d, cs_=cs_: e.tensor_scalar(out=hk16[:, cs_], in0=hk32[d][:, cs_], scalar1=nrm[:, 1:2], scalar2=None, op0=ALU.mult),
                        ["nrm"] + [("hk32", d, tt) for tt in range(4 * q, 4 * q + 4)], [("hk16", q)])
                    P.dma("sp", kscr_d[d, :, cs_], hk16[:, cs_], r=[("hk16", q)], w=[("kscr", d)], key=("hk16", q))
            P.dma("sp", biasb[:], bscr_d.partition_broadcast(128), r=["bscr"], w=["biasb"], key="biasb")
            P.barrier()
        with ExitStack() as st3:
            shw = P.sb([128, 3 * 4 * 64], F32, "shw", st3)
            P.dma("sp", shw[:], shw_d.partition_broadcast(128), r=[], w=["shw"], key="shw")
            Z = P.sb([128, 64, 130], BF16, "Z", st3)
            c1 = P.sb([128, 64, 128], F32, "c1", st3)
            c2 = P.sb([128, 64, 128], F32, "c2", st3)
            for comp in range(3):
                V(lambda e: e.memset(Z[:, :, 0:1], 0.0), [], ["Z"])
                V(lambda e: e.memset(Z[:, :, 129:130], 0.0), [], ["Z"])
                for b in range(2):
                    off = HB + (comp * 64 * 2 + b) * L
                    p0 = 64 * b
                    src0 = AP(hyp_t, off, [[L, 1], [2 * L, 64], [1, 129]])
                    P.dma("sp", Z[p0:p0 + 1, :, 1:130], src0, r=[], w=["Z"], key=("Z", 0))
                    src1 = AP(hyp_t, off + 127, [[128, 62], [2 * L, 64], [1, 130]])
                    P.dma("sp", Z[p0 + 1:p0 + 63, :, 0:130], src1, r=[], w=["Z"], key=("Z", 1))
                    src2 = AP(hyp_t, off + 63 * 128 - 1, [[L, 1], [2 * L, 64], [1, 129]])
                    P.dma("sp", Z[p0 + 63:p0 + 64, :, 0:129], src2, r=[], w=["Z"], key=("Z", 2))
                wb = lambda tap: fview(shw[:, (comp * 4 + tap) * 64:(comp * 4 + tap) * 64 + 64], [[1, 64], [0, 128]])
                V(lambda e: e.tensor_tensor(out=c1[:], in0=Z[:, :, 0:128], in1=wb(0), op=ALU.mult), ["Z", "shw"], ["c1"])
                G(lambda e: e.tensor_tensor(out=c2[:], in0=Z[:, :, 1:129], in1=wb(1), op=ALU.mult), ["Z", "shw"], ["c2"])
                V(lambda e: e.tensor_tensor(out=c1[:], in0=c1[:], in1=c2[:], op=ALU.add), ["c1", "c2"], ["c1"])
                G(lambda e: e.tensor_tensor(out=c2[:], in0=Z[:, :, 2:130], in1=wb(2), op=ALU.mult), ["Z", "shw", "c1"], ["c2"])
                V(lambda e: e.tensor_tensor(out=c1[:], in0=c1[:], in1=c2[:], op=ALU.add), ["c1", "c2"], ["c1"])
                G(lambda e, comp=comp: e.tensor_tensor(out=gates[comp][:], in0=c1[:], in1=wb(3), op=ALU.add), ["c1", "shw"], [("gate", comp)])
            P.barrier()
        AB = P.sb([128, 128 * 3 * 64], BF16, "AB")
        A_sb = fview(AB[:], [[192, 128], [64, 3], [1, 64]])
        B_sb = fview(AB[:], [[128, 128], [64, 2], [1, 64]])
        YK = P.sb([128, 2 * 64 * 128], BF16, "YK")
        Y_sb = fview(YK[:], [[64 * 128, 2], [128, 64], [1, 128]])
        kt = fview(YK[:], [[128, 64], [1, 128]])
        KH = P.sb([128, 128, 2, 64], BF16, "KH")
        Ech = [P.sb([128, 8, 2, 128], BF16, "Ech%d" % i) for i in range(2)]
        mt = [P.sb([128, 4, 64], F32, "mtm%d" % i) for i in range(4)]
        s1 = P.sb([128, 8, 64], F32, "s1")
        s2 = P.sb([128, 8, 64], F32, "s2")
        psA = [P.ps([128, 256], F32, "psA%d" % i) for i in range(2)]
        ps3 = [P.ps([128, 4, 128], F32, "ps3_%d" % i) for i in range(2)]
        psI = [P.ps([128, 8, 64], F32, "psI%d" % i) for i in range(2)]
        nE = [0]

        def load_E(tab_d, g):
            j = nE[0] % 2
            nE[0] += 1
            P.dma("sp", Ech[j][:], tab_d[:, 8 * g:8 * g + 8], r=[], w=["Ech%d" % j], key="Ech%d" % j)
            return Ech[j], "Ech%d" % j

        def fwd_step1(lhs_fn, lhs_key, wtab, wname):
            for c in range(64):
                pa = psA[c % 2]
                pn = "psA%d" % (c % 2)
                P.op("pe", lambda e, pa=pa, c=c: e.matmul(pa[:], lhs_fn(c), wtab[:], start=True, stop=True), r=[lhs_key, wname], w=[pn])
                A_(lambda e, pa=pa, c=c: e.activation(out=AB[:, :].rearrange("p (f s c) -> p f s c", s=3, c=64)[:, :, 1, c], in_=pa[:, 0:128], func=AF.Copy),
                   [pn], ["AB"])
                A_(lambda e, pa=pa, c=c: e.activation(out=AB[:, :].rearrange("p (f s c) -> p f s c", s=3, c=64)[:, :, 2, c], in_=pa[:, 128:256], func=AF.Copy),
                   [pn], ["AB"])
                V(lambda e, pa=pa, c=c: e.tensor_scalar(out=AB[:, :].rearrange("p (f s c) -> p f s c", s=3, c=64)[:, :, 0, c], in0=pa[:, 128:256],
                                                        scalar1=-1.0, scalar2=None, op0=ALU.mult),
                  [pn], ["AB"])

        A4 = AB[:, :].rearrange("p (f s c) -> p f s c", s=3, c=64)
        B4 = AB[:, 0:128 * 2 * 64].rearrange("p (t r c) -> p t r c", r=2, c=64)
        Y4 = YK[:, :].rearrange("p (r c f) -> p r c f", r=2, c=64)
        K3 = YK[:, 0:64 * 128].rearrange("p (c t) -> p c t", c=64)

        def fwd_step3(evac):
            for g in range(16):
                Et, En = load_E(Ef_d, g)
                for q2 in range(2):
                    pg = ps3[(2 * g + q2) % 2]
                    pgn = "ps3_%d" % ((2 * g + q2) % 2)
                    for q in range(4):
                        j = q2 * 4 + q
                        fl = 8 * g + j
                        P.op("pe", lambda e, pg=pg, q=q, j=j, fl=fl, Et=Et: e.matmul(pg[:, q, :], Et[:, j, 0, :], AB[:, fl * 192 + 64:fl * 192 + 192], start=True, stop=False),
                             r=[En, "AB"], w=[pgn])
                        P.op("pe", lambda e, pg=pg, q=q, j=j, fl=fl, Et=Et: e.matmul(pg[:, q, :], Et[:, j, 1, :], AB[:, fl * 192:fl * 192 + 128], start=False, stop=True),
                             r=[En, "AB"], w=[pgn])
                    evac(pg, pgn, 8 * g + 4 * q2)

        def evac_filter(pg, pgn, fl0):
            A_(lambda e: e.activation(out=KH[:, fl0:fl0 + 4, :, :], in_=pg[:, :, :].rearrange("p q (r c) -> p q r c", r=2), func=AF.Copy,
                                      scale=1.0 / NFFT),
               [pgn], ["KH"])

        def evac_mac(pg, pgn, fl0):
            Xr = pg[:, :, 0:64]
            Xi = pg[:, :, 64:128]
            Kr = KH[:, fl0:fl0 + 4, 0, :]
            Ki = KH[:, fl0:fl0 + 4, 1, :]
            V(lambda e: e.tensor_tensor(out=mt[0][:], in0=Xr, in1=Kr, op=ALU.mult), [pgn, "KH"], ["mtm0"])
            V(lambda e: e.tensor_tensor(out=mt[1][:], in0=Xi, in1=Ki, op=ALU.mult), [pgn, "KH"], ["mtm1"])
            G(lambda e: e.tensor_tensor(out=Y4[:, 0, :, fl0:fl0 + 4].rearrange("p c f -> p f c"), in0=mt[0][:], in1=mt[1][:], op=ALU.subtract),
              ["mtm0", "mtm1"], ["YK"])
            V(lambda e: e.tensor_tensor(out=mt[2][:], in0=Xr, in1=Ki, op=ALU.mult), [pgn, "KH"], ["mtm2"])
            V(lambda e: e.tensor_tensor(out=mt[3][:], in0=Xi, in1=Kr, op=ALU.mult), [pgn, "KH"], ["mtm3"])
            G(lambda e: e.tensor_tensor(out=Y4[:, 1, :, fl0:fl0 + 4].rearrange("p c f -> p f c"), in0=mt[2][:], in1=mt[3][:], op=ALU.add),
              ["mtm2", "mtm3"], ["YK"])

        def inverse(src, srcname, gate, gatename, dst, dstname, o):
            for c in range(64):
                pa = psA[c % 2]
                pn = "psA%d" % (c % 2)
                P.op("pe", lambda e, pa=pa, c=c: e.matmul(pa[:], Y4[:, 0, c, :], W1i[:, 0, :], start=True, stop=False), r=["YK", "W1i"], w=[pn])
                P.op("pe", lambda e, pa=pa, c=c: e.matmul(pa[:], Y4[:, 1, c, :], W1i[:, 1, :], start=False, stop=True), r=["YK", "W1i"], w=[pn])
                A_(lambda e, pa=pa, c=c: e.activation(out=B4[:, :, 0, c], in_=pa[:, 0:128], func=AF.Copy), [pn], ["AB"])
                V(lambda e, pa=pa, c=c: e.tensor_copy(out=B4[:, :, 1, c], in_=pa[:, 128:256]), [pn], ["AB"])
            for g in range(16):
                Et, En = load_E(Ei_d, g)
                pg = psI[g % 2]
                pgn = "psI%d" % (g % 2)
                for j in range(8):
                    tb = 8 * g + j
                    P.op("pe", lambda e, pg=pg, j=j, tb=tb, Et=Et: e.matmul(pg[:, j, :], Et[:, j, 0, :], B4[:, tb, 0, :], start=True, stop=False),
                         r=[En, "AB"], w=[pgn])
                    P.op("pe", lambda e, pg=pg, j=j, tb=tb, Et=Et: e.matmul(pg[:, j, :], Et[:, j, 1, :], B4[:, tb, 1, :], start=False, stop=True),
                         r=[En, "AB"], w=[pgn])
                tsl = slice(8 * g, 8 * g + 8)
                sv = src[:, :, tsl].rearrange("p c t -> p t c")
                gv = gate[:, :, tsl].rearrange("p c t -> p t c")
                dv = dst[:, :, tsl].rearrange("p c t -> p t c")
                bb = fview(biasb[:, 64 * o:64 * o + 64], [[0, 8], [1, 64]])
                G(lambda e, sv=sv, bb=bb: e.tensor_tensor(out=s1[:], in0=sv, in1=bb, op=ALU.mult), [(srcname, g), "biasb"], ["s1"])
                V(lambda e, pg=pg: e.tensor_tensor(out=s2[:], in0=pg[:], in1=s1[:], op=ALU.add), [pgn, "s1"], ["s2"])
                G(lambda e, gv=gv, dv=dv: e.tensor_tensor(out=dv, in0=s2[:], in1=gv, op=ALU.mult), ["s2", (gatename, g)], [(dstname, g)])

        gk = lambda name: [(name, g) for g in range(16)]
        for comp, nm in ((0, "gout"), (1, "gmid"), (2, "vv")):
            for g in range(16):
                P.lastw[(nm, g)] = P.lastw.get(("gate", comp))
                P.readers[(nm, g)] = []
        for o in range(2):
            for dd in range(2):
                srck = AP(kscr_t, (dd * 128 + o * 64) * L, [[128, 64], [L, 64], [1, 128]])
                P.dma("sp", K3[64 * dd:64 * dd + 64, :, :], srck, r=[("kscr", dd)], w=["YK"], key=("kt", dd))
            fwd_step1(lambda c: K3[:, c, :], "YK", W1k, "W1k")
            fwd_step3(evac_filter)
            if o == 0:
                src, srcname, gate, gatename, dst, dstname = gates[2], "vv", gates[1], "gmid", gates[2], "vv"
            else:
                src, srcname, gate, gatename, dst, dstname = gates[2], "vv", gates[0], "gout", gates[1], "gmid"
            for c in range(64):
                pa = psA[c % 2]
                pn = "psA%d" % (c % 2)
                P.op("pe", lambda e, pa=pa, c=c: e.matmul(pa[:], src[:, c, :], W1s[:], start=True, stop=True), r=gk(srcname) + ["W1s"], w=[pn])
                A_(lambda e, pa=pa, c=c: e.activation(out=A4[:, :, 1, c], in_=pa[:, 0:128], func=AF.Copy), [pn], ["AB"])
                A_(lambda e, pa=pa, c=c: e.activation(out=A4[:, :, 2, c], in_=pa[:, 128:256], func=AF.Copy), [pn], ["AB"])
                V(lambda e, pa=pa, c=c: e.tensor_scalar(out=A4[:, :, 0, c], in0=pa[:, 128:256], scalar1=-1.0, scalar2=None, op0=ALU.mult), [pn], ["AB"])
            fwd_step3(evac_mac)
            inverse(src, srcname, gate, gatename, dst, dstname, o)
        outst = gates[1]
        for b in range(2):
            dsto = AP(yhy_t, YB + b * L, [[128, 64], [2 * L, 64], [1, 128]])
            P.dma("sp", dsto, outst[64 * b:64 * b + 64, :, :], r=gk("gmid"), w=[], key=("yout", b), final=True)
        P.barrier()
        P.cur = P.st


HY_DELTAS = np.abs(np.linspace(math.log(1e-2) / 1.5, math.log(1e-2) / 0.3, 512, dtype=np.float32)).astype(np.float32)


def hyena_pos_tables():
    f32 = np.float32
    t_norm = np.linspace(0.0, 1.0, L, dtype=f32)
    bands = np.linspace(1e-4, 15.0, 16, dtype=f32)[None, :]
    ang = f32(2.0 * math.pi / L) * np.arange(L, dtype=f32)[:, None] * bands
    z = np.concatenate([t_norm[:, None], np.cos(ang), -np.sin(ang)], axis=-1).astype(f32)
    idx = (L - np.arange(L)) % L
    zT = np.ascontiguousarray(np.stack([z.T, z[idx].T], 0))
    tn = np.ascontiguousarray(np.stack([t_norm[None, :], t_norm[idx][None, :]], 0))
    return zT, tn


_HY_CONST = {}


def hyena_inputs(p_hy, short_w, short_b, w1, b1, w2, b2, w3, freq, bias):
    if not _HY_CONST:
        _HY_CONST.update(hyena_tables())
        zT, tn = hyena_pos_tables()
        _HY_CONST["zT"] = zT
        _HY_CONST["tn"] = tn
    ims = []
    for c in range(NCORES):
        ch = slice(64 * c, 64 * c + 64)
        hyp = None if p_hy is None else np.ascontiguousarray(np.stack([p_hy[:, :, comp * 512 + 64 * c:comp * 512 + 64 * c + 64] for comp in range(3)], 0).transpose(0, 3, 1, 2))
        shw = np.zeros((3, 4, 64), np.float32)
        for comp in range(3):
            shw[comp, 0:3] = short_w[:, comp * 512 + 64 * c:comp * 512 + 64 * c + 64]
            shw[comp, 3] = short_b[comp * 512 + 64 * c:comp * 512 + 64 * c + 64]
        hw3 = np.zeros((64, 2, 128), np.float32)
        for o in range(2):
            for d in range(2):
                hw3[:, d, o * 64:(o + 1) * 64] = w3[:, o * 1024 + d * 512 + 64 * c:o * 1024 + d * 512 + 64 * c + 64]
        d = {"hyp": hyp, "shw": shw.reshape(1, -1), "hw1": np.ascontiguousarray(w1), "hw2": np.ascontiguousarray(w2), "hw3": hw3,
             "mlpv": np.ascontiguousarray(np.stack([b1, b2, freq], 1)),
             "hbias": np.ascontiguousarray(bias[:, ch].reshape(128, 1)),
             "ndelta": np.ascontiguousarray(-np.tile(HY_DELTAS[ch], 2).reshape(128, 1))}
        d.update(_HY_CONST)
        ims.append(d)
    return ims


def hyena_outputs(res):
    return np.concatenate([np.asarray(res[c]["yhy"]).transpose(1, 2, 0) for c in range(NCORES)], axis=-1)


I32 = mybir.dt.int32


def _msel_table():
    t = np.arange(L)
    m = np.zeros((4, 4, L), np.float32)
    for q, win in enumerate((2, 4, 8, 16)):
        h = win // 2
        cnt = (np.minimum(t + h, L) - np.maximum(t - h, 0)).astype(np.float32)
        m[q, q] = (1.0 / cnt).astype(np.float32)
    return m


def emit_regather(P, p_all, idxp_d, pm):
    with ExitStack() as st:
        P.cur = st
        idx = P.sb([128, 16], I32, "idxp")
        P.dma("sp", idx[:], idxp_d, r=[], w=["idxp"], key="idxp")
        gb = [P.sb([128, 2048], BF16, "gb%d" % i) for i in range(4)]
        for j in range(16):
            rg, blk = j // 4, j % 4
            g_ = gb[j % 4]
            gn = "gb%d" % (j % 4)
            P.op("pool", lambda g, g_=g_, j=j: g.indirect_dma_start(out=g_[:], out_offset=None, in_=p_all,
                                                                   in_offset=bass.IndirectOffsetOnAxis(ap=idx[:, j:j + 1], axis=0)),
                 r=["idxp", "p_all"], w=[gn], dma=gn)
            P.dma("sp", pm[128 * rg:128 * rg + 128, 2048 * blk:2048 * blk + 2048], g_[:], r=[gn], w=["pm"], key=gn + "o")
        P.barrier()
        P.cur = P.st


def build_fused():
    nc = bass.Bass("TRN2", target_bir_lowering=False)
    ext = lambda name, shape, dt=F32: nc.dram_tensor(name, list(shape), dt, kind="ExternalInput").ap()
    itn = lambda name, shape, dt: nc.dram_tensor(name, list(shape), dt, kind="Internal").ap()
    xin = ext("xT", [D, NT])
    memT = ext("memT", [D, NMEM])
    xout = nc.dram_tensor("xoT", [D, NT], F32, kind="ExternalOutput").ap()
    idxp = [ext("idxp_e", [128, 16], I32), ext("idxp_o", [128, 16], I32)]
    idxy = [ext("idxy_e", [128, 32], I32), ext("idxy_o", [128, 32], I32)]
    xbuf = itn("xbuf", [D, NT], F32)
    p_loc = itn("p_loc", [2048, NT], BF16)
    p_all = itn("p_all", [NCORES * 2048, NT], BF16)
    pm = itn("pm", [512, L], BF16)
    y_loc = itn("y_loc", [4096, 512], BF16)
    y_all = itn("y_all", [NCORES * 4096, 512], BF16)
    ys0 = itn("ys0", [2, 32, 2, L], BF16)
    kscr = itn("kscr", [2, 128, L], BF16)
    bscr = itn("bscr", [1, 128], F32)
    yl = y_loc.rearrange("(r a) t -> r (a t)", a=16)
    W = []
    for i in range(DEPTH):
        odd = i % 2 == 1
        d = {"w_in": ext("w_in%d" % i, [128, 8, 2048]), "g_in": ext("g_in%d" % i, [128, 8]),
             "wout": ext("wout%d" % i, [128, 8, 1024]), "wq": ext("wq%d" % i, [128, 8, 1024]), "wk": ext("wk%d" % i, [128, 8, 1024]),
             "wv": ext("wv%d" % i, [128, 8, 1024]), "wo": ext("wo%d" % i, [128, 8, 1024]), "gains": ext("gains%d" % i, [128, 4, 8]),
             "w1": ext("w1_%d" % i, [128, 8, 4096]), "w2": ext("w2_%d" % i, [128, 32, 1024]),
             "g0": ext("g0_%d" % i, [128, 8]), "g1": ext("g1_%d" % i, [128, 8])}
        if odd:
            d.update({"wglu": ext("wglu%d" % i, [128, 4, 512]),
                      "par": ext("par%d" % i, [2, 2, 128, 4]), "bmat": ext("bmat%d" % i, [2, 128, 2, 16]),
                      "cmat": ext("cmat%d" % i, [2, 2, 128, 2, 16]), "dsk": ext("dsk%d" % i, [2, 32, 1]),
                      "shw": ext("shw%d" % i, [1, 3 * 4 * 64]), "hw1": ext("hw1_%d" % i, [33, 64]), "hw2": ext("hw2_%d" % i, [64, 64]),
                      "hw3": ext("hw3_%d" % i, [64, 2, 128]), "mlpv": ext("mlpv%d" % i, [64, 3]), "hbias": ext("hbias%d" % i, [128, 1])})
        else:
            d.update({"poolw": ext("poolw%d" % i, [128, 1, 128]), "pscale": ext("pscale%d" % i, [128, 1]), "convw": ext("convw%d" % i, [128, 3])})
        W.append(d)
    msel = ext("msel", [4, L])
    ident = ext("ident", [128, 128])
    hyc = {"ndelta": ext("ndelta", [128, 1]), "zT": ext("zT", [2, 33, L]), "tn": ext("tn", [2, 1, L]),
           "W1s": ext("W1s", [128, 256], BF16), "W1k": ext("W1k", [128, 256], BF16), "W1i": ext("W1i", [128, 2, 256], BF16),
           "Ef": ext("Ef", [128, 128, 2, 128], BF16), "Ei": ext("Ei", [128, 128, 2, 128], BF16)}
    with ExitStack() as st:
        P = Prog(nc, st)
        for i in range(DEPTH):
            odd = i % 2 == 1
            w = W[i]
            emit_PA(P, {"xT": xin if i == 0 else xbuf, "w": w["w_in"], "g": w["g_in"], "pT": p_loc})
            P.allgather(p_loc, p_all, r=[], w=["p_all"])
            emit_regather(P, p_all, idxp[1 if odd else 0], pm)
            if not odd:
                emit_PBE(P, {"pu": pm[0:128], "cb": pm[128:256], "cc": pm[256:384], "ch": pm[384:512], "poolw": w["poolw"],
                             "pscale": w["pscale"], "msel": msel, "convw": w["convw"], "ypool": yl[0:128], "yconv": yl[128:256]})
            else:
                emit_PBS5(P, {"u": pm[0:128].rearrange("(g i b) t -> g i b t", g=2, b=2), "par": w["par"], "bmat": w["bmat"],
                              "cmat": w["cmat"], "dsk": w["dsk"], "ident": ident,
                              "ys": yl[0:128].rearrange("(g i b) t -> g i b t", g=2, b=2), "ys0": ys0})
                hio = {"hyp_t": pm.tensor, "hyp_base": 128 * L, "yhy_t": y_loc.tensor, "yhy_base": 128 * L, "kscr": kscr, "bscr": bscr,
                       "shw": w["shw"], "hw1": w["hw1"], "hw2": w["hw2"], "hw3": w["hw3"], "mlpv": w["mlpv"], "hbias": w["hbias"]}
                hio.update(hyc)
                emit_PBHY(P, hio)
            P.allgather(y_loc, y_all, r=[], w=["y_all"])
            pio = {"xT": xin if i == 0 else xbuf, "xoT": xbuf, "y_all": y_all, "idx_y": idxy[1 if odd else 0], "memT": memT,
                   "wout": w["wout"], "wq": w["wq"], "wk": w["wk"], "wv": w["wv"], "wo": w["wo"], "gains": w["gains"]}
            if odd:
                pio["wglu"] = w["wglu"]
            emit_PC1(P, pio, odd)
            emit_PC2(P, {"xT": xbuf, "w1": w["w1"], "w2": w["w2"], "g0": w["g0"], "g1": w["g1"],
                         "xoT": xout if i == DEPTH - 1 else xbuf})
        P.finish()
    return nc


def _idx_tables(c):
    kb, kq = c // 4, c % 4
    p = np.arange(128)
    idxp_e = np.zeros((128, 16), np.int32)
    idxp_o = np.zeros((128, 16), np.int32)
    for rg in range(4):
        for blk in range(4):
            j = rg * 4 + blk
            if rg == 0:
                idxp_e[:, j] = (4 * (c % 2) + blk) * 2048 + 128 * (c // 2) + p
            else:
                idxp_e[:, j] = (4 * (p // 64) + blk) * 2048 + 512 * rg + 64 * c + (p % 64)
            chl = 64 * rg + p // 2
            b = p % 2
            chan = np.where(chl < 64, 64 * c + chl, 512 + ((chl - 64) // 64) * 512 + 64 * c + ((chl - 64) % 64))
            idxp_o[:, j] = (4 * b + blk) * 2048 + chan
    idxy_e = np.zeros((128, 32), np.int32)
    idxy_o = np.zeros((128, 32), np.int32)
    for k in range(8):
        m = 128 * k + p
        for t in range(4):
            blk16 = kq * 4 + t
            if k < 4:
                src_o, row_o = m // 64, 2 * (m % 64) + kb
                src_e, row_e = 2 * (m // 128) + kb, m % 128
            else:
                mm = m - 512
                src_o, row_o = mm // 64, 128 + 2 * (mm % 64) + kb
                src_e, row_e = mm // 64, 128 + kb * 64 + (mm % 64)
            idxy_o[:, 4 * k + t] = src_o * 4096 + row_o * 16 + blk16
            idxy_e[:, 4 * k + t] = src_e * 4096 + row_e * 16 + blk16
    return idxp_e, idxp_o, idxy_e, idxy_o


def kernel(x, mem, norm_mix, norm_xattn, norm_mem, norm_mlp, xa_wq, xa_wk, xa_wv, xa_wo, mlp_w1, mlp_w2, ev_w_in, ev_pool_w,
           ev_pool_scale, ev_conv_w, ev_w_out, od_w_in, od_s5_lambda_re, od_s5_lambda_im, od_s5_log_dt, od_s5_b_re, od_s5_b_im,
           od_s5_c_re, od_s5_c_im, od_s5_d, od_s5_w_glu, od_hy_short_w, od_hy_short_b, od_hy_w1, od_hy_b1, od_hy_w2, od_hy_b2,
           od_hy_w3, od_hy_freq, od_hy_bias, od_w_out):
    f = lambda a: np.ascontiguousarray(np.asarray(a, dtype=np.float32))
    x = f(x).reshape(NTOK, D)
    mem = f(mem)
    msel = _msel_table()
    nc = prog("fused", build_fused)
    tabs = hyena_tables()
    zT, tn = hyena_pos_tables()
    ims = [dict() for _ in range(NCORES)]
    shared = {"ident": np.eye(128, dtype=np.float32), "zT": zT, "tn": tn}
    shared.update(tabs)
    for i in range(DEPTH):
        j = i // 2
        odd = i % 2 == 1
        shared["w_in%d" % i] = wlay(f(od_w_in[j] if odd else ev_w_in[j]))
        shared["g_in%d" % i] = glay(f(norm_mix[i, 0]))
        shared["wout%d" % i] = wlay(f(od_w_out[j] if odd else ev_w_out[j]))
        for nm, arr in (("wq", xa_wq), ("wk", xa_wk), ("wv", xa_wv), ("wo", xa_wo)):
            shared["%s%d" % (nm, i)] = wlay(f(arr[i]))
        shared["gains%d" % i] = np.ascontiguousarray(np.stack([glay(f(norm_mix[i, 1])), glay(f(norm_xattn[i, 0])), glay(f(norm_xattn[i, 1])),
                                                               glay(f(norm_mem[i]))], axis=1))
        shared["w1_%d" % i] = wlay(f(mlp_w1[i]))
        shared["w2_%d" % i] = wlay(f(mlp_w2[i]))
        shared["g0_%d" % i] = glay(f(norm_mlp[i, 0]))
        shared["g1_%d" % i] = glay(f(norm_mlp[i, 1]))
        if odd:
            shared["wglu%d" % i] = wlay(f(od_s5_w_glu[j]))
            s5i = s5_inputs(None, f(od_s5_lambda_re[j]),
                            f(od_s5_lambda_im[j]), f(od_s5_log_dt[j]), f(od_s5_b_re[j]), f(od_s5_b_im[j]), f(od_s5_c_re[j]), f(od_s5_c_im[j]),
                            f(od_s5_d[j]))
            hyi = hyena_inputs(None, f(od_hy_short_w[j]), f(od_hy_short_b[j]), f(od_hy_w1[j]), f(od_hy_b1[j]), f(od_hy_w2[j]),
                               f(od_hy_b2[j]), f(od_hy_w3[j]), f(od_hy_freq[j]), f(od_hy_bias[j]))
            for c in range(NCORES):
                for nm in ("par", "bmat", "cmat", "dsk"):
                    ims[c]["%s%d" % (nm, i)] = s5i[c][nm]
                for nm in ("shw", "mlpv", "hbias"):
                    ims[c]["%s%d" % (nm, i)] = hyi[c][nm]
                ims[c]["hw3_%d" % i] = hyi[c]["hw3"]
                ims[c]["hw1_%d" % i] = hyi[c]["hw1"]
                ims[c]["hw2_%d" % i] = hyi[c]["hw2"]
                ims[c]["ndelta"] = hyi[c]["ndelta"]
        else:
            for c in range(NCORES):
                q = c // 2
                ims[c]["poolw%d" % i] = np.ascontiguousarray(f(ev_pool_w[j][q]).reshape(128, 1, 128))
                ims[c]["pscale%d" % i] = np.ascontiguousarray(f(ev_pool_scale[j])[128 * q:128 * q + 128].reshape(128, 1))
                ims[c]["convw%d" % i] = np.ascontiguousarray(np.tile(f(ev_conv_w[j])[:, 64 * c:64 * c + 64].T, (2, 1)))
    for c in range(NCORES):
        ie, io_, ye, yo = _idx_tables(c)
        ims[c].update({"xT": _tok_T(x, c), "memT": np.ascontiguousarray(mem[c // (NCORES // B)].T), "msel": msel[c // 2],
                       "idxp_e": ie, "idxp_o": io_, "idxy_e": ye, "idxy_o": yo})
        ims[c].update(shared)
    res = run(nc, ims)
    out = np.concatenate([np.asarray(r["xoT"]).T for r in res], axis=0)
    return np.ascontiguousarray(out.reshape(B, L, D).astype(np.float32))


def _tok_T(a, c):
    return np.ascontiguousarray(a[c * NT:(c + 1) * NT].T)
```

```python
import math
from contextlib import ExitStack

import numpy as np
import ml_dtypes

import concourse.bass as bass
import concourse.mybir as mybir
from concourse.ap import AP
from concourse.bass_utils import run_bass_kernel_spmd

F32 = mybir.dt.float32
BF16 = mybir.dt.bfloat16
ALU = mybir.AluOpType
AF = mybir.ActivationFunctionType

NCORES = 8
D = 1024
B = 2
L = 8192
NTOK = B * L
NT = NTOK // NCORES
DEPTH = 4
NMEM = 256
EPS = 1e-6
MAGIC = 12582912.0
TWO_PI = 2.0 * math.pi


class Prog:
    def __init__(self, nc, st):
        self.nc = nc
        self.st = st
        self.eng = {"pe": nc.tensor, "dve": nc.vector, "act": nc.scalar, "pool": nc.gpsimd, "sp": nc.sync}
        self.cnt = {e: 0 for e in self.eng}
        self.csem = {e: st.enter_context(nc.semaphore("cs_" + e)) for e in self.eng}
        self.seen = {e: {} for e in self.eng}
        self.lastw = {}
        self.readers = {}
        self.dsem = {}
        self.final = []
        self.nsb = 0
        self.cur = st
        self.ncc = 0

    def sb(self, shape, dt, name=None, st=None):
        self.nsb += 1
        self.nsb += 1
        return (st or self.cur).enter_context(self.nc.sbuf_tensor("s%d_" % self.nsb + (name or "t"), list(shape), dt))

    def ps(self, shape, dt=F32, name=None, st=None):
        self.nsb += 1
        self.nsb += 1
        return (st or self.cur).enter_context(self.nc.psum_tensor("p%d_" % self.nsb + (name or "t"), list(shape), dt))

    def op(self, e, fn, r=(), w=(), dma=None, final=False):
        deps = {}
        mysem = None
        if dma is not None:
            if dma not in self.dsem:
                self.dsem[dma] = [self.st.enter_context(self.nc.semaphore("ds%d" % len(self.dsem))), 0]
            mysem = self.dsem[dma][0]

        def need(tok):
            if tok is None:
                return
            sem, val, src = tok
            if src == "pe" and e == "pe":
                return
            if sem is mysem:
                return
            k = id(sem)
            if k not in deps or deps[k][1] < val:
                deps[k] = (sem, val)

        for k in r:
            need(self.lastw.get(k))
        for k in w:
            need(self.lastw.get(k))
            for t in self.readers.get(k, ()):
                need(t)
        eng = self.eng[e]
        for k, (sem, val) in deps.items():
            if self.seen[e].get(k, 0) < val:
                eng.wait_ge(sem, val)
                self.seen[e][k] = val
        if dma is None:
            self.cnt[e] += 1
            tok = (self.csem[e], self.cnt[e], e)
            fn(eng).then_inc(self.csem[e], 1)
        else:
            d = self.dsem[dma]
            d[1] += 16
            tok = (d[0], d[1], "dma")
            fn(eng).then_inc(d[0], 16)
        for k in r:
            self.readers.setdefault(k, []).append(tok)
        for k in w:
            self.lastw[k] = tok
            self.readers[k] = []
        if final:
            self.final.append(tok)
        return tok

    def dma(self, e, out, in_, r, w, key, final=False, **kw):
        return self.op(e, lambda g: g.dma_start(out=out, in_=in_, **kw), r=r, w=w, dma=key, final=final)

    def allgather(self, src, dst, r, w):
        sem = self.st.enter_context(self.nc.semaphore("cc%d" % self.ncc))
        self.ncc += 1
        if not hasattr(self, "ccd"):
            self.ccd = self.st.enter_context(self.nc.sbuf_tensor("s_ccdummy", [128, 4], F32))

        def fn(g):
            g.collective_compute("AllGather", ALU.bypass, replica_groups=[list(range(NCORES))], ins=[src], outs=[dst]).then_inc(sem)
            g.wait_ge(sem, 1)
            return g.memset(self.ccd[:], 0.0)

        return self.op("pool", fn, r=r, w=list(w) + ["ccdummy"])

    def finish(self):
        eng = self.eng["sp"]
        for sem, val, _ in self.final:
            eng.wait_ge(sem, val)
        for e in ("pe", "dve", "act", "pool"):
            if self.cnt[e]:
                eng.wait_ge(self.csem[e], self.cnt[e])


def keys(name, n):
    return [(name, i) for i in range(n)]


def load_weight(P, dram, sbt, name, nk):
    for k in range(nk):
        P.dma("pool", sbt[:, k, :], dram[:, k, :], r=[], w=[name], key=name)


class DenseCtx:
    def __init__(self, P, T):
        self.P = P
        self.T = T
        self.ones = P.sb([128, 128], BF16, "ones")
        P.op("dve", lambda e: e.memset(self.ones[:], 1.0), w=["ones"])
        self.eps = P.sb([128, 1], F32, "epsc")
        P.op("dve", lambda e: e.memset(self.eps[:], EPS), w=["eps"])
        self.ps_stat = [P.ps([128, T], F32, "ps_stat%d" % i) for i in range(1)]
        self.nstat = 0
        self.lnt = P.sb([128, T], F32, "lnt")

    def rstd_from_sq(self, sq_fn, sq_keys, nchunks, rstd, rstd_key):
        P = self.P
        ps = self.ps_stat[0]
        pk = "ps_stat0"
        for k in range(nchunks):
            P.op("pe", lambda e, k=k: e.matmul(ps[:], self.ones[:], sq_fn(k), start=(k == 0), stop=(k == nchunks - 1)),
                 r=["ones", sq_keys[k]], w=[pk])
        P.op("act", lambda e: e.activation(out=self.lnt[:], in_=ps[:], func=AF.Ln, bias=self.eps[:, 0:1], scale=1.0 / D),
             r=[pk, "eps"], w=["lnt"])
        P.op("act", lambda e: e.activation(out=rstd, in_=self.lnt[:], func=AF.Exp, scale=-0.5),
             r=["lnt"], w=[rstd_key])


def emit_prenorm(C, xt, xkeys, gT, gkey, sq, sqname, h, hname, rstd, rstdkey):
    P = C.P
    for k in range(8):
        P.op("act", lambda e, k=k: e.activation(out=sq[:, k, :], in_=xt[:, k, :], func=AF.Square),
             r=[xkeys[k]], w=[(sqname, k)])
    C.rstd_from_sq(lambda k: sq[:, k, :], keys(sqname, 8), 8, rstd[:], rstdkey)
    for k in range(8):
        P.op("dve", lambda e, k=k: e.scalar_tensor_tensor(out=h[:, k, :], in0=xt[:, k, :], scalar=gT[:, k:k + 1], in1=rstd[:],
                                                          op0=ALU.mult, op1=ALU.mult),
             r=[xkeys[k], gkey, rstdkey], w=[(hname, k)])


def emit_postnorm_residual(C, mo, moname, sq, sqname, gT, gkey, xt, xkeys, rstd, rstdkey, tmp, tmpname):
    P = C.P
    C.rstd_from_sq(lambda k: sq[:, k, :], keys(sqname, 8), 8, rstd[:], rstdkey)
    for k in range(8):
        P.op("dve", lambda e, k=k: e.scalar_tensor_tensor(out=tmp[:, k % 2, :], in0=mo[:, k, :], scalar=gT[:, k:k + 1], in1=rstd[:],
                                                          op0=ALU.mult, op1=ALU.mult),
             r=[(moname, k), gkey, rstdkey], w=[(tmpname, k % 2)])
        P.op("pool", lambda e, k=k: e.tensor_tensor(out=xt[:, k, :], in0=xt[:, k, :], in1=tmp[:, k % 2, :], op=ALU.add),
             r=[xkeys[k], (tmpname, k % 2)], w=[xkeys[k]])


def emit_proj(P, psb, psname, w_sb, wname, nk, m, rhs_fn, rhs_keys, T):
    for k in range(nk):
        P.op("pe", lambda e, k=k: e.matmul(psb, w_sb[:, k, 128 * m:128 * m + 128], rhs_fn(k), start=(k == 0), stop=(k == nk - 1)),
             r=[wname, rhs_keys[k]], w=[psname])


def emit_PA(P, io, T=512):
    nc = P.nc
    xT = io["xT"]
    w = io["w"]
    g = io["g"]
    pT = io["pT"]
    with ExitStack() as st:
        P.cur = st
        C = DenseCtx(P, T)
        wsb = P.sb([128, 8, 2048], BF16, "w_in")
        load_weight(P, w, wsb, "w_in", 8)
        gT = P.sb([128, 8], F32, "gT")
        P.dma("sp", gT[:], g, r=[], w=["gT"], key="gT")
        xts = [P.sb([128, 8, T], F32, "xt%d" % i) for i in range(2)]
        sq = P.sb([128, 8, T], BF16, "sq")
        hs = [P.sb([128, 8, T], BF16, "h%d" % i) for i in range(2)]
        rstd = P.sb([128, T], F32, "rstd")
        stage = [P.sb([128, 4, T], BF16, "stage%d" % i) for i in range(2)]
        pso = [P.ps([128, T], F32, "pso%d" % i) for i in range(4)]
        xv = xT.rearrange("(k p) n -> p k n", p=128)
        pv = pT.rearrange("(m p) n -> p m n", p=128)
        nev = 0
        for t in range(NT // T):
            j = t % 2
            xt = xts[j]
            xk = keys("xt%d" % j, 8)
            P.dma("sp", xt[:], xv[:, :, t * T:(t + 1) * T], r=[], w=xk, key="xt%d" % j)
            emit_prenorm(C, xt, xk, gT, "gT", sq, "sq", hs[j], "h%d" % j, rstd, "rstd")
            hk = keys("h%d" % j, 8)
            for mg in range(4):
                sj = (t * 4 + mg) % 2
                for mi in range(4):
                    m = mg * 4 + mi
                    pb = pso[m % 4]
                    emit_proj(P, pb[:], "pso%d" % (m % 4), wsb, "w_in", 8, m, lambda k: hs[j][:, k, :], hk, T)
                    eng = "act" if nev % 2 == 0 else "dve"
                    nev += 1
                    if eng == "act":
                        P.op("act", lambda e, pb=pb, mi=mi: e.activation(out=stage[sj][:, mi, :], in_=pb[:], func=AF.Copy),
                             r=["pso%d" % (m % 4)], w=[("stage%d" % sj, mi)])
                    else:
                        P.op("dve", lambda e, pb=pb, mi=mi: e.tensor_copy(out=stage[sj][:, mi, :], in_=pb[:]),
                             r=["pso%d" % (m % 4)], w=[("stage%d" % sj, mi)])
                P.dma("act", pv[:, mg * 4:mg * 4 + 4, t * T:(t + 1) * T], stage[sj][:], r=keys("stage%d" % sj, 4), w=[],
                      key="stage%d" % sj, final=True)
        P.barrier()
        P.cur = P.st


def emit_PC2(P, io, T=256):
    nc = P.nc
    xT = io["xT"]
    w1 = io["w1"]
    w2 = io["w2"]
    g0 = io["g0"]
    g1 = io["g1"]
    xoT = io["xoT"]
    with ExitStack() as st:
        P.cur = st
        C = DenseCtx(P, T)
        w1s = P.sb([128, 8, 4096], BF16, "w1")
        w2s = P.sb([128, 32, 1024], BF16, "w2")
        load_weight(P, w1, w1s, "w1", 8)
        load_weight(P, w2, w2s, "w2", 32)
        g0T = P.sb([128, 8], F32, "g0T")
        g1T = P.sb([128, 8], F32, "g1T")
        P.dma("sp", g0T[:], g0, r=[], w=["g0T"], key="g0T")
        P.dma("sp", g1T[:], g1, r=[], w=["g1T"], key="g1T")
        xts = [P.sb([128, 8, T], F32, "xt%d" % i) for i in range(2)]
        sq = P.sb([128, 8, T], BF16, "sq")
        h = P.sb([128, 8, T], BF16, "h")
        rstd = P.sb([128, T], F32, "rstd")
        a = P.sb([128, 32, T], BF16, "a")
        rl = [P.sb([128, T], F32, "rl%d" % i) for i in range(2)]
        mo = P.sb([128, 8, T], F32, "mo")
        tmp = P.sb([128, 2, T], F32, "tmp")
        psu = [P.ps([128, T], F32, "psu%d" % i) for i in range(3)]
        psd = [P.ps([128, T], F32, "psd%d" % i) for i in range(2)]
        xv = xT.rearrange("(k p) n -> p k n", p=128)
        ov = xoT.rearrange("(k p) n -> p k n", p=128)
        for t in range(NT // T):
            j = t % 2
            xt = xts[j]
            xk = keys("xt%d" % j, 8)
            P.dma("sp", xt[:], xv[:, :, t * T:(t + 1) * T], r=[], w=xk, key="xt%d" % j)
            emit_prenorm(C, xt, xk, g0T, "g0T", sq, "sq", h, "h", rstd, "rstd")
            hk = keys("h", 8)
            for f in range(32):
                pb = psu[f % 3]
                pn = "psu%d" % (f % 3)
                emit_proj(P, pb[:], pn, w1s, "w1", 8, f, lambda k: h[:, k, :], hk, T)
                rj = f % 2
                P.op("act", lambda e, pb=pb, rj=rj: e.activation(out=rl[rj][:], in_=pb[:], func=AF.Relu), r=[pn], w=["rl%d" % rj])
                veng = "dve" if f % 2 == 0 else "pool"
                P.op(veng, lambda e, rj=rj, f=f: e.tensor_tensor(out=a[:, f, :], in0=rl[rj][:], in1=rl[rj][:], op=ALU.mult),
                     r=["rl%d" % rj], w=[("a", f)])
            ak = keys("a", 32)
            for m in range(8):
                pb = psd[m % 2]
                pn = "psd%d" % (m % 2)
                emit_proj(P, pb[:], pn, w2s, "w2", 32, m, lambda k: a[:, k, :], ak, T)
                P.op("act", lambda e, pb=pb, m=m: e.activation(out=mo[:, m, :], in_=pb[:], func=AF.Copy), r=[pn], w=[("mo", m)])
                P.op("act", lambda e, pb=pb, m=m: e.activation(out=sq[:, m, :], in_=pb[:], func=AF.Square), r=[pn], w=[("sq", m)])
            emit_postnorm_residual(C, mo, "mo", sq, "sq", g1T, "g1T", xt, xk, rstd, "rstd", tmp, "tmp")
            P.dma("act", ov[:, :, t * T:(t + 1) * T], xt[:], r=xk, w=[], key="xo%d" % j, final=True)
        P.barrier()
        P.cur = P.st


def wlay(w):
    kk = w.shape[0] // 128
    return np.ascontiguousarray(w.reshape(kk, 128, w.shape[1]).transpose(1, 0, 2))


def glay(g):
    return np.ascontiguousarray(g.reshape(-1, 128).T)


_PROGS = {}


def prog(name, builder):
    if name not in _PROGS:
        _PROGS[name] = builder()
    return _PROGS[name]


def run(nc, in_maps):
    res = run_bass_kernel_spmd(nc, in_maps, core_ids=list(range(NCORES)))
    return res.results


def barrier(P):
    engs = ("pe", "dve", "act", "pool", "sp")
    for e in engs:
        eng = P.eng[e]
        for e2 in engs:
            if e2 != e and P.cnt[e2] > P.seen[e].get(id(P.csem[e2]), 0):
                eng.wait_ge(P.csem[e2], P.cnt[e2])
                P.seen[e][id(P.csem[e2])] = P.cnt[e2]
        for sem, cntv in P.dsem.values():
            if cntv > P.seen[e].get(id(sem), 0):
                eng.wait_ge(sem, cntv)
                P.seen[e][id(sem)] = cntv


Prog.barrier = barrier


def emit_PC1(P, io, odd, T=512):
    nc = P.nc
    din = lambda name, shape, dt=F32: io[name]
    xT = din("xT", [D, NT])
    y_all = io["y_all"]
    idxy_d = io["idx_y"]
    if odd:
        wglu_d = din("wglu", [128, 4, 512])
    memT = din("memT", [D, NMEM])
    wout_d = din("wout", [128, 8, 1024])
    wq_d = din("wq", [128, 8, 1024])
    wk_d = din("wk", [128, 8, 1024])
    wv_d = din("wv", [128, 8, 1024])
    wo_d = din("wo", [128, 8, 1024])
    gains_d = din("gains", [128, 4, 8])
    xoT = io["xoT"]
    with ExitStack() as st:
        P.cur = st
        C = DenseCtx(P, T)
        idxy = P.sb([128, 32], mybir.dt.int32, "idxy")
        P.dma("sp", idxy[:], idxy_d, r=[], w=["idxy"], key="idxy")
        gains = P.sb([128, 4, 8], F32, "gains")
        P.dma("sp", gains[:], gains_d, r=[], w=["gains"], key="gains")
        wout = P.sb([128, 8, 1024], BF16, "wout")
        wq = P.sb([128, 8, 1024], BF16, "wq")
        wo = P.sb([128, 8, 1024], BF16, "wo")
        kT = P.sb([128, 8, NMEM], BF16, "kT")
        vS = P.sb([128, 2, 1024], BF16, "vS")
        psg = [P.ps([128, T], F32, "psg%d" % i) for i in range(3)]
        pss = [P.ps([128, T], F32, "pss%d" % i) for i in range(2)]
        psd = P.ps([128, T], F32, "psden")
        pso = P.ps([128, T], F32, "pspv")
        with ExitStack() as st2:
            wk = P.sb([128, 8, 1024], BF16, "wk", st2)
            wv = P.sb([128, 8, 1024], BF16, "wv", st2)
            mt = P.sb([128, 8, NMEM], F32, "mt", st2)
            msq = P.sb([128, 8, NMEM], BF16, "msq", st2)
            mn = P.sb([128, 8, NMEM], BF16, "mn", st2)
            mrstd = P.sb([128, T], F32, "mrstd", st2)
            load_weight(P, wk_d, wk, "wk", 8)
            load_weight(P, wv_d, wv, "wv", 8)
            P.dma("sp", mt[:], memT.rearrange("(k p) n -> p k n", p=128), r=[], w=keys("mt", 8), key="mt")
            for k in range(8):
                P.op("act", lambda e, k=k: e.activation(out=msq[:, k, :], in_=mt[:, k, :], func=AF.Square), r=[("mt", k)], w=[("msq", k)])
            ps = C.ps_stat[0]
            for k in range(8):
                P.op("pe", lambda e, k=k: e.matmul(ps[:, 0:NMEM], C.ones[:], msq[:, k, :], start=(k == 0), stop=(k == 7)),
                     r=["ones", ("msq", k)], w=["ps_stat0"])
            P.op("act", lambda e: e.activation(out=C.lnt[:, 0:NMEM], in_=ps[:, 0:NMEM], func=AF.Ln, bias=C.eps[:, 0:1], scale=1.0 / D),
                 r=["ps_stat0", "eps"], w=["lnt"])
            P.op("act", lambda e: e.activation(out=mrstd[:, 0:NMEM], in_=C.lnt[:, 0:NMEM], func=AF.Exp, scale=-0.5), r=["lnt"], w=["mrstd"])
            for k in range(8):
                P.op("dve", lambda e, k=k: e.scalar_tensor_tensor(out=mn[:, k, :], in0=mt[:, k, :], scalar=gains[:, 3, k:k + 1],
                                                                  in1=mrstd[:, 0:NMEM], op0=ALU.mult, op1=ALU.mult),
                     r=[("mt", k), "gains", "mrstd"], w=[("mn", k)])
            for m in range(8):
                pb = psg[m % 3]
                pn = "psg%d" % (m % 3)
                for k in range(8):
                    P.op("pe", lambda e, k=k, m=m, pb=pb: e.matmul(pb[:, 0:NMEM], wk[:, k, 128 * m:128 * m + 128], mn[:, k, :],
                                                                   start=(k == 0), stop=(k == 7)),
                         r=["wk", ("mn", k)], w=[pn])
                P.op("act", lambda e, m=m, pb=pb: e.activation(out=kT[:, m, :], in_=pb[:, 0:NMEM], func=AF.Copy), r=[pn], w=[("kT", m)])
            for j in range(2):
                for dh in range(2):
                    i = j * 2 + dh
                    pb = psg[i % 3]
                    pn = "psg%d" % (i % 3)
                    for k in range(8):
                        P.op("pe", lambda e, k=k, j=j, dh=dh, pb=pb: e.matmul(pb[:, 0:512], mn[:, k, 128 * j:128 * j + 128],
                                                                            wv[:, k, 512 * dh:512 * dh + 512], start=(k == 0), stop=(k == 7)),
                             r=["wv", ("mn", k)], w=[pn])
                    P.op("act", lambda e, j=j, dh=dh, pb=pb: e.activation(out=vS[:, j, 512 * dh:512 * dh + 512], in_=pb[:, 0:512], func=AF.Copy),
                         r=[pn], w=[("vS", j, dh)])
            P.barrier()
        vkeys = lambda j: [("vS", j, 0), ("vS", j, 1)]
        load_weight(P, wout_d, wout, "wout", 8)
        load_weight(P, wq_d, wq, "wq", 8)
        load_weight(P, wo_d, wo, "wo", 8)
        if odd:
            wglu = P.sb([128, 4, 512], BF16, "wglu")
            load_weight(P, wglu_d, wglu, "wglu", 4)
            gt = [P.sb([128, T], F32, "gt%d" % i) for i in range(2)]
            gg = P.sb([128, 4, T], BF16, "gg")
            glu = P.sb([128, 4, T], BF16, "glu")
        xt = P.sb([128, 8, T], F32, "xt")
        yt = P.sb([128, 8, T], BF16, "yt")
        sq = P.sb([128, 8, T], BF16, "sq")
        h = P.sb([128, 8, T], BF16, "h")
        mo = P.sb([128, 8, T], F32, "mo")
        tmp = P.sb([128, 2, T], F32, "tmp")
        rstd = P.sb([128, T], F32, "rstd")
        qT = P.sb([128, 8, T], BF16, "qT")
        ee = [P.sb([128, 2, T], BF16, "ee%d" % i) for i in range(2)]
        rden = [P.sb([128, T], F32, "rden%d" % i) for i in range(2)]
        lnd = P.sb([128, T], F32, "lnd")
        oT = P.sb([128, 8, T], BF16, "oT")
        xv = xT.rearrange("(k p) n -> p k n", p=128)
        ov = xoT.rearrange("(k p) n -> p k n", p=128)
        xk = keys("xt", 8)
        for t in range(NT // T):
            cs = slice(t * T, (t + 1) * T)
            P.dma("sp", xt[:], xv[:, :, cs], r=[], w=xk, key="xt")
            for k in range(8):
                tok = P.op("pool", lambda g, k=k, t=t: g.indirect_dma_start(out=yt[:, k, :], out_offset=None, in_=y_all,
                                                                           in_offset=bass.IndirectOffsetOnAxis(ap=idxy[:, 4 * k + t:4 * k + t + 1], axis=0)),
                           r=["idxy", "y_all"], w=[("yt", k)], dma="yt")
            for k in range(8):
                P.lastw[("yt", k)] = tok
            rhs_keys = keys("yt", 8)
            rhs_fn = lambda k: yt[:, k, :]
            if odd:
                for c in range(4):
                    g0, g1 = gt[0], gt[1]
                    P.op("dve", lambda e, c=c: e.tensor_copy(out=g0[:], in_=yt[:, c, :]), r=[("yt", c)], w=["gt0"])
                    P.op("act", lambda e: e.activation(out=g1[:], in_=g0[:], func=AF.Square), r=["gt0"], w=["gt1"])
                    P.op("dve", lambda e: e.tensor_scalar(out=g1[:], in0=g1[:], scalar1=0.044715, scalar2=1.0, op0=ALU.mult, op1=ALU.add),
                         r=["gt1"], w=["gt1"])
                    P.op("dve", lambda e: e.tensor_tensor(out=g1[:], in0=g1[:], in1=g0[:], op=ALU.mult), r=["gt1", "gt0"], w=["gt1"])
                    P.op("act", lambda e: e.activation(out=g1[:], in_=g1[:], func=AF.Sigmoid, scale=1.5957691216057308), r=["gt1"], w=["gt1"])
                    P.op("dve", lambda e, c=c: e.tensor_tensor(out=gg[:, c, :], in0=g0[:], in1=g1[:], op=ALU.mult),
                         r=["gt0", "gt1"], w=[("gg", c)])
                for m in range(4):
                    pb = psg[m % 3]
                    pn = "psg%d" % (m % 3)
                    for c in range(4):
                        P.op("pe", lambda e, c=c, m=m, pb=pb: e.matmul(pb[:], wglu[:, c, 128 * m:128 * m + 128], gg[:, c, :],
                                                                       start=(c == 0), stop=(c == 3)),
                             r=["wglu", ("gg", c)], w=[pn])
                    P.op("act", lambda e, pb=pb: e.activation(out=gt[1][:], in_=pb[:], func=AF.Sigmoid), r=[pn], w=["gt1"])
                    P.op("dve", lambda e, m=m: e.tensor_tensor(out=glu[:, m, :], in0=gg[:, m, :], in1=gt[1][:], op=ALU.mult),
                         r=[("gg", m), "gt1"], w=[("glu", m)])
                rhs_keys = keys("glu", 4) + keys("yt", 8)[4:]
                rhs_fn = lambda k: (glu[:, k, :] if k < 4 else yt[:, k, :])

            def proj_postnorm(wsb, wname, rfn, rkeys, gidx):
                for m in range(8):
                    pb = psg[m % 3]
                    pn = "psg%d" % (m % 3)
                    emit_proj(P, pb[:], pn, wsb, wname, 8, m, rfn, rkeys, T)
                    P.op("act", lambda e, pb=pb, m=m: e.activation(out=mo[:, m, :], in_=pb[:], func=AF.Copy), r=[pn], w=[("mo", m)])
                    P.op("act", lambda e, pb=pb, m=m: e.activation(out=sq[:, m, :], in_=pb[:], func=AF.Square), r=[pn], w=[("sq", m)])
                emit_postnorm_residual(C, mo, "mo", sq, "sq", gains[:, gidx, :], "gains", xt, xk, rstd, "rstd", tmp, "tmp")

            proj_postnorm(wout, "wout", rhs_fn, rhs_keys, 0)
            emit_prenorm(C, xt, xk, gains[:, 1, :], "gains", sq, "sq", h, "h", rstd, "rstd")
            for m in range(8):
                pb = psg[m % 3]
                pn = "psg%d" % (m % 3)
                emit_proj(P, pb[:], pn, wq, "wq", 8, m, lambda k: h[:, k, :], keys("h", 8), T)
                P.op("act", lambda e, pb=pb, m=m: e.activation(out=qT[:, m, :], in_=pb[:], func=AF.Copy, scale=1.0 / 16.0), r=[pn], w=[("qT", m)])
            for hd in range(4):
                e2 = ee[hd % 2]
                en = "ee%d" % (hd % 2)
                for j in range(2):
                    pb = pss[j]
                    pn = "pss%d" % j
                    for dd in range(2):
                        dch = 2 * hd + dd
                        P.op("pe", lambda e, pb=pb, dch=dch, j=j, dd=dd: e.matmul(pb[:], kT[:, dch, 128 * j:128 * j + 128], qT[:, dch, :],
                                                                                 start=(dd == 0), stop=(dd == 1)),
                             r=[("kT", dch), ("qT", dch)], w=[pn])
                    P.op("act", lambda e, pb=pb, j=j, e2=e2: e.activation(out=e2[:, j, :], in_=pb[:], func=AF.Exp), r=[pn], w=[(en, j)])
                for j in range(2):
                    P.op("pe", lambda e, j=j, e2=e2: e.matmul(psd[:], C.ones[:], e2[:, j, :], start=(j == 0), stop=(j == 1)),
                         r=["ones", (en, j)], w=["psden"])
                rd = rden[hd % 2]
                rn = "rden%d" % (hd % 2)
                P.op("act", lambda e: e.activation(out=lnd[:], in_=psd[:], func=AF.Ln), r=["psden"], w=["lnd"])
                P.op("act", lambda e, rd=rd: e.activation(out=rd[:], in_=lnd[:], func=AF.Exp, scale=-1.0), r=["lnd"], w=[rn])
                for dd in range(2):
                    dch = 2 * hd + dd
                    for j in range(2):
                        P.op("pe", lambda e, j=j, dch=dch, e2=e2: e.matmul(pso[:], vS[:, j, 128 * dch:128 * dch + 128], e2[:, j, :],
                                                                          start=(j == 0), stop=(j == 1)),
                             r=vkeys(j) + [(en, j)], w=["pspv"])
                    P.op("dve", lambda e, dch=dch, rd=rd: e.tensor_tensor(out=oT[:, dch, :], in0=pso[:], in1=rd[:], op=ALU.mult),
                         r=["pspv", rn], w=[("oT", dch)])
            proj_postnorm(wo, "wo", lambda k: oT[:, k, :], keys("oT", 8), 2)
            P.dma("act", ov[:, :, cs], xt[:], r=xk, w=[], key="xo", final=True)
        P.barrier()
        P.cur = P.st


def emit_PBE(P, io):
    nc = P.nc
    din = lambda name, shape, dt=F32: io[name]
    pu_d = din("pu", [128, L], BF16)
    poolw_d = din("poolw", [128, 1, 128])
    pscale_d = din("pscale", [128, 1])
    msel_d = din("msel", [4, L])
    cb_d = din("cb", [128, L], BF16)
    cc_d = din("cc", [128, L], BF16)
    ch_d = din("ch", [128, L], BF16)
    convw_d = din("convw", [128, 3])
    ypool_d = io["ypool"]
    yconv_d = io["yconv"]
    H = 8
    Wd = L + 2 * H
    BL = 2048
    with ExitStack() as st:
        P.cur = st
        psb = [P.ps([128, 512], F32, "psb%d" % i) for i in range(2)]
        with ExitStack() as st2:
            u = P.sb([128, Wd], BF16, "u", st2)
            A = P.sb([128, Wd], F32, "A", st2)
            Bf = P.sb([128, Wd], F32, "Bf", st2)
            acc = P.sb([128, L], F32, "acc", st2)
            mt = [P.sb([128, BL], F32, "mt%d" % i, st2) for i in range(2)]
            tmpb = P.sb([128, BL], F32, "tmpb", st2)
            pooled = P.sb([128, L], BF16, "pooled", st2)
            pw = P.sb([128, 1, 128], BF16, "pw", st2)
            psc = P.sb([128, 1], F32, "psc", st2)
            ost = [P.sb([128, 2048], BF16, "ost%d" % i, st2) for i in range(2)]
            load_weight(P, poolw_d, pw, "pw", 1)
            P.dma("sp", psc[:], pscale_d, r=[], w=["psc"], key="psc")
            P.op("dve", lambda e: e.memset(u[:, 0:H], 0.0), w=["u"])
            P.op("dve", lambda e: e.memset(u[:, Wd - H:Wd], 0.0), w=["u"])
            P.op("pool", lambda e: e.memset(A[:], 0.0), w=["A"])
            P.op("pool", lambda e: e.memset(Bf[:], 0.0), w=["Bf"])
            P.dma("sp", u[:, H:H + L], pu_d, r=[], w=["u"], key="u")
            nm = 0

            def accumulate(src, srcname, i):
                nonlocal nm
                for blk in range(L // BL):
                    m = mt[nm % 2]
                    mn_ = "mt%d" % (nm % 2)
                    nm += 1
                    P.dma("sp", m[:], msel_d[i:i + 1, blk * BL:(blk + 1) * BL].partition_broadcast(128), r=[], w=[mn_], key=mn_)
                    s_ = src[:, H + blk * BL:H + (blk + 1) * BL]
                    a_ = acc[:, blk * BL:(blk + 1) * BL]
                    if i == 0:
                        P.op("dve", lambda e, s_=s_, a_=a_, m=m: e.tensor_tensor(out=a_, in0=s_, in1=m[:], op=ALU.mult),
                             r=[srcname, mn_], w=[("acc", blk)])
                    else:
                        P.op("dve", lambda e, s_=s_, m=m: e.tensor_tensor(out=tmpb[:], in0=s_, in1=m[:], op=ALU.mult),
                             r=[srcname, mn_], w=["tmpb"])
                        P.op("pool", lambda e, a_=a_: e.tensor_tensor(out=a_, in0=a_, in1=tmpb[:], op=ALU.add),
                             r=["tmpb", ("acc", blk)], w=[("acc", blk)])

            P.op("dve", lambda e: e.tensor_tensor(out=A[:, 1:Wd], in0=u[:, 0:Wd - 1], in1=u[:, 1:Wd], op=ALU.add), r=["u"], w=["A"])
            accumulate(A, "A", 0)
            P.op("dve", lambda e: e.tensor_tensor(out=Bf[:, 2:Wd - 2], in0=A[:, 1:Wd - 3], in1=A[:, 3:Wd - 1], op=ALU.add), r=["A"], w=["Bf"])
            accumulate(Bf, "Bf", 1)
            P.op("dve", lambda e: e.tensor_tensor(out=A[:, 4:Wd - 4], in0=Bf[:, 2:Wd - 6], in1=Bf[:, 6:Wd - 2], op=ALU.add), r=["Bf"], w=["A"])
            accumulate(A, "A", 2)
            P.op("dve", lambda e: e.tensor_tensor(out=Bf[:, 8:Wd - 8], in0=A[:, 4:Wd - 12], in1=A[:, 12:Wd - 4], op=ALU.add), r=["A"], w=["Bf"])
            accumulate(Bf, "Bf", 3)
            for blk in range(L // BL):
                sl = slice(blk * BL, (blk + 1) * BL)
                P.op("dve", lambda e, sl=sl, blk=blk: e.tensor_tensor(out=pooled[:, sl], in0=acc[:, sl], in1=u[:, H + blk * BL:H + (blk + 1) * BL],
                                                                      op=ALU.subtract),
                     r=[("acc", blk), "u"], w=[("pooled", blk)])
            for blk in range(L // BL):
                o_ = ost[blk % 2]
                on = "ost%d" % (blk % 2)
                for s in range(BL // 512):
                    i = blk * 4 + s
                    pb = psb[i % 2]
                    pn = "psb%d" % (i % 2)
                    P.op("pe", lambda e, pb=pb, i=i: e.matmul(pb[:], pw[:, 0, :], pooled[:, i * 512:(i + 1) * 512], start=True, stop=True),
                         r=["pw", ("pooled", blk)], w=[pn])
                    P.op("act", lambda e, pb=pb, o_=o_, s=s: e.activation(out=o_[:, s * 512:(s + 1) * 512], in_=pb[:], func=AF.Copy, scale=psc[:, 0:1]),
                         r=[pn, "psc"], w=[(on, s)])
                P.dma("act", ypool_d[:, blk * BL:(blk + 1) * BL], o_[:], r=keys(on, 4), w=[], key=on, final=True)
            P.barrier()
        with ExitStack() as st3:
            cb = P.sb([128, L], BF16, "cb", st3)
            cc = P.sb([128, L], BF16, "cc", st3)
            chh = P.sb([128, L], BF16, "chh", st3)
            cw = P.sb([128, 3], F32, "cw", st3)
            mm_ = P.sb([128, L + 2], F32, "mm_", st3)
            aa = P.sb([128, L], F32, "aa", st3)
            yo = P.sb([128, L], BF16, "yo", st3)
            P.dma("sp", cb[:], cb_d, r=[], w=["cb"], key="cb")
            P.dma("sp", cc[:], cc_d, r=[], w=["cc"], key="cc")
            P.dma("sp", chh[:], ch_d, r=[], w=["chh"], key="chh")
            P.dma("sp", cw[:], convw_d, r=[], w=["cw"], key="cw")
            P.op("pool", lambda e: e.memset(mm_[:, 0:1], 0.0), w=["mm_"])
            P.op("pool", lambda e: e.memset(mm_[:, L + 1:L + 2], 0.0), w=["mm_"])
            P.op("dve", lambda e: e.tensor_tensor(out=mm_[:, 1:L + 1], in0=cc[:], in1=chh[:], op=ALU.mult), r=["cc", "chh"], w=["mm_"])
            P.op("dve", lambda e: e.tensor_scalar(out=aa[:], in0=mm_[:, 1:L + 1], scalar1=cw[:, 1:2], scalar2=None, op0=ALU.mult),
                 r=["mm_", "cw"], w=["aa"])
            P.op("dve", lambda e: e.scalar_tensor_tensor(out=aa[:], in0=mm_[:, 0:L], scalar=cw[:, 0:1], in1=aa[:], op0=ALU.mult, op1=ALU.add),
                 r=["mm_", "cw", "aa"], w=["aa"])
            P.op("dve", lambda e: e.scalar_tensor_tensor(out=aa[:], in0=mm_[:, 2:L + 2], scalar=cw[:, 2:3], in1=aa[:], op0=ALU.mult, op1=ALU.add),
                 r=["mm_", "cw", "aa"], w=["aa"])
            P.op("pool", lambda e: e.tensor_tensor(out=yo[:], in0=aa[:], in1=cb[:], op=ALU.mult), r=["aa", "cb"], w=["yo"])
            P.dma("act", yconv_d, yo[:], r=["yo"], w=[], key="yo", final=True)
            P.barrier()
        P.cur = P.st


def rev_ap(ap, n):
    a = ap
    return AP(a.tensor, a.offset + (n - 1), [list(a.ap[0]), [-1, n]])


def emit_PBS5(P, io):
    nc = P.nc
    din = lambda name, shape, dt=F32: io[name]
    u_d = din("u", [2, 32, 2, L], BF16)
    par_d = din("par", [2, 2, 128, 4])
    bmat_d = din("bmat", [2, 128, 2, 16])
    cmat_d = din("cmat", [2, 2, 128, 2, 16])
    dsk_d = din("dsk", [2, 32, 1])
    ident_d = din("ident", [128, 128])
    ys_d = io["ys"]
    ys0_d = io["ys0"]
    BLK = 1024
    NSUB = BLK // 512
    NB = L // BLK
    with ExitStack() as st:
        P.cur = st
        ident = P.sb([128, 128], F32, "ident")
        P.dma("sp", ident[:], ident_d, r=[], w=["ident"], key="ident")
        tv = P.sb([128, BLK], F32, "tv")
        P.op("pool", lambda e: e.iota(tv[:], pattern=[[1, BLK]], base=0, channel_multiplier=0, allow_small_or_imprecise_dtypes=True), w=["tv"])
        sn = P.sb([128, BLK], F32, "sn")
        cs = P.sb([128, BLK], F32, "cs")
        wri = [(P.sb([128, BLK], F32, "wre%d" % i), P.sb([128, BLK], F32, "wim%d" % i)) for i in range(2)]
        xri = [(P.sb([128, BLK], F32, "xr%d" % i), P.sb([128, BLK], F32, "xi%d" % i)) for i in range(2)]
        m_ = [P.sb([128, BLK], F32, "m%d" % i) for i in range(4)]
        bri = [(P.sb([128, BLK], F32, "bre%d" % i), P.sb([128, BLK], F32, "bim%d" % i)) for i in range(2)]
        dm = [[P.sb([128, BLK], BF16, "dm%d_%d" % (i, j)) for j in range(4)] for i in range(2)]
        G = lambda fn, r, w: P.op("pool", fn, r=r, w=w)
        ub = [P.sb([32, BLK], BF16, "ub%d" % i) for i in range(3)]
        ystage = [P.sb([32, BLK], BF16, "ystage%d" % i) for i in range(2)]
        carry = P.sb([128, 8], F32, "carry")
        yprev = P.sb([32, BLK], BF16, "yprev")
        par = P.sb([128, 4], F32, "par")
        bm = P.sb([128, 2, 16], F32, "bm")
        cm = P.sb([128, 2, 16], F32, "cm")
        sc = P.sb([128, 32], F32, "sc")
        cb16 = P.sb([128, 2, 16], F32, "cb16")
        t16 = P.sb([128, 2, 16], F32, "t16")
        bd = P.sb([128, 2, 32], F32, "bd")
        lb16 = P.sb([32, 2, 128], BF16, "lb16")
        cbd = P.sb([128, 3, 32], BF16, "cbd")
        dsk = P.sb([32, 1], F32, "dsk")
        ddiag = P.sb([32, 32], BF16, "ddiag")
        ps_ri = [[P.ps([128, 512], F32, "ps_ri%d%d" % (i, j)) for j in range(2)] for i in range(2)]
        ps_y = [P.ps([32, 512], F32, "ps_y%d" % i) for i in range(2)]
        ps_t = P.ps([32, 128], F32, "ps_t")
        col = lambda i: sc[:, i:i + 1]
        (LRE, DT, AA, RHO, TH, THT, RN, FR, SINT, COST, LBR, LBI, NUMR, MAG, INV, CR, CI, NCI, T1, T2, TC, LIM, RT, RS, RC, NRS) = range(26)
        V = lambda fn, r, w: P.op("dve", fn, r=r, w=w)
        nys = 0
        nub = 0
        for gp in range(2):
            P.dma("sp", bm[:], bmat_d[gp], r=[], w=["bm"], key="bm")
            P.dma("sp", dsk[:], dsk_d[gp], r=[], w=["dsk"], key="dsk")
            V(lambda e: e.tensor_scalar(out=ddiag[:], in0=ident[0:32, 0:32], scalar1=dsk[:, 0:1], scalar2=None, op0=ALU.mult),
              ["ident", "dsk"], ["ddiag"])
            for d in range(2):
                P.dma("sp", par[:], par_d[gp, d], r=[], w=["par"], key="par")
                P.dma("sp", cm[:], cmat_d[gp, d], r=[], w=["cm"], key="cm")
                S = ["sc"]
                V(lambda e: e.tensor_scalar(out=col(LRE), in0=par[:, 0:1], scalar1=-1e-4, scalar2=None, op0=ALU.min), ["par"], S)
                V(lambda e: e.tensor_copy(out=col(LIM), in_=par[:, 1:2]), ["par"], S)
                P.op("act", lambda e: e.activation(out=col(DT), in_=par[:, 2:3], func=AF.Exp), r=["par"], w=S)
                V(lambda e: e.tensor_tensor(out=col(AA), in0=col(LRE), in1=col(DT), op=ALU.mult), S, S)
                P.op("act", lambda e: e.activation(out=col(RHO), in_=col(AA), func=AF.Exp), r=S, w=S)
                V(lambda e: e.tensor_tensor(out=col(TH), in0=col(LIM), in1=col(DT), op=ALU.mult), S, S)
                V(lambda e: e.tensor_scalar(out=col(THT), in0=col(TH), scalar1=1.0 / TWO_PI, scalar2=None, op0=ALU.mult), S, S)
                V(lambda e: e.tensor_scalar(out=col(RN), in0=col(THT), scalar1=MAGIC, scalar2=MAGIC, op0=ALU.add, op1=ALU.subtract), S, S)
                V(lambda e: e.tensor_tensor(out=col(FR), in0=col(THT), in1=col(RN), op=ALU.subtract), S, S)
                P.op("act", lambda e: e.activation(out=col(SINT), in_=col(FR), func=AF.Sin, scale=TWO_PI), r=S, w=S)
                V(lambda e: e.tensor_scalar(out=col(TC), in0=col(THT), scalar1=0.25, scalar2=None, op0=ALU.add), S, S)
                V(lambda e: e.tensor_scalar(out=col(RN), in0=col(TC), scalar1=MAGIC, scalar2=MAGIC, op0=ALU.add, op1=ALU.subtract), S, S)
                V(lambda e: e.tensor_tensor(out=col(FR), in0=col(TC), in1=col(RN), op=ALU.subtract), S, S)
                P.op("act", lambda e: e.activation(out=col(COST), in_=col(FR), func=AF.Sin, scale=TWO_PI), r=S, w=S)
                V(lambda e: e.tensor_tensor(out=col(LBR), in0=col(RHO), in1=col(COST), op=ALU.mult), S, S)
                V(lambda e: e.tensor_tensor(out=col(LBI), in0=col(RHO), in1=col(SINT), op=ALU.mult), S, S)
                V(lambda e: e.tensor_scalar(out=col(NUMR), in0=col(LBR), scalar1=-1.0, scalar2=None, op0=ALU.add), S, S)
                V(lambda e: e.tensor_tensor(out=col(MAG), in0=col(LRE), in1=col(LRE), op=ALU.mult), S, S)
                V(lambda e: e.tensor_tensor(out=col(T1), in0=col(LIM), in1=col(LIM), op=ALU.mult), S, S)
                V(lambda e: e.tensor_tensor(out=col(MAG), in0=col(MAG), in1=col(T1), op=ALU.add), S, S)
                V(lambda e: e.reciprocal(out=col(INV), in_=col(MAG)), S, S)
                V(lambda e: e.tensor_tensor(out=col(T1), in0=col(NUMR), in1=col(LRE), op=ALU.mult), S, S)
                V(lambda e: e.tensor_tensor(out=col(T2), in0=col(LBI), in1=col(LIM), op=ALU.mult), S, S)
                V(lambda e: e.tensor_tensor(out=col(T1), in0=col(T1), in1=col(T2), op=ALU.add), S, S)
                V(lambda e: e.tensor_tensor(out=col(CR), in0=col(T1), in1=col(INV), op=ALU.mult), S, S)
                V(lambda e: e.tensor_tensor(out=col(T1), in0=col(LBI), in1=col(LRE), op=ALU.mult), S, S)
                V(lambda e: e.tensor_tensor(out=col(T2), in0=col(NUMR), in1=col(LIM), op=ALU.mult), S, S)
                V(lambda e: e.tensor_tensor(out=col(T1), in0=col(T1), in1=col(T2), op=ALU.subtract), S, S)
                V(lambda e: e.tensor_tensor(out=col(CI), in0=col(T1), in1=col(INV), op=ALU.mult), S, S)
                V(lambda e: e.tensor_scalar(out=col(NCI), in0=col(CI), scalar1=-1.0, scalar2=None, op0=ALU.mult), S, S)
                V(lambda e: e.tensor_scalar(out=t16[:, 0, :], in0=bm[:, 0, :], scalar1=col(CR), scalar2=None, op0=ALU.mult), ["bm"] + S, ["t16"])
                V(lambda e: e.scalar_tensor_tensor(out=cb16[:, 0, :], in0=bm[:, 1, :], scalar=col(NCI), in1=t16[:, 0, :], op0=ALU.mult, op1=ALU.add),
                  ["bm", "t16"] + S, ["cb16"])
                V(lambda e: e.tensor_scalar(out=t16[:, 1, :], in0=bm[:, 1, :], scalar1=col(CR), scalar2=None, op0=ALU.mult), ["bm"] + S, ["t16"])
                V(lambda e: e.scalar_tensor_tensor(out=cb16[:, 1, :], in0=bm[:, 0, :], scalar=col(CI), in1=t16[:, 1, :], op0=ALU.mult, op1=ALU.add),
                  ["bm", "t16"] + S, ["cb16"])
                V(lambda e: e.memset(bd[:], 0.0), [], ["bd"])
                for ri in range(2):
                    V(lambda e, ri=ri: e.tensor_copy(out=bd[0:64, ri, 0:16], in_=cb16[0:64, ri, :]), ["cb16"], ["bd"])
                    V(lambda e, ri=ri: e.tensor_copy(out=bd[64:128, ri, 16:32], in_=cb16[64:128, ri, :]), ["cb16"], ["bd"])
                for ri in range(2):
                    P.op("pe", lambda e, ri=ri: e.transpose(out=ps_t[:], in_=bd[:, ri, :], identity=ident[:]), r=["bd", "ident"], w=["ps_t"])
                    V(lambda e, ri=ri: e.tensor_copy(out=lb16[:, ri, :], in_=ps_t[:]), ["ps_t"], [("lb16", ri)])
                V(lambda e: e.memset(cbd[:], 0.0), [], ["cbd"])
                V(lambda e: e.tensor_copy(out=cbd[0:64, 0, 0:16], in_=cm[0:64, 0, :]), ["cm"], ["cbd"])
                V(lambda e: e.tensor_copy(out=cbd[64:128, 0, 16:32], in_=cm[64:128, 0, :]), ["cm"], ["cbd"])
                V(lambda e: e.tensor_scalar(out=cbd[0:64, 1, 0:16], in0=cm[0:64, 1, :], scalar1=-1.0, scalar2=None, op0=ALU.mult), ["cm"], ["cbd"])
                V(lambda e: e.tensor_scalar(out=cbd[64:128, 1, 16:32], in0=cm[64:128, 1, :], scalar1=-1.0, scalar2=None, op0=ALU.mult), ["cm"], ["cbd"])
                V(lambda e: e.tensor_scalar(out=cbd[0:64, 2, 0:16], in0=cm[0:64, 0, :], scalar1=-1.0, scalar2=None, op0=ALU.mult), ["cm"], ["cbd"])
                V(lambda e: e.tensor_scalar(out=cbd[64:128, 2, 16:32], in0=cm[64:128, 0, :], scalar1=-1.0, scalar2=None, op0=ALU.mult), ["cm"], ["cbd"])
                V(lambda e: e.tensor_scalar(out=m_[0][:], in0=tv[:], scalar1=col(THT), scalar2=None, op0=ALU.mult), ["tv"] + S, ["m0"])
                V(lambda e: e.tensor_scalar(out=m_[1][:], in0=m_[0][:], scalar1=MAGIC, scalar2=MAGIC, op0=ALU.add, op1=ALU.subtract), ["m0"], ["m1"])
                V(lambda e: e.tensor_tensor(out=m_[1][:], in0=m_[0][:], in1=m_[1][:], op=ALU.subtract), ["m0", "m1"], ["m1"])
                P.op("act", lambda e: e.activation(out=sn[:], in_=m_[1][:], func=AF.Sin, scale=TWO_PI), r=["m1"], w=["sn"])
                V(lambda e: e.tensor_scalar(out=m_[0][:], in0=m_[0][:], scalar1=0.25, scalar2=None, op0=ALU.add), ["m0"], ["m0"])
                V(lambda e: e.tensor_scalar(out=m_[1][:], in0=m_[0][:], scalar1=MAGIC, scalar2=MAGIC, op0=ALU.add, op1=ALU.subtract), ["m0"], ["m1"])
                V(lambda e: e.tensor_tensor(out=m_[1][:], in0=m_[0][:], in1=m_[1][:], op=ALU.subtract), ["m0", "m1"], ["m1"])
                P.op("act", lambda e: e.activation(out=cs[:], in_=m_[1][:], func=AF.Sin, scale=TWO_PI), r=["m1"], w=["cs"])
                if d == 1:
                    V(lambda e: e.tensor_copy(out=m_[0][:], in_=rev_ap(cs[:], BLK)), ["cs"], ["m0"])
                    V(lambda e: e.tensor_copy(out=cs[:], in_=m_[0][:]), ["m0"], ["cs"])
                    V(lambda e: e.tensor_copy(out=m_[0][:], in_=rev_ap(sn[:], BLK)), ["sn"], ["m0"])
                    V(lambda e: e.tensor_copy(out=sn[:], in_=m_[0][:]), ["m0"], ["sn"])
                V(lambda e: e.tensor_scalar(out=col(RT), in0=col(THT), scalar1=float(BLK), scalar2=None, op0=ALU.mult), S, S)
                V(lambda e: e.tensor_scalar(out=col(RN), in0=col(RT), scalar1=MAGIC, scalar2=MAGIC, op0=ALU.add, op1=ALU.subtract), S, S)
                V(lambda e: e.tensor_tensor(out=col(FR), in0=col(RT), in1=col(RN), op=ALU.subtract), S, S)
                P.op("act", lambda e: e.activation(out=col(RS), in_=col(FR), func=AF.Sin, scale=TWO_PI), r=S, w=S)
                V(lambda e: e.tensor_scalar(out=col(TC), in0=col(RT), scalar1=0.25, scalar2=None, op0=ALU.add), S, S)
                V(lambda e: e.tensor_scalar(out=col(RN), in0=col(TC), scalar1=MAGIC, scalar2=MAGIC, op0=ALU.add, op1=ALU.subtract), S, S)
                V(lambda e: e.tensor_tensor(out=col(FR), in0=col(TC), in1=col(RN), op=ALU.subtract), S, S)
                P.op("act", lambda e: e.activation(out=col(RC), in_=col(FR), func=AF.Sin, scale=TWO_PI), r=S, w=S)
                V(lambda e: e.tensor_scalar(out=col(NRS), in0=col(RS), scalar1=-1.0, scalar2=None, op0=ALU.mult), S, S)
                order = list(range(NB)) if d == 0 else list(range(NB - 1, -1, -1))
                rho_bc = col(RHO).to_broadcast([128, BLK])
                units = [(oi, bi, b) for oi, bi in enumerate(order) for b in range(2)]

                def stage1(oi, bi, b):
                    nonlocal nub
                    t0 = bi * BLK
                    uj = nub % 3
                    nub += 1
                    ubt = ub[uj]
                    un = "ub%d" % uj
                    P.dma("sp", ubt[:], u_d[gp, :, b, t0:t0 + BLK], r=[], w=[un], key=un)
                    bre, bim = bri[b]
                    for s4 in range(NSUB):
                        c_ = slice(s4 * 512, (s4 + 1) * 512)
                        pr, pi = ps_ri[s4 % 2]
                        prn, pin = "ps_r%d" % (s4 % 2), "ps_i%d" % (s4 % 2)
                        P.op("pe", lambda e: e.matmul(pr[:], lb16[:, 0, :], ubt[:, c_], start=True, stop=True), r=[("lb16", 0), un], w=[prn])
                        P.op("pe", lambda e: e.matmul(pi[:], lb16[:, 1, :], ubt[:, c_], start=True, stop=True), r=[("lb16", 1), un], w=[pin])
                        P.op("act", lambda e: e.activation(out=bre[:, c_], in_=pr[:], func=AF.Copy), r=[prn], w=[("bre%d" % b, s4)])
                        P.op("act", lambda e: e.activation(out=bim[:, c_], in_=pi[:], func=AF.Copy), r=[pin], w=[("bim%d" % b, s4)])
                    return ubt, un

                def stageB(oi, bi, b):
                    bre, bim = bri[b]
                    wre, wim = wri[b]
                    brk, bik = keys("bre%d" % b, NSUB), keys("bim%d" % b, NSUB)
                    V(lambda e: e.tensor_tensor(out=m_[0][:], in0=bre[:], in1=cs[:], op=ALU.mult), brk + ["cs"], ["m0"])
                    V(lambda e: e.tensor_tensor(out=m_[1][:], in0=bim[:], in1=sn[:], op=ALU.mult), bik + ["sn"], ["m1"])
                    V(lambda e: e.tensor_tensor(out=wre[:], in0=m_[0][:], in1=m_[1][:], op=ALU.add), ["m0", "m1"], ["wre%d" % b])
                    V(lambda e: e.tensor_tensor(out=m_[2][:], in0=bim[:], in1=cs[:], op=ALU.mult), bik + ["cs"], ["m2"])
                    V(lambda e: e.tensor_tensor(out=m_[3][:], in0=bre[:], in1=sn[:], op=ALU.mult), brk + ["sn"], ["m3"])
                    V(lambda e: e.tensor_tensor(out=wim[:], in0=m_[2][:], in1=m_[3][:], op=ALU.subtract), ["m2", "m3"], ["wim%d" % b])

                def stage2(oi, bi, b):
                    wre, wim = wri[b]
                    xr, xi = xri[b]
                    if oi > 0:
                        cr_, ci_ = carry[:, 2 * b:2 * b + 1], carry[:, 2 * b + 1:2 * b + 2]
                        ir_, ii_ = carry[:, 4 + 2 * b:5 + 2 * b], carry[:, 5 + 2 * b:6 + 2 * b]
                        V(lambda e: e.tensor_scalar(out=col(T1), in0=cr_, scalar1=col(RC), scalar2=None, op0=ALU.mult), ["carry"] + S, S)
                        V(lambda e: e.scalar_tensor_tensor(out=ir_, in0=ci_, scalar=col(NRS), in1=col(T1), op0=ALU.mult, op1=ALU.add), ["carry"] + S, ["carry"])
                        V(lambda e: e.tensor_scalar(out=col(T2), in0=ci_, scalar1=col(RC), scalar2=None, op0=ALU.mult), ["carry"] + S, S)
                        V(lambda e: e.scalar_tensor_tensor(out=ii_, in0=cr_, scalar=col(RS), in1=col(T2), op0=ALU.mult, op1=ALU.add), ["carry"] + S, ["carry"])
                    for (wt, wn, xt_, xn, ci2) in ((wre, "wre%d" % b, xr, "xr%d" % b, 0), (wim, "wim%d" % b, xi, "xi%d" % b, 1)):
                        init = 0.0 if oi == 0 else carry[:, 4 + 2 * b + ci2:5 + 2 * b + ci2]
                        if d == 0:
                            dat, out_ = wt[:], xt_[:]
                            last = xt_[:, BLK - 1:BLK]
                        else:
                            dat, out_ = rev_ap(wt[:], BLK), rev_ap(xt_[:], BLK)
                            last = xt_[:, 0:1]
                        V(lambda e: e.tensor_tensor_scan(out=out_, data0=rho_bc, data1=dat, initial=init, op0=ALU.mult, op1=ALU.add),
                          [wn, "carry"] + S, [xn])
                        V(lambda e: e.tensor_copy(out=carry[:, 2 * b + ci2:2 * b + ci2 + 1], in_=last), [xn], ["carry"])

                def stage3(oi, bi, b, ubt, un):
                    nonlocal nys
                    t0 = bi * BLK
                    xr, xi = xri[b]
                    xrn, xin_ = "xr%d" % b, "xi%d" % b
                    q4 = dm[b]
                    qn = ["dm%d_%d" % (b, i) for i in range(4)]
                    V(lambda e: e.tensor_tensor(out=q4[0][:], in0=xr[:], in1=cs[:], op=ALU.mult), [xrn, "cs"], [qn[0]])
                    V(lambda e: e.tensor_tensor(out=q4[1][:], in0=xi[:], in1=sn[:], op=ALU.mult), [xin_, "sn"], [qn[1]])
                    V(lambda e: e.tensor_tensor(out=q4[2][:], in0=xr[:], in1=sn[:], op=ALU.mult), [xrn, "sn"], [qn[2]])
                    V(lambda e: e.tensor_tensor(out=q4[3][:], in0=xi[:], in1=cs[:], op=ALU.mult), [xin_, "cs"], [qn[3]])
                    yj = nys % 2
                    nys += 1
                    yst = ystage[yj]
                    yn = "ystage%d" % yj
                    for s4 in range(NSUB):
                        c_ = slice(s4 * 512, (s4 + 1) * 512)
                        py = ps_y[s4 % 2]
                        pyn = "ps_y%d" % (s4 % 2)
                        P.op("pe", lambda e: e.matmul(py[:], cbd[:, 0, :], q4[0][:, c_], start=True, stop=False), r=["cbd", qn[0]], w=[pyn])
                        P.op("pe", lambda e: e.matmul(py[:], cbd[:, 2, :], q4[1][:, c_], start=False, stop=False), r=["cbd", qn[1]], w=[pyn])
                        P.op("pe", lambda e: e.matmul(py[:], cbd[:, 1, :], q4[2][:, c_], start=False, stop=False), r=["cbd", qn[2]], w=[pyn])
                        P.op("pe", lambda e: e.matmul(py[:], cbd[:, 1, :], q4[3][:, c_], start=False, stop=(d == 1)), r=["cbd", qn[3]], w=[pyn])
                        if d == 0:
                            P.op("pe", lambda e: e.matmul(py[:], ddiag[:], ubt[:, c_], start=False, stop=True), r=["ddiag", un], w=[pyn])
                        P.op("act", lambda e: e.activation(out=yst[:, c_], in_=py[:], func=AF.Copy), r=[pyn], w=[(yn, s4)])
                    if d == 0:
                        P.dma("act", ys0_d[gp, :, b, t0:t0 + BLK], yst[:], r=keys(yn, NSUB), w=[("ys0", gp, b, bi)], key=yn)
                    else:
                        P.dma("sp", yprev[:], ys0_d[gp, :, b, t0:t0 + BLK], r=[("ys0", gp, b, bi)], w=["yprev"], key="yprev")
                        V(lambda e: e.tensor_tensor(out=yst[:], in0=yst[:], in1=yprev[:], op=ALU.add), keys(yn, NSUB) + ["yprev"], keys(yn, NSUB))
                        P.dma("act", ys_d[gp, :, b, t0:t0 + BLK], yst[:], r=keys(yn, NSUB), w=[], key=yn, final=True)

                nu = len(units)
                ubs = {0: stage1(*units[0])}
                if nu > 1:
                    ubs[1] = stage1(*units[1])
                stageB(*units[0])
                for k, u_ in enumerate(units):
                    if k + 2 < nu:
                        ubs[k + 2] = stage1(*units[k + 2])
                    if k + 1 < nu:
                        stageB(*units[k + 1])
                    stage2(*u_)
                    stage3(*u_, *ubs.pop(k))
        P.barrier()
        P.cur = P.st


def s5_inputs(u, lam_re, lam_im, log_dt, b_re, b_im, c_re, c_im, dsk):
    ident = np.eye(128, dtype=np.float32)
    ims = []
    for c in range(NCORES):
        g0 = 4 * c
        uu = None if u is None else np.ascontiguousarray(u[:, :, 64 * c:64 * c + 64].reshape(B, L, 2, 32).transpose(2, 3, 0, 1))
        par = np.zeros((2, 2, 128, 4), np.float32)
        bmat = np.zeros((2, 128, 2, 16), np.float32)
        cmat = np.zeros((2, 2, 128, 2, 16), np.float32)
        for gp in range(2):
            for g2 in range(2):
                g = g0 + 2 * gp + g2
                ps = slice(64 * g2, 64 * g2 + 64)
                bmat[gp, ps, 0] = b_re[g]
                bmat[gp, ps, 1] = b_im[g]
                for d in range(2):
                    par[gp, d, ps, 0] = lam_re[d, g]
                    par[gp, d, ps, 1] = lam_im[d, g]
                    par[gp, d, ps, 2] = log_dt[d, g]
                    cmat[gp, d, ps, 0] = c_re[d, g].T
                    cmat[gp, d, ps, 1] = c_im[d, g].T
        ims.append({"u": uu, "par": par, "bmat": bmat, "cmat": cmat,
                    "dsk": np.ascontiguousarray(dsk[64 * c:64 * c + 64].reshape(2, 32, 1)), "ident": ident})
    return ims


def s5_outputs(res):
    outs = []
    for c in range(NCORES):
        ys = np.asarray(res[c]["ys"])
        outs.append(ys.transpose(0, 3, 4, 1, 2).reshape(2, B, L, 64))
    return np.concatenate(outs, axis=-1)


def fview(ap, dims):
    return AP(ap.tensor, ap.offset, [list(ap.ap[0])] + [list(d) for d in dims])


NFFT = 2 * L


def hyena_tables():
    j = np.arange(128)
    ang = 2.0 * np.pi * np.outer(j, j) / 128.0
    Wr, Wi = np.cos(ang), -np.sin(ang)
    W1s = np.concatenate([np.concatenate([Wr[:64], Wi[:64]], 1), np.concatenate([-Wi[:64], Wr[:64]], 1)], 0)
    W1k = np.concatenate([Wr, Wi], 1)
    Wcr, Wci = np.cos(ang), np.sin(ang)
    W1i = np.stack([np.concatenate([Wcr, Wci], 1), np.concatenate([-Wci, Wcr], 1)], 1)
    tl = np.arange(128)[:, None, None]
    fl = np.arange(128)[None, :, None]
    fh = np.arange(128)[None, None, :]
    a = 2.0 * np.pi * ((tl * (fl + 128 * fh)) % NFFT) / NFFT
    Ef = np.stack([np.cos(a), -np.sin(a)], 2)
    ta = np.arange(64)[None, None, :]
    g = 2.0 * np.pi * ((tl * (fl + 128 * ta)) % NFFT) / NFFT
    Gr, Gi = np.cos(g), np.sin(g)
    Ei = np.stack([np.concatenate([Gr, Gi], 2), np.concatenate([-Gi, Gr], 2)], 2)
    bf = lambda x: np.ascontiguousarray(x.astype(np.float32).astype(ml_dtypes.bfloat16))
    return {"W1s": bf(W1s), "W1k": bf(W1k), "W1i": bf(W1i), "Ef": bf(Ef), "Ei": bf(Ei)}


def emit_PBHY(P, io):
    nc = P.nc
    din = lambda name, shape, dt=F32: io[name]
    hyp_t = io["hyp_t"]
    HB = io["hyp_base"]
    shw_d = din("shw", [1, 3 * 4 * 64])
    w1_d = din("hw1", [33, 64])
    w2_d = din("hw2", [64, 64])
    w3_d = din("hw3", [64, 2, 128])
    mlpv_d = din("mlpv", [64, 3])
    hbias_d = din("hbias", [128, 1])
    ndelta_d = din("ndelta", [128, 1])
    zT_d = din("zT", [2, 33, L])
    tn_d = din("tn", [2, 1, L])
    W1s_d = din("W1s", [128, 256], BF16)
    W1k_d = din("W1k", [128, 256], BF16)
    W1i_d = din("W1i", [128, 2, 256], BF16)
    Ef_d = din("Ef", [128, 128, 2, 128], BF16)
    Ei_d = din("Ei", [128, 128, 2, 128], BF16)
    yhy_t = io["yhy_t"]
    YB = io["yhy_base"]
    kscr_d = io["kscr"]
    bscr_d = io["bscr"]
    kscr_t = kscr_d.tensor
    with ExitStack() as st:
        P.cur = st
        V = lambda fn, r, w: P.op("dve", fn, r=r, w=w)
        G = lambda fn, r, w: P.op("pool", fn, r=r, w=w)
        A_ = lambda fn, r, w: P.op("act", fn, r=r, w=w)
        gates = [P.sb([128, 64, 128], BF16, "gate%d" % i) for i in range(3)]
        W1s = P.sb([128, 256], BF16, "W1s")
        W1k = P.sb([128, 256], BF16, "W1k")
        W1i = P.sb([128, 2, 256], BF16, "W1i")
        P.dma("sp", W1s[:], W1s_d, r=[], w=["W1s"], key="W1s")
        P.dma("sp", W1k[:], W1k_d, r=[], w=["W1k"], key="W1k")
        P.dma("sp", W1i[:], W1i_d, r=[], w=["W1i"], key="W1i")
        biasb = P.sb([128, 128], F32, "biasb")
        with ExitStack() as st2:
            w1 = P.sb([33, 64], F32, "w1", st2)
            w2 = P.sb([64, 64], F32, "w2", st2)
            w3 = P.sb([64, 2, 128], F32, "w3", st2)
            mlpv = P.sb([64, 3], F32, "mlpv", st2)
            fsc = P.sb([64, 4], F32, "fsc", st2)
            hbias = P.sb([128, 1], F32, "hbias", st2)
            ndelta = P.sb([128, 1], F32, "ndelta", st2)
            for (t_, d_, n_) in ((w1, w1_d, "w1"), (w2, w2_d, "w2"), (w3, w3_d, "w3"), (mlpv, mlpv_d, "mlpv"), (hbias, hbias_d, "hbias"),
                                 (ndelta, ndelta_d, "ndelta")):
                P.dma("sp", t_[:], d_, r=[], w=[n_], key=n_)
            V(lambda e: e.tensor_scalar(out=fsc[:, 0:1], in0=mlpv[:, 2:3], scalar1=1.0 / TWO_PI, scalar2=None, op0=ALU.mult), ["mlpv"], ["fsc"])
            V(lambda e: e.tensor_tensor(out=fsc[:, 1:2], in0=fsc[:, 0:1], in1=mlpv[:, 0:1], op=ALU.mult), ["mlpv", "fsc"], ["fsc"])
            V(lambda e: e.tensor_tensor(out=fsc[:, 2:3], in0=fsc[:, 0:1], in1=mlpv[:, 1:2], op=ALU.mult), ["mlpv", "fsc"], ["fsc"])
            hk32 = [P.sb([128, L], F32, "hk32_%d" % i, st2) for i in range(2)]
            hk16 = P.sb([128, L], BF16, "hk16", st2)
            zt = [P.sb([33, 512], F32, "zt%d" % i, st2) for i in range(2)]
            tnb = [P.sb([128, 512], F32, "tnb%d" % i, st2) for i in range(2)]
            tq = P.sb([64, 512], F32, "tqm", st2)
            tr = P.sb([64, 512], F32, "trm", st2)
            h1 = P.sb([64, 512], F32, "h1", st2)
            h2 = P.sb([64, 512], F32, "h2", st2)
            dec = P.sb([128, 512], F32, "dec", st2)
            asum = P.sb([128, 32], F32, "asum", st2)
            nrm = P.sb([128, 4], F32, "nrm", st2)
            psm1 = P.ps([64, 512], F32, "psm1", st2)
            psm2 = P.ps([64, 512], F32, "psm2", st2)
            psm3 = P.ps([128, 512], F32, "psm3", st2)

            def sin_layer(ps, psn, bcol, hout, hn):
                V(lambda e: e.tensor_scalar(out=tq[:], in0=ps[:], scalar1=fsc[:, 0:1], scalar2=fsc[:, bcol:bcol + 1], op0=ALU.mult, op1=ALU.add),
                  [psn, "fsc"], ["tqm"])
                V(lambda e: e.tensor_scalar(out=tr[:], in0=tq[:], scalar1=MAGIC, scalar2=MAGIC, op0=ALU.add, op1=ALU.subtract), ["tqm"], ["trm"])
                V(lambda e: e.tensor_tensor(out=tr[:], in0=tq[:], in1=tr[:], op=ALU.subtract), ["tqm", "trm"], ["trm"])
                A_(lambda e: e.activation(out=hout[:], in_=tr[:], func=AF.Sin, scale=TWO_PI), ["trm"], [hn])

            for d in range(2):
                for tt in range(L // 512):
                    i = d * 16 + tt
                    z_ = zt[i % 2]
                    zn = "zt%d" % (i % 2)
                    tb_ = tnb[i % 2]
                    tbn = "tnb%d" % (i % 2)
                    cs_ = slice(tt * 512, (tt + 1) * 512)
                    P.dma("sp", z_[:], zT_d[d, :, cs_], r=[], w=[zn], key=zn)
                    P.dma("sp", tb_[:], tn_d[d, :, cs_].partition_broadcast(128), r=[], w=[tbn], key=tbn)
                    P.op("pe", lambda e, z_=z_: e.matmul(psm1[:], w1[:], z_[:], start=True, stop=True), r=["w1", zn], w=["psm1"])
                    sin_layer(psm1, "psm1", 1, h1, "h1")
                    P.op("pe", lambda e: e.matmul(psm2[:], w2[:], h1[:], start=True, stop=True), r=["w2", "h1"], w=["psm2"])
                    sin_layer(psm2, "psm2", 2, h2, "h2")
                    P.op("pe", lambda e, d=d: e.matmul(psm3[:], w3[:, d, :], h2[:], start=True, stop=True), r=["w3", "h2"], w=["psm3"])
                    A_(lambda e, tb_=tb_: e.activation(out=dec[:], in_=tb_[:], func=AF.Exp, scale=ndelta[:, 0:1]), [tbn, "ndelta"], ["dec"])
                    V(lambda e, d=d, cs_=cs_: e.tensor_tensor(out=hk32[d][:, cs_], in0=psm3[:], in1=dec[:], op=ALU.mult),
                      ["psm3", "dec"], [("hk32", d, tt)])
                    V(lambda e, d=d, cs_=cs_, i=i: e.tensor_reduce(out=asum[:, i:i + 1], in_=hk32[d][:, cs_], axis=mybir.AxisListType.X, op=ALU.add,
                                                                   apply_absolute_value=True),
                      [("hk32", d, tt)], ["asum"])
            V(lambda e: e.tensor_reduce(out=nrm[:, 0:1], in_=asum[:], axis=mybir.AxisListType.X, op=ALU.add), ["asum"], ["nrm"])
            V(lambda e: e.tensor_scalar(out=nrm[:, 0:1], in0=nrm[:, 0:1], scalar1=1e-6, scalar2=None, op0=ALU.add), ["nrm"], ["nrm"])
            V(lambda e: e.reciprocal(out=nrm[:, 1:2], in_=nrm[:, 0:1]), ["nrm"], ["nrm"])
            V(lambda e: e.scalar_tensor_tensor(out=nrm[:, 2:3], in0=hk32[1][:, 0:1], scalar=nrm[:, 1:2], in1=hbias[:], op0=ALU.mult, op1=ALU.add),
              ["nrm", "hbias", ("hk32", 1, 0)], ["nrm"])
            P.dma("sp", bscr_d.rearrange("o p -> p o"), nrm[:, 2:3], r=["nrm"], w=["bscr"], key="bscr")
            for d in range(2):
                for q in range(4):
                    cs_ = slice(q * 2048, (q + 1) * 2048)
                    eng = V
                    eng(lambda e, d=d, cs_=cs_: e.tensor_scalar(out=hk16[:, cs_], in0=hk32[d][:, cs_], scalar1=nrm[:, 1:2], scalar2=None, op0=ALU.mult),
                        ["nrm"] + [("hk32", d, tt) for tt in range(4 * q, 4 * q + 4)], [("hk16", q)])
                    P.dma("sp", kscr_d[d, :, cs_], hk16[:, cs_], r=[("hk16", q)], w=[("kscr", d)], key=("hk16", q))
            P.dma("sp", biasb[:], bscr_d.partition_broadcast(128), r=["bscr"], w=["biasb"], key="biasb")
            P.barrier()
        with ExitStack() as st3:
            shw = P.sb([128, 3 * 4 * 64], F32, "shw", st3)
            P.dma("sp", shw[:], shw_d.partition_broadcast(128), r=[], w=["shw"], key="shw")
            Z = P.sb([128, 64, 130], BF16, "Z", st3)
            c1 = P.sb([128, 64, 128], F32, "c1", st3)
            c2 = P.sb([128, 64, 128], F32, "c2", st3)
            for comp in range(3):
                V(lambda e: e.memset(Z[:, :, 0:1], 0.0), [], ["Z"])
                V(lambda e: e.memset(Z[:, :, 129:130], 0.0), [], ["Z"])
                for b in range(2):
                    off = HB + (comp * 64 * 2 + b) * L
                    p0 = 64 * b
                    src0 = AP(hyp_t, off, [[L, 1], [2 * L, 64], [1, 129]])
                    P.dma("sp", Z[p0:p0 + 1, :, 1:130], src0, r=[], w=["Z"], key=("Z", 0))
                    src1 = AP(hyp_t, off + 127, [[128, 62], [2 * L, 64], [1, 130]])
                    P.dma("sp", Z[p0 + 1:p0 + 63, :, 0:130], src1, r=[], w=["Z"], key=("Z", 1))
                    src2 = AP(hyp_t, off + 63 * 128 - 1, [[L, 1], [2 * L, 64], [1, 129]])
                    P.dma("sp", Z[p0 + 63:p0 + 64, :, 0:129], src2, r=[], w=["Z"], key=("Z", 2))
                wb = lambda tap: fview(shw[:, (comp * 4 + tap) * 64:(comp * 4 + tap) * 64 + 64], [[1, 64], [0, 128]])
                V(lambda e: e.tensor_tensor(out=c1[:], in0=Z[:, :, 0:128], in1=wb(0), op=ALU.mult), ["Z", "shw"], ["c1"])
                V(lambda e: e.tensor_tensor(out=c2[:], in0=Z[:, :, 1:129], in1=wb(1), op=ALU.mult), ["Z", "shw"], ["c2"])
                V(lambda e: e.tensor_tensor(out=c1[:], in0=c1[:], in1=c2[:], op=ALU.add), ["c1", "c2"], ["c1"])
                V(lambda e: e.tensor_tensor(out=c2[:], in0=Z[:, :, 2:130], in1=wb(2), op=ALU.mult), ["Z", "shw", "c1"], ["c2"])
                V(lambda e: e.tensor_tensor(out=c1[:], in0=c1[:], in1=c2[:], op=ALU.add), ["c1", "c2"], ["c1"])
                V(lambda e, comp=comp: e.tensor_tensor(out=gates[comp][:], in0=c1[:], in1=wb(3), op=ALU.add), ["c1", "shw"], [("gate", comp)])
            P.barrier()
        AB = P.sb([128, 128 * 3 * 64], BF16, "AB")
        A_sb = fview(AB[:], [[192, 128], [64, 3], [1, 64]])
        B_sb = fview(AB[:], [[128, 128], [64, 2], [1, 64]])
        YK = P.sb([128, 2 * 64 * 128], BF16, "YK")
        Y_sb = fview(YK[:], [[64 * 128, 2], [128, 64], [1, 128]])
        kt = fview(YK[:], [[128, 64], [1, 128]])
        KH = P.sb([128, 128, 2, 64], BF16, "KH")
        Ech = [P.sb([128, 8, 2, 128], BF16, "Ech%d" % i) for i in range(2)]
        mt = [P.sb([128, 4, 64], F32, "mtm%d" % i) for i in range(4)]
        s1 = P.sb([128, 8, 64], F32, "s1")
        s2 = P.sb([128, 8, 64], F32, "s2")
        psA = [P.ps([128, 256], F32, "psA%d" % i) for i in range(2)]
        ps3 = [P.ps([128, 4, 128], F32, "ps3_%d" % i) for i in range(2)]
        psI = [P.ps([128, 8, 64], F32, "psI%d" % i) for i in range(2)]
        nE = [0]

        def load_E(tab_d, g):
            j = nE[0] % 2
            nE[0] += 1
            P.dma("sp", Ech[j][:], tab_d[:, 8 * g:8 * g + 8], r=[], w=["Ech%d" % j], key="Ech%d" % j)
            return Ech[j], "Ech%d" % j

        def evacA(pa, pn, c):
            A_(lambda e: e.activation(out=AB[:, 384 * c + 128:384 * c + 384], in_=pa[:, 0:256], func=AF.Copy), [pn], ["AB"])
            V(lambda e: e.tensor_scalar(out=AB[:, 384 * c:384 * c + 128], in0=pa[:, 128:256], scalar1=-1.0, scalar2=None, op0=ALU.mult),
              [pn], ["AB"])

        def fwd_step1(lhs_fn, lhs_key, wtab, wname):
            for c in range(64):
                pa = psA[c % 2]
                pn = "psA%d" % (c % 2)
                P.op("pe", lambda e, pa=pa, c=c: e.matmul(pa[:], lhs_fn(c), wtab[:], start=True, stop=True), r=[lhs_key, wname], w=[pn])
                evacA(pa, pn, c)

        A4 = AB[:, :].rearrange("p (f s c) -> p f s c", s=3, c=64)
        B4 = AB[:, 0:128 * 2 * 64].rearrange("p (t r c) -> p t r c", r=2, c=64)
        Y4 = YK[:, :].rearrange("p (r c f) -> p r c f", r=2, c=64)
        K3 = YK[:, 0:64 * 128].rearrange("p (c t) -> p c t", c=64)

        def fwd_step3(evac):
            for g in range(16):
                Et, En = load_E(Ef_d, g)
                for q2 in range(2):
                    pg = ps3[(2 * g + q2) % 2]
                    pgn = "ps3_%d" % ((2 * g + q2) % 2)
                    for q in range(4):
                        j = q2 * 4 + q
                        fl = 8 * g + j
                        P.op("pe", lambda e, pg=pg, q=q, j=j, fl=fl, Et=Et: e.matmul(pg[:, q, :], Et[:, j, 0, :], fview(AB[:, fl + 128:fl + 129], [[128, 2], [384, 64]]), start=True, stop=False),
                             r=[En, "AB"], w=[pgn])
                        P.op("pe", lambda e, pg=pg, q=q, j=j, fl=fl, Et=Et: e.matmul(pg[:, q, :], Et[:, j, 1, :], fview(AB[:, fl:fl + 1], [[128, 2], [384, 64]]), start=False, stop=True),
                             r=[En, "AB"], w=[pgn])
                    evac(pg, pgn, 8 * g + 4 * q2)

        def evac_filter(pg, pgn, fl0):
            A_(lambda e: e.activation(out=KH[:, fl0:fl0 + 4, :, :], in_=pg[:, :, :].rearrange("p q (r c) -> p q r c", r=2), func=AF.Copy,
                                      scale=1.0 / NFFT),
               [pgn], ["KH"])

        def evac_mac(pg, pgn, fl0):
            Xr = pg[:, :, 0:64]
            Xi = pg[:, :, 64:128]
            Kr = KH[:, fl0:fl0 + 4, 0, :]
            Ki = KH[:, fl0:fl0 + 4, 1, :]
            V(lambda e: e.tensor_tensor(out=mt[0][:], in0=Xr, in1=Kr, op=ALU.mult), [pgn, "KH"], ["mtm0"])
            V(lambda e: e.tensor_tensor(out=mt[1][:], in0=Xi, in1=Ki, op=ALU.mult), [pgn, "KH"], ["mtm1"])
            V(lambda e: e.tensor_tensor(out=Y4[:, 0, :, fl0:fl0 + 4].rearrange("p c f -> p f c"), in0=mt[0][:], in1=mt[1][:], op=ALU.subtract),
              ["mtm0", "mtm1"], ["YK"])
            V(lambda e: e.tensor_tensor(out=mt[2][:], in0=Xr, in1=Ki, op=ALU.mult), [pgn, "KH"], ["mtm2"])
            V(lambda e: e.tensor_tensor(out=mt[3][:], in0=Xi, in1=Kr, op=ALU.mult), [pgn, "KH"], ["mtm3"])
            V(lambda e: e.tensor_tensor(out=Y4[:, 1, :, fl0:fl0 + 4].rearrange("p c f -> p f c"), in0=mt[2][:], in1=mt[3][:], op=ALU.add),
              ["mtm2", "mtm3"], ["YK"])

        def inverse(src, srcname, gate, gatename, dst, dstname, o):
            for c in range(64):
                pa = psA[c % 2]
                pn = "psA%d" % (c % 2)
                P.op("pe", lambda e, pa=pa, c=c: e.matmul(pa[:], Y4[:, 0, c, :], W1i[:, 0, :], start=True, stop=False), r=["YK", "W1i"], w=[pn])
                P.op("pe", lambda e, pa=pa, c=c: e.matmul(pa[:], Y4[:, 1, c, :], W1i[:, 1, :], start=False, stop=True), r=["YK", "W1i"], w=[pn])
                A_(lambda e, pa=pa, c=c: e.activation(out=AB[:, 256 * c:256 * c + 256], in_=pa[:, 0:256], func=AF.Copy), [pn], ["AB"])
            for g in range(16):
                Et, En = load_E(Ei_d, g)
                pg = psI[g % 2]
                pgn = "psI%d" % (g % 2)
                for j in range(8):
                    tb = 8 * g + j
                    P.op("pe", lambda e, pg=pg, j=j, tb=tb, Et=Et: e.matmul(pg[:, j, :], Et[:, j, 0, :], fview(AB[:, tb:tb + 1], [[256, 64]]), start=True, stop=False),
                         r=[En, "AB"], w=[pgn])
                    P.op("pe", lambda e, pg=pg, j=j, tb=tb, Et=Et: e.matmul(pg[:, j, :], Et[:, j, 1, :], fview(AB[:, 128 + tb:128 + tb + 1], [[256, 64]]), start=False, stop=True),
                         r=[En, "AB"], w=[pgn])
                tsl = slice(8 * g, 8 * g + 8)
                sv = src[:, :, tsl].rearrange("p c t -> p t c")
                gv = gate[:, :, tsl].rearrange("p c t -> p t c")
                dv = dst[:, :, tsl].rearrange("p c t -> p t c")
                bb = fview(biasb[:, 64 * o:64 * o + 64], [[0, 8], [1, 64]])
                V(lambda e, sv=sv, bb=bb: e.tensor_tensor(out=s1[:], in0=sv, in1=bb, op=ALU.mult), [(srcname, g), "biasb"], ["s1"])
                V(lambda e, pg=pg: e.tensor_tensor(out=s2[:], in0=pg[:], in1=s1[:], op=ALU.add), [pgn, "s1"], ["s2"])
                V(lambda e, gv=gv, dv=dv: e.tensor_tensor(out=dv, in0=s2[:], in1=gv, op=ALU.mult), ["s2", (gatename, g)], [(dstname, g)])

        gk = lambda name: [(name, g) for g in range(16)]
        for comp, nm in ((0, "gout"), (1, "gmid"), (2, "vv")):
            for g in range(16):
                P.lastw[(nm, g)] = P.lastw.get(("gate", comp))
                P.readers[(nm, g)] = []
        for o in range(2):
            for dd in range(2):
                srck = AP(kscr_t, (dd * 128 + o * 64) * L, [[128, 64], [L, 64], [1, 128]])
                P.dma("sp", K3[64 * dd:64 * dd + 64, :, :], srck, r=[("kscr", dd)], w=["YK"], key=("kt", dd))
            fwd_step1(lambda c: K3[:, c, :], "YK", W1k, "W1k")
            fwd_step3(evac_filter)
            if o == 0:
                src, srcname, gate, gatename, dst, dstname = gates[2], "vv", gates[1], "gmid", gates[2], "vv"
            else:
                src, srcname, gate, gatename, dst, dstname = gates[2], "vv", gates[0], "gout", gates[1], "gmid"
            for c in range(64):
                pa = psA[c % 2]
                pn = "psA%d" % (c % 2)
                P.op("pe", lambda e, pa=pa, c=c: e.matmul(pa[:], src[:, c, :], W1s[:], start=True, stop=True), r=gk(srcname) + ["W1s"], w=[pn])
                evacA(pa, pn, c)
            fwd_step3(evac_mac)
            inverse(src, srcname, gate, gatename, dst, dstname, o)
        outst = gates[1]
        for b in range(2):
            dsto = AP(yhy_t, YB + b * L, [[128, 64], [2 * L, 64], [1, 128]])
            P.dma("act", dsto, outst[64 * b:64 * b + 64, :, :], r=gk("gmid"), w=[], key=("yout", b), final=True)
        P.barrier()
        P.cur = P.st


HY_DELTAS = np.abs(np.linspace(math.log(1e-2) / 1.5, math.log(1e-2) / 0.3, 512, dtype=np.float32)).astype(np.float32)


def hyena_pos_tables():
    f32 = np.float32
    t_norm = np.linspace(0.0, 1.0, L, dtype=f32)
    bands = np.linspace(1e-4, 15.0, 16, dtype=f32)[None, :]
    ang = f32(2.0 * math.pi / L) * np.arange(L, dtype=f32)[:, None] * bands
    z = np.concatenate([t_norm[:, None], np.cos(ang), -np.sin(ang)], axis=-1).astype(f32)
    idx = (L - np.arange(L)) % L
    zT = np.ascontiguousarray(np.stack([z.T, z[idx].T], 0))
    tn = np.ascontiguousarray(np.stack([t_norm[None, :], t_norm[idx][None, :]], 0))
    return zT, tn


_HY_CONST = {}


def hyena_inputs(p_hy, short_w, short_b, w1, b1, w2, b2, w3, freq, bias):
    if not _HY_CONST:
        _HY_CONST.update(hyena_tables())
        zT, tn = hyena_pos_tables()
        _HY_CONST["zT"] = zT
        _HY_CONST["tn"] = tn
    ims = []
    for c in range(NCORES):
        ch = slice(64 * c, 64 * c + 64)
        hyp = None if p_hy is None else np.ascontiguousarray(np.stack([p_hy[:, :, comp * 512 + 64 * c:comp * 512 + 64 * c + 64] for comp in range(3)], 0).transpose(0, 3, 1, 2))
        shw = np.zeros((3, 4, 64), np.float32)
        for comp in range(3):
            shw[comp, 0:3] = short_w[:, comp * 512 + 64 * c:comp * 512 + 64 * c + 64]
            shw[comp, 3] = short_b[comp * 512 + 64 * c:comp * 512 + 64 * c + 64]
        hw3 = np.zeros((64, 2, 128), np.float32)
        for o in range(2):
            for d in range(2):
                hw3[:, d, o * 64:(o + 1) * 64] = w3[:, o * 1024 + d * 512 + 64 * c:o * 1024 + d * 512 + 64 * c + 64]
        d = {"hyp": hyp, "shw": shw.reshape(1, -1), "hw1": np.ascontiguousarray(w1), "hw2": np.ascontiguousarray(w2), "hw3": hw3,
             "mlpv": np.ascontiguousarray(np.stack([b1, b2, freq], 1)),
             "hbias": np.ascontiguousarray(bias[:, ch].reshape(128, 1)),
             "ndelta": np.ascontiguousarray(-np.tile(HY_DELTAS[ch], 2).reshape(128, 1))}
        d.update(_HY_CONST)
        ims.append(d)
    return ims


def hyena_outputs(res):
    return np.concatenate([np.asarray(res[c]["yhy"]).transpose(1, 2, 0) for c in range(NCORES)], axis=-1)


I32 = mybir.dt.int32


def _msel_table():
    t = np.arange(L)
    m = np.zeros((4, 4, L), np.float32)
    for q, win in enumerate((2, 4, 8, 16)):
        h = win // 2
        cnt = (np.minimum(t + h, L) - np.maximum(t - h, 0)).astype(np.float32)
        m[q, q] = (1.0 / cnt).astype(np.float32)
    return m


def emit_regather(P, p_all, idxp_d, pm):
    with ExitStack() as st:
        P.cur = st
        idx = P.sb([128, 16], I32, "idxp")
        P.dma("sp", idx[:], idxp_d, r=[], w=["idxp"], key="idxp")
        gb = [P.sb([128, 2048], BF16, "gb%d" % i) for i in range(4)]
        for j in range(16):
            rg, blk = j // 4, j % 4
            g_ = gb[j % 4]
            gn = "gb%d" % (j % 4)
            P.op("pool", lambda g, g_=g_, j=j: g.indirect_dma_start(out=g_[:], out_offset=None, in_=p_all,
                                                                   in_offset=bass.IndirectOffsetOnAxis(ap=idx[:, j:j + 1], axis=0)),
                 r=["idxp", "p_all"], w=[gn], dma=gn)
            P.dma("act", pm[128 * rg:128 * rg + 128, 2048 * blk:2048 * blk + 2048], g_[:], r=[gn], w=["pm"], key=gn + "o")
        P.barrier()
        P.cur = P.st


def build_fused():
    nc = bass.Bass("TRN2", target_bir_lowering=False)
    ext = lambda name, shape, dt=F32: nc.dram_tensor(name, list(shape), dt, kind="ExternalInput").ap()
    itn = lambda name, shape, dt: nc.dram_tensor(name, list(shape), dt, kind="Internal").ap()
    xin = ext("xT", [D, NT])
    memT = ext("memT", [D, NMEM])
    xout = nc.dram_tensor("xoT", [D, NT], F32, kind="ExternalOutput").ap()
    idxp = [ext("idxp_e", [128, 16], I32), ext("idxp_o", [128, 16], I32)]
    idxy = [ext("idxy_e", [128, 32], I32), ext("idxy_o", [128, 32], I32)]
    xbuf = itn("xbuf", [D, NT], F32)
    p_loc = itn("p_loc", [2048, NT], BF16)
    p_all = itn("p_all", [NCORES * 2048, NT], BF16)
    pm = itn("pm", [512, L], BF16)
    y_loc = itn("y_loc", [4096, 512], BF16)
    y_all = itn("y_all", [NCORES * 4096, 512], BF16)
    ys0 = itn("ys0", [2, 32, 2, L], BF16)
    kscr = itn("kscr", [2, 128, L], BF16)
    bscr = itn("bscr", [1, 128], F32)
    yl = y_loc.rearrange("(r a) t -> r (a t)", a=16)
    W = []
    for i in range(DEPTH):
        odd = i % 2 == 1
        d = {"w_in": ext("w_in%d" % i, [128, 8, 2048]), "g_in": ext("g_in%d" % i, [128, 8]),
             "wout": ext("wout%d" % i, [128, 8, 1024]), "wq": ext("wq%d" % i, [128, 8, 1024]), "wk": ext("wk%d" % i, [128, 8, 1024]),
             "wv": ext("wv%d" % i, [128, 8, 1024]), "wo": ext("wo%d" % i, [128, 8, 1024]), "gains": ext("gains%d" % i, [128, 4, 8]),
             "w1": ext("w1_%d" % i, [128, 8, 4096]), "w2": ext("w2_%d" % i, [128, 32, 1024]),
             "g0": ext("g0_%d" % i, [128, 8]), "g1": ext("g1_%d" % i, [128, 8])}
        if odd:
            d.update({"wglu": ext("wglu%d" % i, [128, 4, 512]),
                      "par": ext("par%d" % i, [2, 2, 128, 4]), "bmat": ext("bmat%d" % i, [2, 128, 2, 16]),
                      "cmat": ext("cmat%d" % i, [2, 2, 128, 2, 16]), "dsk": ext("dsk%d" % i, [2, 32, 1]),
                      "shw": ext("shw%d" % i, [1, 3 * 4 * 64]), "hw1": ext("hw1_%d" % i, [33, 64]), "hw2": ext("hw2_%d" % i, [64, 64]),
                      "hw3": ext("hw3_%d" % i, [64, 2, 128]), "mlpv": ext("mlpv%d" % i, [64, 3]), "hbias": ext("hbias%d" % i, [128, 1])})
        else:
            d.update({"poolw": ext("poolw%d" % i, [128, 1, 128]), "pscale": ext("pscale%d" % i, [128, 1]), "convw": ext("convw%d" % i, [128, 3])})
        W.append(d)
    msel = ext("msel", [4, L])
    ident = ext("ident", [128, 128])
    hyc = {"ndelta": ext("ndelta", [128, 1]), "zT": ext("zT", [2, 33, L]), "tn": ext("tn", [2, 1, L]),
           "W1s": ext("W1s", [128, 256], BF16), "W1k": ext("W1k", [128, 256], BF16), "W1i": ext("W1i", [128, 2, 256], BF16),
           "Ef": ext("Ef", [128, 128, 2, 128], BF16), "Ei": ext("Ei", [128, 128, 2, 128], BF16)}
    with ExitStack() as st:
        P = Prog(nc, st)
        for i in range(DEPTH):
            odd = i % 2 == 1
            w = W[i]
            emit_PA(P, {"xT": xin if i == 0 else xbuf, "w": w["w_in"], "g": w["g_in"], "pT": p_loc})
            P.allgather(p_loc, p_all, r=[], w=["p_all"])
            emit_regather(P, p_all, idxp[1 if odd else 0], pm)
            if not odd:
                emit_PBE(P, {"pu": pm[0:128], "cb": pm[128:256], "cc": pm[256:384], "ch": pm[384:512], "poolw": w["poolw"],
                             "pscale": w["pscale"], "msel": msel, "convw": w["convw"], "ypool": yl[0:128], "yconv": yl[128:256]})
            else:
                emit_PBS5(P, {"u": pm[0:128].rearrange("(g i b) t -> g i b t", g=2, b=2), "par": w["par"], "bmat": w["bmat"],
                              "cmat": w["cmat"], "dsk": w["dsk"], "ident": ident,
                              "ys": yl[0:128].rearrange("(g i b) t -> g i b t", g=2, b=2), "ys0": ys0})
                hio = {"hyp_t": pm.tensor, "hyp_base": 128 * L, "yhy_t": y_loc.tensor, "yhy_base": 128 * L, "kscr": kscr, "bscr": bscr,
                       "shw": w["shw"], "hw1": w["hw1"], "hw2": w["hw2"], "hw3": w["hw3"], "mlpv": w["mlpv"], "hbias": w["hbias"]}
                hio.update(hyc)
                emit_PBHY(P, hio)
            P.allgather(y_loc, y_all, r=[], w=["y_all"])
            pio = {"xT": xin if i == 0 else xbuf, "xoT": xbuf, "y_all": y_all, "idx_y": idxy[1 if odd else 0], "memT": memT,
                   "wout": w["wout"], "wq": w["wq"], "wk": w["wk"], "wv": w["wv"], "wo": w["wo"], "gains": w["gains"]}
            if odd:
                pio["wglu"] = w["wglu"]
            emit_PC1(P, pio, odd)
            emit_PC2(P, {"xT": xbuf, "w1": w["w1"], "w2": w["w2"], "g0": w["g0"], "g1": w["g1"],
                         "xoT": xout if i == DEPTH - 1 else xbuf})
        P.finish()
    return nc


def _idx_tables(c):
    kb, kq = c // 4, c % 4
    p = np.arange(128)
    idxp_e = np.zeros((128, 16), np.int32)
    idxp_o = np.zeros((128, 16), np.int32)
    for rg in range(4):
        for blk in range(4):
            j = rg * 4 + blk
            if rg == 0:
                idxp_e[:, j] = (4 * (c % 2) + blk) * 2048 + 128 * (c // 2) + p
            else:
                idxp_e[:, j] = (4 * (p // 64) + blk) * 2048 + 512 * rg + 64 * c + (p % 64)
            chl = 64 * rg + p // 2
            b = p % 2
            chan = np.where(chl < 64, 64 * c + chl, 512 + ((chl - 64) // 64) * 512 + 64 * c + ((chl - 64) % 64))
            idxp_o[:, j] = (4 * b + blk) * 2048 + chan
    idxy_e = np.zeros((128, 32), np.int32)
    idxy_o = np.zeros((128, 32), np.int32)
    for k in range(8):
        m = 128 * k + p
        for t in range(4):
            blk16 = kq * 4 + t
            if k < 4:
                src_o, row_o = m // 64, 2 * (m % 64) + kb
                src_e, row_e = 2 * (m // 128) + kb, m % 128
            else:
                mm = m - 512
                src_o, row_o = mm // 64, 128 + 2 * (mm % 64) + kb
                src_e, row_e = mm // 64, 128 + kb * 64 + (mm % 64)
            idxy_o[:, 4 * k + t] = src_o * 4096 + row_o * 16 + blk16
            idxy_e[:, 4 * k + t] = src_e * 4096 + row_e * 16 + blk16
    return idxp_e, idxp_o, idxy_e, idxy_o


def kernel(x, mem, norm_mix, norm_xattn, norm_mem, norm_mlp, xa_wq, xa_wk, xa_wv, xa_wo, mlp_w1, mlp_w2, ev_w_in, ev_pool_w,
           ev_pool_scale, ev_conv_w, ev_w_out, od_w_in, od_s5_lambda_re, od_s5_lambda_im, od_s5_log_dt, od_s5_b_re, od_s5_b_im,
           od_s5_c_re, od_s5_c_im, od_s5_d, od_s5_w_glu, od_hy_short_w, od_hy_short_b, od_hy_w1, od_hy_b1, od_hy_w2, od_hy_b2,
           od_hy_w3, od_hy_freq, od_hy_bias, od_w_out):
    f = lambda a: np.ascontiguousarray(np.asarray(a, dtype=np.float32))
    x = f(x).reshape(NTOK, D)
    mem = f(mem)
    msel = _msel_table()
    nc = prog("fused", build_fused)
    tabs = hyena_tables()
    zT, tn = hyena_pos_tables()
    ims = [dict() for _ in range(NCORES)]
    shared = {"ident": np.eye(128, dtype=np.float32), "zT": zT, "tn": tn}
    shared.update(tabs)
    for i in range(DEPTH):
        j = i // 2
        odd = i % 2 == 1
        shared["w_in%d" % i] = wlay(f(od_w_in[j] if odd else ev_w_in[j]))
        shared["g_in%d" % i] = glay(f(norm_mix[i, 0]))
        shared["wout%d" % i] = wlay(f(od_w_out[j] if odd else ev_w_out[j]))
        for nm, arr in (("wq", xa_wq), ("wk", xa_wk), ("wv", xa_wv), ("wo", xa_wo)):
            shared["%s%d" % (nm, i)] = wlay(f(arr[i]))
        shared["gains%d" % i] = np.ascontiguousarray(np.stack([glay(f(norm_mix[i, 1])), glay(f(norm_xattn[i, 0])), glay(f(norm_xattn[i, 1])),
                                                               glay(f(norm_mem[i]))], axis=1))
        shared["w1_%d" % i] = wlay(f(mlp_w1[i]))
        shared["w2_%d" % i] = wlay(f(mlp_w2[i]))
        shared["g0_%d" % i] = glay(f(norm_mlp[i, 0]))
        shared["g1_%d" % i] = glay(f(norm_mlp[i, 1]))
        if odd:
            shared["wglu%d" % i] = wlay(f(od_s5_w_glu[j]))
            s5i = s5_inputs(None, f(od_s5_lambda_re[j]),
                            f(od_s5_lambda_im[j]), f(od_s5_log_dt[j]), f(od_s5_b_re[j]), f(od_s5_b_im[j]), f(od_s5_c_re[j]), f(od_s5_c_im[j]),
                            f(od_s5_d[j]))
            hyi = hyena_inputs(None, f(od_hy_short_w[j]), f(od_hy_short_b[j]), f(od_hy_w1[j]), f(od_hy_b1[j]), f(od_hy_w2[j]),
                               f(od_hy_b2[j]), f(od_hy_w3[j]), f(od_hy_freq[j]), f(od_hy_bias[j]))
            for c in range(NCORES):
                for nm in ("par", "bmat", "cmat", "dsk"):
                    ims[c]["%s%d" % (nm, i)] = s5i[c][nm]
                for nm in ("shw", "mlpv", "hbias"):
                    ims[c]["%s%d" % (nm, i)] = hyi[c][nm]
                ims[c]["hw3_%d" % i] = hyi[c]["hw3"]
                ims[c]["hw1_%d" % i] = hyi[c]["hw1"]
                ims[c]["hw2_%d" % i] = hyi[c]["hw2"]
                ims[c]["ndelta"] = hyi[c]["ndelta"]
        else:
            for c in range(NCORES):
                q = c // 2
                ims[c]["poolw%d" % i] = np.ascontiguousarray(f(ev_pool_w[j][q]).reshape(128, 1, 128))
                ims[c]["pscale%d" % i] = np.ascontiguousarray(f(ev_pool_scale[j])[128 * q:128 * q + 128].reshape(128, 1))
                ims[c]["convw%d" % i] = np.ascontiguousarray(np.tile(f(ev_conv_w[j])[:, 64 * c:64 * c + 64].T, (2, 1)))
    for c in range(NCORES):
        ie, io_, ye, yo = _idx_tables(c)
        ims[c].update({"xT": _tok_T(x, c), "memT": np.ascontiguousarray(mem[c // (NCORES // B)].T), "msel": msel[c // 2],
                       "idxp_e": ie, "idxp_o": io_, "idxy_e": ye, "idxy_o": yo})
        ims[c].update(shared)
    res = run(nc, ims)
    out = np.concatenate([np.asarray(r["xoT"]).T for r in res], axis=0)
    return np.ascontiguousarray(out.reshape(B, L, D).astype(np.float32))


def _tok_T(a, c):
    return np.ascontiguousarray(a[c * NT:(c + 1) * NT].T)
```

```python
import math
from contextlib import ExitStack

import numpy as np
import ml_dtypes

import concourse.bass as bass
import concourse.mybir as mybir
from concourse.ap import AP
from concourse.bass_utils import run_bass_kernel_spmd

F32 = mybir.dt.float32
BF16 = mybir.dt.bfloat16
ALU = mybir.AluOpType
AF = mybir.ActivationFunctionType

NCORES = 8
D = 1024
B = 2
L = 8192
NTOK = B * L
NT = NTOK // NCORES
DEPTH = 4
NMEM = 256
EPS = 1e-6
MAGIC = 12582912.0
TWO_PI = 2.0 * math.pi


class Prog:
    def __init__(self, nc, st):
        self.nc = nc
        self.st = st
        self.eng = {"pe": nc.tensor, "dve": nc.vector, "act": nc.scalar, "pool": nc.gpsimd, "sp": nc.sync}
        self.cnt = {e: 0 for e in self.eng}
        self.csem = {e: st.enter_context(nc.semaphore("cs_" + e)) for e in self.eng}
        self.seen = {e: {} for e in self.eng}
        self.lastw = {}
        self.readers = {}
        self.dsem = {}
        self.final = []
        self.nsb = 0
        self.cur = st
        self.ncc = 0

    def sb(self, shape, dt, name=None, st=None):
        self.nsb += 1
        self.nsb += 1
        return (st or self.cur).enter_context(self.nc.sbuf_tensor("s%d_" % self.nsb + (name or "t"), list(shape), dt))

    def ps(self, shape, dt=F32, name=None, st=None):
        self.nsb += 1
        self.nsb += 1
        return (st or self.cur).enter_context(self.nc.psum_tensor("p%d_" % self.nsb + (name or "t"), list(shape), dt))

    def op(self, e, fn, r=(), w=(), dma=None, final=False):
        deps = {}
        mysem = None
        if dma is not None:
            if dma not in self.dsem:
                self.dsem[dma] = [self.st.enter_context(self.nc.semaphore("ds%d" % len(self.dsem))), 0]
            mysem = self.dsem[dma][0]

        def need(tok):
            if tok is None:
                return
            sem, val, src = tok
            if src == "pe" and e == "pe":
                return
            if sem is mysem:
                return
            k = id(sem)
            if k not in deps or deps[k][1] < val:
                deps[k] = (sem, val)

        for k in r:
            need(self.lastw.get(k))
        for k in w:
            need(self.lastw.get(k))
            for t in self.readers.get(k, ()):
                need(t)
        eng = self.eng[e]
        for k, (sem, val) in deps.items():
            if self.seen[e].get(k, 0) < val:
                eng.wait_ge(sem, val)
                self.seen[e][k] = val
        if dma is None:
            self.cnt[e] += 1
            tok = (self.csem[e], self.cnt[e], e)
            fn(eng).then_inc(self.csem[e], 1)
        else:
            d = self.dsem[dma]
            d[1] += 16
            tok = (d[0], d[1], "dma")
            fn(eng).then_inc(d[0], 16)
        for k in r:
            self.readers.setdefault(k, []).append(tok)
        for k in w:
            self.lastw[k] = tok
            self.readers[k] = []
        if final:
            self.final.append(tok)
        return tok

    def dma(self, e, out, in_, r, w, key, final=False, **kw):
        return self.op(e, lambda g: g.dma_start(out=out, in_=in_, **kw), r=r, w=w, dma=key, final=final)

    def allgather(self, src, dst, r, w):
        sem = self.st.enter_context(self.nc.semaphore("cc%d" % self.ncc))
        self.ncc += 1
        if not hasattr(self, "ccd"):
            self.ccd = self.st.enter_context(self.nc.sbuf_tensor("s_ccdummy", [128, 4], F32))

        def fn(g):
            g.collective_compute("AllGather", ALU.bypass, replica_groups=[list(range(NCORES))], ins=[src], outs=[dst]).then_inc(sem)
            g.wait_ge(sem, 1)
            return g.memset(self.ccd[:], 0.0)

        return self.op("pool", fn, r=r, w=list(w) + ["ccdummy"])

    def finish(self):
        eng = self.eng["sp"]
        for sem, val, _ in self.final:
            eng.wait_ge(sem, val)
        for e in ("pe", "dve", "act", "pool"):
            if self.cnt[e]:
                eng.wait_ge(self.csem[e], self.cnt[e])


def keys(name, n):
    return [(name, i) for i in range(n)]


def load_weight(P, dram, sbt, name, nk):
    for k in range(nk):
        P.dma("pool", sbt[:, k, :], dram[:, k, :], r=[], w=[name], key=name)


class DenseCtx:
    def __init__(self, P, T):
        self.P = P
        self.T = T
        self.ones = P.sb([128, 128], BF16, "ones")
        P.op("dve", lambda e: e.memset(self.ones[:], 1.0), w=["ones"])
        self.eps = P.sb([128, 1], F32, "epsc")
        P.op("dve", lambda e: e.memset(self.eps[:], EPS), w=["eps"])
        self.ps_stat = [P.ps([128, T], F32, "ps_stat%d" % i) for i in range(1)]
        self.nstat = 0
        self.lnt = P.sb([128, T], F32, "lnt")

    def rstd_from_sq(self, sq_fn, sq_keys, nchunks, rstd, rstd_key):
        P = self.P
        ps = self.ps_stat[0]
        pk = "ps_stat0"
        for k in range(nchunks):
            P.op("pe", lambda e, k=k: e.matmul(ps[:], self.ones[:], sq_fn(k), start=(k == 0), stop=(k == nchunks - 1)),
                 r=["ones", sq_keys[k]], w=[pk])
        P.op("act", lambda e: e.activation(out=self.lnt[:], in_=ps[:], func=AF.Ln, bias=self.eps[:, 0:1], scale=1.0 / D),
             r=[pk, "eps"], w=["lnt"])
        P.op("act", lambda e: e.activation(out=rstd, in_=self.lnt[:], func=AF.Exp, scale=-0.5),
             r=["lnt"], w=[rstd_key])


def emit_prenorm(C, xt, xkeys, gT, gkey, sq, sqname, h, hname, rstd, rstdkey):
    P = C.P
    for k in range(8):
        P.op("act", lambda e, k=k: e.activation(out=sq[:, k, :], in_=xt[:, k, :], func=AF.Square),
             r=[xkeys[k]], w=[(sqname, k)])
    C.rstd_from_sq(lambda k: sq[:, k, :], keys(sqname, 8), 8, rstd[:], rstdkey)
    for k in range(8):
        P.op("dve", lambda e, k=k: e.scalar_tensor_tensor(out=h[:, k, :], in0=xt[:, k, :], scalar=gT[:, k:k + 1], in1=rstd[:],
                                                          op0=ALU.mult, op1=ALU.mult),
             r=[xkeys[k], gkey, rstdkey], w=[(hname, k)])


def emit_postnorm_residual(C, mo, moname, sq, sqname, gT, gkey, xt, xkeys, rstd, rstdkey, tmp, tmpname):
    P = C.P
    C.rstd_from_sq(lambda k: sq[:, k, :], keys(sqname, 8), 8, rstd[:], rstdkey)
    for k in range(8):
        P.op("dve", lambda e, k=k: e.scalar_tensor_tensor(out=tmp[:, k % 2, :], in0=mo[:, k, :], scalar=gT[:, k:k + 1], in1=rstd[:],
                                                          op0=ALU.mult, op1=ALU.mult),
             r=[(moname, k), gkey, rstdkey], w=[(tmpname, k % 2)])
        P.op("dve", lambda e, k=k: e.tensor_tensor(out=xt[:, k, :], in0=xt[:, k, :], in1=tmp[:, k % 2, :], op=ALU.add),
             r=[xkeys[k], (tmpname, k % 2)], w=[xkeys[k]])


def emit_proj(P, psb, psname, w_sb, wname, nk, m, rhs_fn, rhs_keys, T):
    for k in range(nk):
        P.op("pe", lambda e, k=k: e.matmul(psb, w_sb[:, k, 128 * m:128 * m + 128], rhs_fn(k), start=(k == 0), stop=(k == nk - 1)),
             r=[wname, rhs_keys[k]], w=[psname])


def emit_PA(P, io, T=512):
    nc = P.nc
    xT = io["xT"]
    w = io["w"]
    g = io["g"]
    pT = io["pT"]
    with ExitStack() as st:
        P.cur = st
        C = DenseCtx(P, T)
        wsb = P.sb([128, 8, 2048], BF16, "w_in")
        load_weight(P, w, wsb, "w_in", 8)
        gT = P.sb([128, 8], F32, "gT")
        P.dma("sp", gT[:], g, r=[], w=["gT"], key="gT")
        xts = [P.sb([128, 8, T], F32, "xt%d" % i) for i in range(2)]
        sqs = [P.sb([128, 8, T], BF16, "sq%d" % i) for i in range(2)]
        hs = [P.sb([128, 8, T], BF16, "h%d" % i) for i in range(2)]
        rstds = [P.sb([128, T], F32, "rstd%d" % i) for i in range(2)]
        stage = [P.sb([128, 4, T], BF16, "stage%d" % i) for i in range(2)]
        pso = [P.ps([128, T], F32, "pso%d" % i) for i in range(4)]
        xv = xT.rearrange("(k p) n -> p k n", p=128)
        pv = pT.rearrange("(m p) n -> p m n", p=128)
        nev = 0

        def pre(t):
            j = t % 2
            xk = keys("xt%d" % j, 8)
            P.dma("sp", xts[j][:], xv[:, :, t * T:(t + 1) * T], r=[], w=xk, key="xt%d" % j)
            emit_prenorm(C, xts[j], xk, gT, "gT", sqs[j], "sq%d" % j, hs[j], "h%d" % j, rstds[j], "rstd%d" % j)

        pre(0)
        for t in range(NT // T):
            j = t % 2
            hk = keys("h%d" % j, 8)
            for mg in range(4):
                if mg == 1 and t + 1 < NT // T:
                    pre(t + 1)
                sj = (t * 4 + mg) % 2
                for mi in range(4):
                    m = mg * 4 + mi
                    pb = pso[m % 4]
                    emit_proj(P, pb[:], "pso%d" % (m % 4), wsb, "w_in", 8, m, lambda k: hs[j][:, k, :], hk, T)
                    eng = "act" if nev % 2 == 0 else "dve"
                    nev += 1
                    if eng == "act":
                        P.op("act", lambda e, pb=pb, mi=mi: e.activation(out=stage[sj][:, mi, :], in_=pb[:], func=AF.Copy),
                             r=["pso%d" % (m % 4)], w=[("stage%d" % sj, mi)])
                    else:
                        P.op("dve", lambda e, pb=pb, mi=mi: e.tensor_copy(out=stage[sj][:, mi, :], in_=pb[:]),
                             r=["pso%d" % (m % 4)], w=[("stage%d" % sj, mi)])
                P.dma("act", pv[:, mg * 4:mg * 4 + 4, t * T:(t + 1) * T], stage[sj][:], r=keys("stage%d" % sj, 4), w=[],
                      key="stage%d" % sj, final=True)
                if io.get("hal") is not None and t in (0, NT // T - 1):
                    hv = io["hal"].rearrange("(m p) n -> p m n", p=128)
                    if t == 0:
                        P.dma("act", hv[:, mg * 4:mg * 4 + 4, 0:8], stage[sj][:, :, 0:8], r=keys("stage%d" % sj, 4), w=[], key="stage%d" % sj,
                              final=True)
                    if t == NT // T - 1:
                        P.dma("act", hv[:, mg * 4:mg * 4 + 4, 8:16], stage[sj][:, :, T - 8:T], r=keys("stage%d" % sj, 4), w=[], key="stage%d" % sj,
                              final=True)
        P.barrier()
        P.cur = P.st


def emit_PC2(P, io, T=256):
    nc = P.nc
    xT = io["xT"]
    w1 = io["w1"]
    w2 = io["w2"]
    g0 = io["g0"]
    g1 = io["g1"]
    xoT = io["xoT"]
    with ExitStack() as st:
        P.cur = st
        C = DenseCtx(P, T)
        w1s = P.sb([128, 8, 4096], BF16, "w1")
        w2s = P.sb([128, 32, 1024], BF16, "w2")
        load_weight(P, w1, w1s, "w1", 8)
        load_weight(P, w2, w2s, "w2", 32)
        g0T = P.sb([128, 8], F32, "g0T")
        g1T = P.sb([128, 8], F32, "g1T")
        P.dma("sp", g0T[:], g0, r=[], w=["g0T"], key="g0T")
        P.dma("sp", g1T[:], g1, r=[], w=["g1T"], key="g1T")
        xts = [P.sb([128, 8, T], F32, "xt%d" % i) for i in range(2)]
        sqs = [P.sb([128, 8, T], BF16, "sq%d" % i) for i in range(2)]
        sqp = P.sb([128, 8, T], BF16, "sqp")
        hs = [P.sb([128, 8, T], BF16, "h%d" % i) for i in range(2)]
        rstds = [P.sb([128, T], F32, "rstd%d" % i) for i in range(2)]
        rstdp = P.sb([128, T], F32, "rstdp")
        a = P.sb([128, 32, T], BF16, "a")
        rl = [P.sb([128, T], F32, "rl%d" % i) for i in range(3)]
        mo = P.sb([128, 8, T], F32, "mo")
        tmp = P.sb([128, 2, T], F32, "tmp")
        psu = [P.ps([128, T], F32, "psu%d" % i) for i in range(3)]
        psd = [P.ps([128, T], F32, "psd%d" % i) for i in range(2)]
        xv = xT.rearrange("(k p) n -> p k n", p=128)
        ov = xoT.rearrange("(k p) n -> p k n", p=128)
        ntile = NT // T

        def pre(t):
            j = t % 2
            xk = keys("xt%d" % j, 8)
            P.dma("sp", xts[j][:], xv[:, :, t * T:(t + 1) * T], r=[], w=xk, key="xt%d" % j)
            emit_prenorm(C, xts[j], xk, g0T, "g0T", sqs[j], "sq%d" % j, hs[j], "h%d" % j, rstds[j], "rstd%d" % j)

        pre(0)
        for t in range(ntile):
            j = t % 2
            xt = xts[j]
            xk = keys("xt%d" % j, 8)
            h = hs[j]
            hk = keys("h%d" % j, 8)
            for f in range(32):
                pb = psu[f % 3]
                pn = "psu%d" % (f % 3)
                emit_proj(P, pb[:], pn, w1s, "w1", 8, f, lambda k: h[:, k, :], hk, T)
                rj = f % 3
                P.op("act", lambda e, pb=pb, rj=rj: e.activation(out=rl[rj][:], in_=pb[:], func=AF.Relu), r=[pn], w=["rl%d" % rj])
                P.op("dve", lambda e, rj=rj, f=f: e.tensor_tensor(out=a[:, f, :], in0=rl[rj][:], in1=rl[rj][:], op=ALU.mult),
                     r=["rl%d" % rj], w=[("a", f)])
            if t + 1 < ntile:
                pre(t + 1)
            ak = keys("a", 32)
            for m in range(8):
                pb = psd[m % 2]
                pn = "psd%d" % (m % 2)
                emit_proj(P, pb[:], pn, w2s, "w2", 32, m, lambda k: a[:, k, :], ak, T)
                P.op("act", lambda e, pb=pb, m=m: e.activation(out=mo[:, m, :], in_=pb[:], func=AF.Copy), r=[pn], w=[("mo", m)])
                P.op("act", lambda e, pb=pb, m=m: e.activation(out=sqp[:, m, :], in_=pb[:], func=AF.Square), r=[pn], w=[("sqp", m)])
            emit_postnorm_residual(C, mo, "mo", sqp, "sqp", g1T, "g1T", xt, xk, rstdp, "rstdp", tmp, "tmp")
            P.dma("act", ov[:, :, t * T:(t + 1) * T], xt[:], r=xk, w=[], key="xo%d" % j, final=True)
        P.barrier()
        P.cur = P.st


def wlay(w):
    kk = w.shape[0] // 128
    return np.ascontiguousarray(w.reshape(kk, 128, w.shape[1]).transpose(1, 0, 2))


def glay(g):
    return np.ascontiguousarray(g.reshape(-1, 128).T)


_PROGS = {}


def prog(name, builder):
    if name not in _PROGS:
        _PROGS[name] = builder()
    return _PROGS[name]


def run(nc, in_maps):
    res = run_bass_kernel_spmd(nc, in_maps, core_ids=list(range(NCORES)))
    return res.results


def barrier(P):
    engs = ("pe", "dve", "act", "pool", "sp")
    for e in engs:
        eng = P.eng[e]
        for e2 in engs:
            if e2 != e and P.cnt[e2] > P.seen[e].get(id(P.csem[e2]), 0):
                eng.wait_ge(P.csem[e2], P.cnt[e2])
                P.seen[e][id(P.csem[e2])] = P.cnt[e2]
        for sem, cntv in P.dsem.values():
            if cntv > P.seen[e].get(id(sem), 0):
                eng.wait_ge(sem, cntv)
                P.seen[e][id(sem)] = cntv


Prog.barrier = barrier


def emit_PC1(P, io, odd, T=512):
    nc = P.nc
    din = lambda name, shape, dt=F32: io[name]
    xT = din("xT", [D, NT])
    y_all = io.get("y_all")
    idxy_d = io.get("idx_y")
    if odd:
        wglu_d = din("wglu", [128, 4, 512])
    memT = din("memT", [D, NMEM])
    wout_d = din("wout", [128, 8, 1024])
    wq_d = din("wq", [128, 8, 1024])
    wk_d = din("wk", [128, 8, 1024])
    wv_d = din("wv", [128, 8, 1024])
    wo_d = din("wo", [128, 8, 1024])
    gains_d = din("gains", [128, 4, 8])
    xoT = io["xoT"]
    with ExitStack() as st:
        P.cur = st
        C = DenseCtx(P, T)
        idxy = P.sb([128, 32], mybir.dt.int32, "idxy")
        if idxy_d is not None:
            P.dma("sp", idxy[:], idxy_d, r=[], w=["idxy"], key="idxy")
        gains = P.sb([128, 4, 8], F32, "gains")
        P.dma("sp", gains[:], gains_d, r=[], w=["gains"], key="gains")
        wout = P.sb([128, 8, 1024], BF16, "wout")
        wq = P.sb([128, 8, 1024], BF16, "wq")
        wo = P.sb([128, 8, 1024], BF16, "wo")
        kT = P.sb([128, 8, NMEM], BF16, "kT")
        vS = P.sb([128, 2, 1024], BF16, "vS")
        psg = [P.ps([128, T], F32, "psg%d" % i) for i in range(3)]
        pss = [P.ps([128, T], F32, "pss%d" % i) for i in range(2)]
        psd = P.ps([128, T], F32, "psden")
        pso = P.ps([128, T], F32, "pspv")
        with ExitStack() as st2:
            wk = P.sb([128, 8, 1024], BF16, "wk", st2)
            wv = P.sb([128, 8, 1024], BF16, "wv", st2)
            mt = P.sb([128, 8, NMEM], F32, "mt", st2)
            msq = P.sb([128, 8, NMEM], BF16, "msq", st2)
            mn = P.sb([128, 8, NMEM], BF16, "mn", st2)
            mrstd = P.sb([128, T], F32, "mrstd", st2)
            load_weight(P, wk_d, wk, "wk", 8)
            load_weight(P, wv_d, wv, "wv", 8)
            P.dma("sp", mt[:], memT.rearrange("(k p) n -> p k n", p=128), r=[], w=keys("mt", 8), key="mt")
            for k in range(8):
                P.op("act", lambda e, k=k: e.activation(out=msq[:, k, :], in_=mt[:, k, :], func=AF.Square), r=[("mt", k)], w=[("msq", k)])
            ps = C.ps_stat[0]
            for k in range(8):
                P.op("pe", lambda e, k=k: e.matmul(ps[:, 0:NMEM], C.ones[:], msq[:, k, :], start=(k == 0), stop=(k == 7)),
                     r=["ones", ("msq", k)], w=["ps_stat0"])
            P.op("act", lambda e: e.activation(out=C.lnt[:, 0:NMEM], in_=ps[:, 0:NMEM], func=AF.Ln, bias=C.eps[:, 0:1], scale=1.0 / D),
                 r=["ps_stat0", "eps"], w=["lnt"])
            P.op("act", lambda e: e.activation(out=mrstd[:, 0:NMEM], in_=C.lnt[:, 0:NMEM], func=AF.Exp, scale=-0.5), r=["lnt"], w=["mrstd"])
            for k in range(8):
                P.op("dve", lambda e, k=k: e.scalar_tensor_tensor(out=mn[:, k, :], in0=mt[:, k, :], scalar=gains[:, 3, k:k + 1],
                                                                  in1=mrstd[:, 0:NMEM], op0=ALU.mult, op1=ALU.mult),
                     r=[("mt", k), "gains", "mrstd"], w=[("mn", k)])
            for m in range(8):
                pb = psg[m % 3]
                pn = "psg%d" % (m % 3)
                for k in range(8):
                    P.op("pe", lambda e, k=k, m=m, pb=pb: e.matmul(pb[:, 0:NMEM], wk[:, k, 128 * m:128 * m + 128], mn[:, k, :],
                                                                   start=(k == 0), stop=(k == 7)),
                         r=["wk", ("mn", k)], w=[pn])
                P.op("act", lambda e, m=m, pb=pb: e.activation(out=kT[:, m, :], in_=pb[:, 0:NMEM], func=AF.Copy), r=[pn], w=[("kT", m)])
            for j in range(2):
                for dh in range(2):
                    i = j * 2 + dh
                    pb = psg[i % 3]
                    pn = "psg%d" % (i % 3)
                    for k in range(8):
                        P.op("pe", lambda e, k=k, j=j, dh=dh, pb=pb: e.matmul(pb[:, 0:512], mn[:, k, 128 * j:128 * j + 128],
                                                                            wv[:, k, 512 * dh:512 * dh + 512], start=(k == 0), stop=(k == 7)),
                             r=["wv", ("mn", k)], w=[pn])
                    P.op("act", lambda e, j=j, dh=dh, pb=pb: e.activation(out=vS[:, j, 512 * dh:512 * dh + 512], in_=pb[:, 0:512], func=AF.Copy),
                         r=[pn], w=[("vS", j, dh)])
            P.barrier()
        vkeys = lambda j: [("vS", j, 0), ("vS", j, 1)]
        load_weight(P, wout_d, wout, "wout", 8)
        load_weight(P, wq_d, wq, "wq", 8)
        load_weight(P, wo_d, wo, "wo", 8)
        if odd:
            wglu = P.sb([128, 4, 512], BF16, "wglu")
            load_weight(P, wglu_d, wglu, "wglu", 4)
            gt = [P.sb([128, T], F32, "gt%d" % i) for i in range(2)]
            gg = P.sb([128, 4, T], BF16, "gg")
            glu = P.sb([128, 4, T], BF16, "glu")
        xt = P.sb([128, 8, T], F32, "xt")
        yt = P.sb([128, 8, T], BF16, "yt")
        sq = P.sb([128, 8, T], BF16, "sq")
        h = P.sb([128, 8, T], BF16, "h")
        mo = P.sb([128, 8, T], F32, "mo")
        tmp = P.sb([128, 2, T], F32, "tmp")
        rstd = P.sb([128, T], F32, "rstd")
        qT = P.sb([128, 8, T], BF16, "qT")
        ee = [P.sb([128, 2, T], BF16, "ee%d" % i) for i in range(2)]
        rden = [P.sb([128, T], F32, "rden%d" % i) for i in range(2)]
        lnd = P.sb([128, T], F32, "lnd")
        oT = P.sb([128, 8, T], BF16, "oT")
        xv = xT.rearrange("(k p) n -> p k n", p=128)
        ov = xoT.rearrange("(k p) n -> p k n", p=128)
        xk = keys("xt", 8)
        for t in range(NT // T):
            cs = slice(t * T, (t + 1) * T)
            P.dma("sp", xt[:], xv[:, :, cs], r=[], w=xk, key="xt")
            if io.get("yT") is not None:
                P.dma("sp", yt[:], io["yT"].rearrange("(k p) n -> p k n", p=128)[:, :, cs], r=[], w=keys("yt", 8), key="ytd")
            for k in (range(8) if io.get("yT") is None else []):
                tok = P.op("pool", lambda g, k=k, t=t: g.indirect_dma_start(out=yt[:, k, :], out_offset=None, in_=y_all,
                                                                           in_offset=bass.IndirectOffsetOnAxis(ap=idxy[:, 4 * k + t:4 * k + t + 1], axis=0)),
                           r=["idxy", "y_all"], w=[("yt", k)], dma="yt")
            for k in (range(8) if io.get("yT") is None else []):
                P.lastw[("yt", k)] = tok
            rhs_keys = keys("yt", 8)
            rhs_fn = lambda k: yt[:, k, :]
            if odd:
                for c in range(4):
                    g0, g1 = gt[0], gt[1]
                    P.op("dve", lambda e, c=c: e.tensor_copy(out=g0[:], in_=yt[:, c, :]), r=[("yt", c)], w=["gt0"])
                    P.op("act", lambda e: e.activation(out=g1[:], in_=g0[:], func=AF.Square), r=["gt0"], w=["gt1"])
                    P.op("dve", lambda e: e.tensor_scalar(out=g1[:], in0=g1[:], scalar1=0.044715, scalar2=1.0, op0=ALU.mult, op1=ALU.add),
                         r=["gt1"], w=["gt1"])
                    P.op("dve", lambda e: e.tensor_tensor(out=g1[:], in0=g1[:], in1=g0[:], op=ALU.mult), r=["gt1", "gt0"], w=["gt1"])
                    P.op("act", lambda e: e.activation(out=g1[:], in_=g1[:], func=AF.Sigmoid, scale=1.5957691216057308), r=["gt1"], w=["gt1"])
                    P.op("dve", lambda e, c=c: e.tensor_tensor(out=gg[:, c, :], in0=g0[:], in1=g1[:], op=ALU.mult),
                         r=["gt0", "gt1"], w=[("gg", c)])
                for m in range(4):
                    pb = psg[m % 3]
                    pn = "psg%d" % (m % 3)
                    for c in range(4):
                        P.op("pe", lambda e, c=c, m=m, pb=pb: e.matmul(pb[:], wglu[:, c, 128 * m:128 * m + 128], gg[:, c, :],
                                                                       start=(c == 0), stop=(c == 3)),
                             r=["wglu", ("gg", c)], w=[pn])
                    P.op("act", lambda e, pb=pb: e.activation(out=gt[1][:], in_=pb[:], func=AF.Sigmoid), r=[pn], w=["gt1"])
                    P.op("dve", lambda e, m=m: e.tensor_tensor(out=glu[:, m, :], in0=gg[:, m, :], in1=gt[1][:], op=ALU.mult),
                         r=[("gg", m), "gt1"], w=[("glu", m)])
                rhs_keys = keys("glu", 4) + keys("yt", 8)[4:]
                rhs_fn = lambda k: (glu[:, k, :] if k < 4 else yt[:, k, :])

            def proj_postnorm(wsb, wname, rfn, rkeys, gidx):
                for m in range(8):
                    pb = psg[m % 3]
                    pn = "psg%d" % (m % 3)
                    emit_proj(P, pb[:], pn, wsb, wname, 8, m, rfn, rkeys, T)
                    P.op("act", lambda e, pb=pb, m=m: e.activation(out=mo[:, m, :], in_=pb[:], func=AF.Copy), r=[pn], w=[("mo", m)])
                    P.op("act", lambda e, pb=pb, m=m: e.activation(out=sq[:, m, :], in_=pb[:], func=AF.Square), r=[pn], w=[("sq", m)])
                emit_postnorm_residual(C, mo, "mo", sq, "sq", gains[:, gidx, :], "gains", xt, xk, rstd, "rstd", tmp, "tmp")

            proj_postnorm(wout, "wout", rhs_fn, rhs_keys, 0)
            emit_prenorm(C, xt, xk, gains[:, 1, :], "gains", sq, "sq", h, "h", rstd, "rstd")
            for m in range(8):
                pb = psg[m % 3]
                pn = "psg%d" % (m % 3)
                emit_proj(P, pb[:], pn, wq, "wq", 8, m, lambda k: h[:, k, :], keys("h", 8), T)
                P.op("act", lambda e, pb=pb, m=m: e.activation(out=qT[:, m, :], in_=pb[:], func=AF.Copy, scale=1.0 / 16.0), r=[pn], w=[("qT", m)])
            for hd in range(4):
                e2 = ee[hd % 2]
                en = "ee%d" % (hd % 2)
                for j in range(2):
                    pb = pss[j]
                    pn = "pss%d" % j
                    for dd in range(2):
                        dch = 2 * hd + dd
                        P.op("pe", lambda e, pb=pb, dch=dch, j=j, dd=dd: e.matmul(pb[:], kT[:, dch, 128 * j:128 * j + 128], qT[:, dch, :],
                                                                                 start=(dd == 0), stop=(dd == 1)),
                             r=[("kT", dch), ("qT", dch)], w=[pn])
                    P.op("act", lambda e, pb=pb, j=j, e2=e2: e.activation(out=e2[:, j, :], in_=pb[:], func=AF.Exp), r=[pn], w=[(en, j)])
                for j in range(2):
                    P.op("pe", lambda e, j=j, e2=e2: e.matmul(psd[:], C.ones[:], e2[:, j, :], start=(j == 0), stop=(j == 1)),
                         r=["ones", (en, j)], w=["psden"])
                rd = rden[hd % 2]
                rn = "rden%d" % (hd % 2)
                P.op("act", lambda e: e.activation(out=lnd[:], in_=psd[:], func=AF.Ln), r=["psden"], w=["lnd"])
                P.op("act", lambda e, rd=rd: e.activation(out=rd[:], in_=lnd[:], func=AF.Exp, scale=-1.0), r=["lnd"], w=[rn])
                for dd in range(2):
                    dch = 2 * hd + dd
                    for j in range(2):
                        P.op("pe", lambda e, j=j, dch=dch, e2=e2: e.matmul(pso[:], vS[:, j, 128 * dch:128 * dch + 128], e2[:, j, :],
                                                                          start=(j == 0), stop=(j == 1)),
                             r=vkeys(j) + [(en, j)], w=["pspv"])
                    P.op("dve", lambda e, dch=dch, rd=rd: e.tensor_tensor(out=oT[:, dch, :], in0=pso[:], in1=rd[:], op=ALU.mult),
                         r=["pspv", rn], w=[("oT", dch)])
            proj_postnorm(wo, "wo", lambda k: oT[:, k, :], keys("oT", 8), 2)
            P.dma("act", ov[:, :, cs], xt[:], r=xk, w=[], key="xo", final=True)
        P.barrier()
        P.cur = P.st


def emit_PBE(P, io):
    nc = P.nc
    din = lambda name, shape, dt=F32: io[name]
    pu_d = din("pu", [128, L], BF16)
    poolw_d = din("poolw", [128, 1, 128])
    pscale_d = din("pscale", [128, 1])
    msel_d = din("msel", [4, L])
    cb_d = din("cb", [128, L], BF16)
    cc_d = din("cc", [128, L], BF16)
    ch_d = din("ch", [128, L], BF16)
    convw_d = din("convw", [128, 3])
    ypool_d = io["ypool"]
    yconv_d = io["yconv"]
    H = 8
    Wd = L + 2 * H
    BL = 2048
    with ExitStack() as st:
        P.cur = st
        psb = [P.ps([128, 512], F32, "psb%d" % i) for i in range(2)]
        with ExitStack() as st2:
            u = P.sb([128, Wd], BF16, "u", st2)
            A = P.sb([128, Wd], F32, "A", st2)
            Bf = P.sb([128, Wd], F32, "Bf", st2)
            acc = P.sb([128, L], F32, "acc", st2)
            mt = [P.sb([128, BL], F32, "mt%d" % i, st2) for i in range(2)]
            tmpb = P.sb([128, BL], F32, "tmpb", st2)
            pooled = P.sb([128, L], BF16, "pooled", st2)
            pw = P.sb([128, 1, 128], BF16, "pw", st2)
            psc = P.sb([128, 1], F32, "psc", st2)
            ost = [P.sb([128, 2048], BF16, "ost%d" % i, st2) for i in range(2)]
            load_weight(P, poolw_d, pw, "pw", 1)
            P.dma("sp", psc[:], pscale_d, r=[], w=["psc"], key="psc")
            P.op("dve", lambda e: e.memset(u[:, 0:H], 0.0), w=["u"])
            P.op("dve", lambda e: e.memset(u[:, Wd - H:Wd], 0.0), w=["u"])
            P.op("pool", lambda e: e.memset(A[:], 0.0), w=["A"])
            P.op("pool", lambda e: e.memset(Bf[:], 0.0), w=["Bf"])
            P.dma("sp", u[:, H:H + L], pu_d, r=[], w=["u"], key="u")
            nm = 0

            def accumulate(src, srcname, i):
                nonlocal nm
                for blk in range(L // BL):
                    m = mt[nm % 2]
                    mn_ = "mt%d" % (nm % 2)
                    nm += 1
                    P.dma("sp", m[:], msel_d[i:i + 1, blk * BL:(blk + 1) * BL].partition_broadcast(128), r=[], w=[mn_], key=mn_)
                    s_ = src[:, H + blk * BL:H + (blk + 1) * BL]
                    a_ = acc[:, blk * BL:(blk + 1) * BL]
                    if i == 0:
                        P.op("dve", lambda e, s_=s_, a_=a_, m=m: e.tensor_tensor(out=a_, in0=s_, in1=m[:], op=ALU.mult),
                             r=[srcname, mn_], w=[("acc", blk)])
                    else:
                        P.op("dve", lambda e, s_=s_, m=m: e.tensor_tensor(out=tmpb[:], in0=s_, in1=m[:], op=ALU.mult),
                             r=[srcname, mn_], w=["tmpb"])
                        P.op("pool", lambda e, a_=a_: e.tensor_tensor(out=a_, in0=a_, in1=tmpb[:], op=ALU.add),
                             r=["tmpb", ("acc", blk)], w=[("acc", blk)])

            P.op("dve", lambda e: e.tensor_tensor(out=A[:, 1:Wd], in0=u[:, 0:Wd - 1], in1=u[:, 1:Wd], op=ALU.add), r=["u"], w=["A"])
            accumulate(A, "A", 0)
            P.op("dve", lambda e: e.tensor_tensor(out=Bf[:, 2:Wd - 2], in0=A[:, 1:Wd - 3], in1=A[:, 3:Wd - 1], op=ALU.add), r=["A"], w=["Bf"])
            accumulate(Bf, "Bf", 1)
            P.op("dve", lambda e: e.tensor_tensor(out=A[:, 4:Wd - 4], in0=Bf[:, 2:Wd - 6], in1=Bf[:, 6:Wd - 2], op=ALU.add), r=["Bf"], w=["A"])
            accumulate(A, "A", 2)
            P.op("dve", lambda e: e.tensor_tensor(out=Bf[:, 8:Wd - 8], in0=A[:, 4:Wd - 12], in1=A[:, 12:Wd - 4], op=ALU.add), r=["A"], w=["Bf"])
            accumulate(Bf, "Bf", 3)
            for blk in range(L // BL):
                sl = slice(blk * BL, (blk + 1) * BL)
                P.op("dve", lambda e, sl=sl, blk=blk: e.tensor_tensor(out=pooled[:, sl], in0=acc[:, sl], in1=u[:, H + blk * BL:H + (blk + 1) * BL],
                                                                      op=ALU.subtract),
                     r=[("acc", blk), "u"], w=[("pooled", blk)])
            for blk in range(L // BL):
                o_ = ost[blk % 2]
                on = "ost%d" % (blk % 2)
                for s in range(BL // 512):
                    i = blk * 4 + s
                    pb = psb[i % 2]
                    pn = "psb%d" % (i % 2)
                    P.op("pe", lambda e, pb=pb, i=i: e.matmul(pb[:], pw[:, 0, :], pooled[:, i * 512:(i + 1) * 512], start=True, stop=True),
                         r=["pw", ("pooled", blk)], w=[pn])
                    P.op("act", lambda e, pb=pb, o_=o_, s=s: e.activation(out=o_[:, s * 512:(s + 1) * 512], in_=pb[:], func=AF.Copy, scale=psc[:, 0:1]),
                         r=[pn, "psc"], w=[(on, s)])
                P.dma("act", ypool_d[:, blk * BL:(blk + 1) * BL], o_[:], r=keys(on, 4), w=[], key=on, final=True)
            P.barrier()
        with ExitStack() as st3:
            cb = P.sb([128, L], BF16, "cb", st3)
            cc = P.sb([128, L], BF16, "cc", st3)
            chh = P.sb([128, L], BF16, "chh", st3)
            cw = P.sb([128, 3], F32, "cw", st3)
            mm_ = P.sb([128, L + 2], F32, "mm_", st3)
            aa = P.sb([128, L], F32, "aa", st3)
            yo = P.sb([128, L], BF16, "yo", st3)
            P.dma("sp", cb[:], cb_d, r=[], w=["cb"], key="cb")
            P.dma("sp", cc[:], cc_d, r=[], w=["cc"], key="cc")
            P.dma("sp", chh[:], ch_d, r=[], w=["chh"], key="chh")
            P.dma("sp", cw[:], convw_d, r=[], w=["cw"], key="cw")
            P.op("pool", lambda e: e.memset(mm_[:, 0:1], 0.0), w=["mm_"])
            P.op("pool", lambda e: e.memset(mm_[:, L + 1:L + 2], 0.0), w=["mm_"])
            P.op("dve", lambda e: e.tensor_tensor(out=mm_[:, 1:L + 1], in0=cc[:], in1=chh[:], op=ALU.mult), r=["cc", "chh"], w=["mm_"])
            P.op("dve", lambda e: e.tensor_scalar(out=aa[:], in0=mm_[:, 1:L + 1], scalar1=cw[:, 1:2], scalar2=None, op0=ALU.mult),
                 r=["mm_", "cw"], w=["aa"])
            P.op("dve", lambda e: e.scalar_tensor_tensor(out=aa[:], in0=mm_[:, 0:L], scalar=cw[:, 0:1], in1=aa[:], op0=ALU.mult, op1=ALU.add),
                 r=["mm_", "cw", "aa"], w=["aa"])
            P.op("dve", lambda e: e.scalar_tensor_tensor(out=aa[:], in0=mm_[:, 2:L + 2], scalar=cw[:, 2:3], in1=aa[:], op0=ALU.mult, op1=ALU.add),
                 r=["mm_", "cw", "aa"], w=["aa"])
            P.op("pool", lambda e: e.tensor_tensor(out=yo[:], in0=aa[:], in1=cb[:], op=ALU.mult), r=["aa", "cb"], w=["yo"])
            P.dma("act", yconv_d, yo[:], r=["yo"], w=[], key="yo", final=True)
            P.barrier()
        P.cur = P.st


def emit_PBE_tok(P, io):
    p_loc = io["p_loc"]
    hal_all = io["hal_all"]
    yT = io["yT"]
    H = 8
    Wd = NT + 2 * H
    with ExitStack() as st:
        P.cur = st
        V = lambda fn, r, w: P.op("dve", fn, r=r, w=w)
        idxh = P.sb([128, 32], mybir.dt.int32, "idxh")
        hm = P.sb([128, 2], F32, "hm")
        pw = P.sb([128, 4, 128], BF16, "pw4")
        psc = P.sb([128, 4], F32, "psc4")
        cw = P.sb([128, 4, 3], F32, "cw4")
        P.dma("sp", idxh[:], io["idxh"], r=[], w=["idxh"], key="pbe_par")
        P.dma("sp", hm[:], io["hmask"], r=[], w=["hm"], key="pbe_par")
        P.dma("sp", psc[:], io["pscale4"], r=[], w=["psc4"], key="pbe_par")
        P.dma("sp", cw[:], io["convw4"], r=[], w=["cw4"], key="pbe_par")
        for k_ in ("idxh", "hm", "psc4", "cw4"):
            P.lastw[k_] = P.lastw["cw4"]
        load_weight(P, io["poolw4"], pw, "pw4", 4)
        hl = P.sb([128, 16, 16], BF16, "hl")
        hr = P.sb([128, 16, 16], BF16, "hr")
        for side, ht, hn in ((0, hl, "hl"), (1, hr, "hr")):
            for ch in range(16):
                tok = P.op("pool", lambda g, ht=ht, ch=ch, side=side: g.indirect_dma_start(
                    out=ht[:, ch, :], out_offset=None, in_=hal_all,
                    in_offset=bass.IndirectOffsetOnAxis(ap=idxh[:, 16 * side + ch:16 * side + ch + 1], axis=0)),
                    r=["idxh", "hal_all"], w=[(hn, ch)], dma=hn)
            for ch in range(16):
                P.lastw[(hn, ch)] = tok
            V(lambda e, ht=ht, side=side: e.tensor_scalar(out=ht[:], in0=ht[:], scalar1=hm[:, side:side + 1], scalar2=None, op0=ALU.mult),
              [(hn, ch) for ch in range(16)] + ["hm"], [(hn, ch) for ch in range(16)])
        us = [P.sb([128, Wd], BF16, "u%d" % i) for i in range(2)]
        A = P.sb([128, Wd], F32, "A")
        Bf = P.sb([128, Wd], F32, "Bf")
        invb = [P.sb([128, NT], F32, "invb%d" % i) for i in range(2)]
        pooled = [P.sb([128, NT], BF16, "pooled%d" % i) for i in range(2)]
        ost = [P.sb([128, NT], BF16, "ost%d" % i) for i in range(2)]
        psb = [P.ps([128, 512], F32, "psb%d" % i) for i in range(2)]
        V(lambda e: e.memset(A[:], 0.0), [], ["A"])
        V(lambda e: e.memset(Bf[:], 0.0), [], ["Bf"])
        for q in range(4):
            u = us[q % 2]
            un = "u%d" % (q % 2)
            P.dma("sp", u[:, H:H + NT], p_loc[128 * q:128 * q + 128, :], r=[], w=[(un, "c")], key=un)
            P.dma("sp", invb[q % 2][:], io["invc"][q:q + 1, :].partition_broadcast(128), r=[], w=["invb%d" % (q % 2)], key="invb%d" % (q % 2))
            V(lambda e: e.tensor_copy(out=u[:, 0:H], in_=hl[:, q, 8:16]), [("hl", q)], [(un, "l")])
            V(lambda e: e.tensor_copy(out=u[:, H + NT:Wd], in_=hr[:, q, 0:8]), [("hr", q)], [(un, "r")])
            uk = [(un, "c"), (un, "l"), (un, "r")]
            V(lambda e: e.tensor_tensor(out=A[:, 1:Wd], in0=u[:, 0:Wd - 1], in1=u[:, 1:Wd], op=ALU.add), uk, ["A"])
            S_, Sn = A, "A"
            if q >= 1:
                V(lambda e: e.tensor_tensor(out=Bf[:, 2:Wd - 2], in0=A[:, 1:Wd - 3], in1=A[:, 3:Wd - 1], op=ALU.add), ["A"], ["Bf"])
                S_, Sn = Bf, "Bf"
            if q >= 2:
                V(lambda e: e.tensor_tensor(out=A[:, 4:Wd - 4], in0=Bf[:, 2:Wd - 6], in1=Bf[:, 6:Wd - 2], op=ALU.add), ["Bf"], ["A"])
                S_, Sn = A, "A"
            if q >= 3:
                V(lambda e: e.tensor_tensor(out=Bf[:, 8:Wd - 8], in0=A[:, 4:Wd - 12], in1=A[:, 12:Wd - 4], op=ALU.add), ["A"], ["Bf"])
                S_, Sn = Bf, "Bf"
            pl = pooled[q % 2]
            pln = "pooled%d" % (q % 2)
            V(lambda e, S_=S_: e.tensor_tensor(out=S_[:, H:H + NT], in0=S_[:, H:H + NT], in1=invb[q % 2][:], op=ALU.mult), [Sn, "invb%d" % (q % 2)], [Sn])
            V(lambda e, S_=S_: e.tensor_tensor(out=pl[:], in0=S_[:, H:H + NT], in1=u[:, H:H + NT], op=ALU.subtract), [Sn] + uk, [pln])
            o_ = ost[q % 2]
            on = "ost%d" % (q % 2)
            for s4 in range(NT // 512):
                pb = psb[s4 % 2]
                pn = "psb%d" % (s4 % 2)
                P.op("pe", lambda e, pb=pb, s4=s4: e.matmul(pb[:], pw[:, q, :], pl[:, s4 * 512:(s4 + 1) * 512], start=True, stop=True),
                     r=["pw4", pln], w=[pn])
                P.op("act", lambda e, pb=pb, s4=s4: e.activation(out=o_[:, s4 * 512:(s4 + 1) * 512], in_=pb[:], func=AF.Copy, scale=psc[:, q:q + 1]),
                     r=[pn, "psc4"], w=[(on, s4)])
            P.dma("act", yT[128 * q:128 * q + 128, :], o_[:], r=keys(on, 4), w=[], key=on, final=True)
        cbs = [P.sb([128, NT], BF16, "cb%d" % i) for i in range(2)]
        ccs = [P.sb([128, NT + 2], BF16, "cc%d" % i) for i in range(2)]
        chs = [P.sb([128, NT + 2], BF16, "chh%d" % i) for i in range(2)]
        mm_ = P.sb([128, NT + 2], F32, "mm_")
        aa = P.sb([128, NT], F32, "aa")
        for j in range(4):
            cb, cc, chh = cbs[j % 2], ccs[j % 2], chs[j % 2]
            n_ = "cv%d" % (j % 2)
            P.dma("sp", cb[:], p_loc[512 + 128 * j:512 + 128 * j + 128, :], r=[], w=[(n_, "b")], key=n_)
            P.dma("sp", cc[:, 1:NT + 1], p_loc[1024 + 128 * j:1024 + 128 * j + 128, :], r=[], w=[(n_, "c")], key=n_)
            P.dma("sp", chh[:, 1:NT + 1], p_loc[1536 + 128 * j:1536 + 128 * j + 128, :], r=[], w=[(n_, "h")], key=n_)
            for k_ in ("b", "c", "h"):
                P.lastw[(n_, k_)] = P.lastw[(n_, "h")]
            V(lambda e: e.tensor_copy(out=cc[:, 0:1], in_=hl[:, 8 + j, 15:16]), [("hl", 8 + j)], [(n_, "cl")])
            V(lambda e: e.tensor_copy(out=cc[:, NT + 1:NT + 2], in_=hr[:, 8 + j, 0:1]), [("hr", 8 + j)], [(n_, "cr")])
            V(lambda e: e.tensor_copy(out=chh[:, 0:1], in_=hl[:, 12 + j, 15:16]), [("hl", 12 + j)], [(n_, "hl")])
            V(lambda e: e.tensor_copy(out=chh[:, NT + 1:NT + 2], in_=hr[:, 12 + j, 0:1]), [("hr", 12 + j)], [(n_, "hr")])
            ck = [(n_, x) for x in ("b", "c", "h", "cl", "cr", "hl", "hr")]
            V(lambda e: e.tensor_tensor(out=mm_[:], in0=cc[:], in1=chh[:], op=ALU.mult), ck, ["mm_"])
            V(lambda e: e.tensor_scalar(out=aa[:], in0=mm_[:, 1:NT + 1], scalar1=cw[:, j, 1:2], scalar2=None, op0=ALU.mult), ["mm_", "cw4"], ["aa"])
            V(lambda e: e.scalar_tensor_tensor(out=aa[:], in0=mm_[:, 0:NT], scalar=cw[:, j, 0:1], in1=aa[:], op0=ALU.mult, op1=ALU.add),
              ["mm_", "cw4", "aa"], ["aa"])
            V(lambda e: e.scalar_tensor_tensor(out=aa[:], in0=mm_[:, 2:NT + 2], scalar=cw[:, j, 2:3], in1=aa[:], op0=ALU.mult, op1=ALU.add),
              ["mm_", "cw4", "aa"], ["aa"])
            o_ = ost[j % 2]
            on = "ost%d" % (j % 2)
            V(lambda e: e.tensor_tensor(out=o_[:], in0=aa[:], in1=cb[:], op=ALU.mult), ["aa"] + ck, keys(on, 4))
            P.dma("act", yT[512 + 128 * j:512 + 128 * j + 128, :], o_[:], r=keys(on, 4), w=[], key=on, final=True)
        P.barrier()
        P.cur = P.st


def rev_ap(ap, n):
    a = ap
    return AP(a.tensor, a.offset + (n - 1), [list(a.ap[0]), [-1, n]])


def emit_PBS5(P, io):
    nc = P.nc
    din = lambda name, shape, dt=F32: io[name]
    u_d = din("u", [2, 32, 2, L], BF16)
    par_d = din("par", [2, 2, 128, 4])
    bmat_d = din("bmat", [2, 128, 2, 16])
    cmat_d = din("cmat", [2, 2, 128, 2, 16])
    dsk_d = din("dsk", [2, 32, 1])
    ident_d = din("ident", [128, 128])
    ys_d = io["ys"]
    ys0_d = io["ys0"]
    BLK = 1024
    NSUB = BLK // 512
    NB = L // BLK
    with ExitStack() as st:
        P.cur = st
        ident = P.sb([128, 128], F32, "ident")
        P.dma("sp", ident[:], ident_d, r=[], w=["ident"], key="ident")
        tv = P.sb([128, BLK], F32, "tv")
        P.op("pool", lambda e: e.iota(tv[:], pattern=[[1, BLK]], base=0, channel_multiplier=0, allow_small_or_imprecise_dtypes=True), w=["tv"])
        sn = P.sb([128, BLK], F32, "sn")
        cs = P.sb([128, BLK], F32, "cs")
        wri = [(P.sb([128, BLK], F32, "wre%d" % i), P.sb([128, BLK], F32, "wim%d" % i)) for i in range(2)]
        xri = [(P.sb([128, BLK], F32, "xr%d" % i), P.sb([128, BLK], F32, "xi%d" % i)) for i in range(2)]
        m_ = [P.sb([128, BLK], F32, "m%d" % i) for i in range(4)]
        bri = [(P.sb([128, BLK], F32, "bre%d" % i), P.sb([128, BLK], F32, "bim%d" % i)) for i in range(2)]
        dm = [[P.sb([128, BLK], BF16, "dm%d_%d" % (i, j)) for j in range(4)] for i in range(2)]
        G = lambda fn, r, w: P.op("pool", fn, r=r, w=w)
        ub = [P.sb([32, BLK], BF16, "ub%d" % i) for i in range(3)]
        ystage = [P.sb([32, BLK], BF16, "ystage%d" % i) for i in range(2)]
        carry = P.sb([128, 8], F32, "carry")
        yprev = P.sb([32, BLK], BF16, "yprev")
        par = P.sb([128, 4], F32, "par")
        bm = P.sb([128, 2, 16], F32, "bm")
        cm = P.sb([128, 2, 16], F32, "cm")
        sc = P.sb([128, 32], F32, "sc")
        cb16 = P.sb([128, 2, 16], F32, "cb16")
        t16 = P.sb([128, 2, 16], F32, "t16")
        bd = P.sb([128, 2, 32], F32, "bd")
        lb16 = P.sb([32, 2, 128], BF16, "lb16")
        cbd = P.sb([128, 3, 32], BF16, "cbd")
        dsk = P.sb([32, 1], F32, "dsk")
        ddiag = P.sb([32, 32], BF16, "ddiag")
        ps_ri = [[P.ps([128, 512], F32, "ps_ri%d%d" % (i, j)) for j in range(2)] for i in range(2)]
        ps_y = [P.ps([32, 512], F32, "ps_y%d" % i) for i in range(2)]
        ps_t = P.ps([32, 128], F32, "ps_t")
        col = lambda i: sc[:, i:i + 1]
        (LRE, DT, AA, RHO, TH, THT, RN, FR, SINT, COST, LBR, LBI, NUMR, MAG, INV, CR, CI, NCI, T1, T2, TC, LIM, RT, RS, RC, NRS) = range(26)
        V = lambda fn, r, w: P.op("dve", fn, r=r, w=w)
        nys = 0
        nub = 0
        for gp in range(2):
            P.dma("sp", bm[:], bmat_d[gp], r=[], w=["bm"], key="bm")
            P.dma("sp", dsk[:], dsk_d[gp], r=[], w=["dsk"], key="dsk")
            V(lambda e: e.tensor_scalar(out=ddiag[:], in0=ident[0:32, 0:32], scalar1=dsk[:, 0:1], scalar2=None, op0=ALU.mult),
              ["ident", "dsk"], ["ddiag"])
            for d in range(2):
                P.dma("sp", par[:], par_d[gp, d], r=[], w=["par"], key="par")
                P.dma("sp", cm[:], cmat_d[gp, d], r=[], w=["cm"], key="cm")
                S = ["sc"]
                V(lambda e: e.tensor_scalar(out=col(LRE), in0=par[:, 0:1], scalar1=-1e-4, scalar2=None, op0=ALU.min), ["par"], S)
                V(lambda e: e.tensor_copy(out=col(LIM), in_=par[:, 1:2]), ["par"], S)
                P.op("act", lambda e: e.activation(out=col(DT), in_=par[:, 2:3], func=AF.Exp), r=["par"], w=S)
                V(lambda e: e.tensor_tensor(out=col(AA), in0=col(LRE), in1=col(DT), op=ALU.mult), S, S)
                P.op("act", lambda e: e.activation(out=col(RHO), in_=col(AA), func=AF.Exp), r=S, w=S)
                V(lambda e: e.tensor_tensor(out=col(TH), in0=col(LIM), in1=col(DT), op=ALU.mult), S, S)
                V(lambda e: e.tensor_scalar(out=col(THT), in0=col(TH), scalar1=1.0 / TWO_PI, scalar2=None, op0=ALU.mult), S, S)
                V(lambda e: e.tensor_scalar(out=col(RN), in0=col(THT), scalar1=MAGIC, scalar2=MAGIC, op0=ALU.add, op1=ALU.subtract), S, S)
                V(lambda e: e.tensor_tensor(out=col(FR), in0=col(THT), in1=col(RN), op=ALU.subtract), S, S)
                P.op("act", lambda e: e.activation(out=col(SINT), in_=col(FR), func=AF.Sin, scale=TWO_PI), r=S, w=S)
                V(lambda e: e.tensor_scalar(out=col(TC), in0=col(THT), scalar1=0.25, scalar2=None, op0=ALU.add), S, S)
                V(lambda e: e.tensor_scalar(out=col(RN), in0=col(TC), scalar1=MAGIC, scalar2=MAGIC, op0=ALU.add, op1=ALU.subtract), S, S)
                V(lambda e: e.tensor_tensor(out=col(FR), in0=col(TC), in1=col(RN), op=ALU.subtract), S, S)
                P.op("act", lambda e: e.activation(out=col(COST), in_=col(FR), func=AF.Sin, scale=TWO_PI), r=S, w=S)
                V(lambda e: e.tensor_tensor(out=col(LBR), in0=col(RHO), in1=col(COST), op=ALU.mult), S, S)
                V(lambda e: e.tensor_tensor(out=col(LBI), in0=col(RHO), in1=col(SINT), op=ALU.mult), S, S)
                V(lambda e: e.tensor_scalar(out=col(NUMR), in0=col(LBR), scalar1=-1.0, scalar2=None, op0=ALU.add), S, S)
                V(lambda e: e.tensor_tensor(out=col(MAG), in0=col(LRE), in1=col(LRE), op=ALU.mult), S, S)
                V(lambda e: e.tensor_tensor(out=col(T1), in0=col(LIM), in1=col(LIM), op=ALU.mult), S, S)
                V(lambda e: e.tensor_tensor(out=col(MAG), in0=col(MAG), in1=col(T1), op=ALU.add), S, S)
                V(lambda e: e.reciprocal(out=col(INV), in_=col(MAG)), S, S)
                V(lambda e: e.tensor_tensor(out=col(T1), in0=col(NUMR), in1=col(LRE), op=ALU.mult), S, S)
                V(lambda e: e.tensor_tensor(out=col(T2), in0=col(LBI), in1=col(LIM), op=ALU.mult), S, S)
                V(lambda e: e.tensor_tensor(out=col(T1), in0=col(T1), in1=col(T2), op=ALU.add), S, S)
                V(lambda e: e.tensor_tensor(out=col(CR), in0=col(T1), in1=col(INV), op=ALU.mult), S, S)
                V(lambda e: e.tensor_tensor(out=col(T1), in0=col(LBI), in1=col(LRE), op=ALU.mult), S, S)
                V(lambda e: e.tensor_tensor(out=col(T2), in0=col(NUMR), in1=col(LIM), op=ALU.mult), S, S)
                V(lambda e: e.tensor_tensor(out=col(T1), in0=col(T1), in1=col(T2), op=ALU.subtract), S, S)
                V(lambda e: e.tensor_tensor(out=col(CI), in0=col(T1), in1=col(INV), op=ALU.mult), S, S)
                V(lambda e: e.tensor_scalar(out=col(NCI), in0=col(CI), scalar1=-1.0, scalar2=None, op0=ALU.mult), S, S)
                V(lambda e: e.tensor_scalar(out=t16[:, 0, :], in0=bm[:, 0, :], scalar1=col(CR), scalar2=None, op0=ALU.mult), ["bm"] + S, ["t16"])
                V(lambda e: e.scalar_tensor_tensor(out=cb16[:, 0, :], in0=bm[:, 1, :], scalar=col(NCI), in1=t16[:, 0, :], op0=ALU.mult, op1=ALU.add),
                  ["bm", "t16"] + S, ["cb16"])
                V(lambda e: e.tensor_scalar(out=t16[:, 1, :], in0=bm[:, 1, :], scalar1=col(CR), scalar2=None, op0=ALU.mult), ["bm"] + S, ["t16"])
                V(lambda e: e.scalar_tensor_tensor(out=cb16[:, 1, :], in0=bm[:, 0, :], scalar=col(CI), in1=t16[:, 1, :], op0=ALU.mult, op1=ALU.add),
                  ["bm", "t16"] + S, ["cb16"])
                V(lambda e: e.memset(bd[:], 0.0), [], ["bd"])
                for ri in range(2):
                    V(lambda e, ri=ri: e.tensor_copy(out=bd[0:64, ri, 0:16], in_=cb16[0:64, ri, :]), ["cb16"], ["bd"])
                    V(lambda e, ri=ri: e.tensor_copy(out=bd[64:128, ri, 16:32], in_=cb16[64:128, ri, :]), ["cb16"], ["bd"])
                for ri in range(2):
                    P.op("pe", lambda e, ri=ri: e.transpose(out=ps_t[:], in_=bd[:, ri, :], identity=ident[:]), r=["bd", "ident"], w=["ps_t"])
                    V(lambda e, ri=ri: e.tensor_copy(out=lb16[:, ri, :], in_=ps_t[:]), ["ps_t"], [("lb16", ri)])
                V(lambda e: e.memset(cbd[:], 0.0), [], ["cbd"])
                V(lambda e: e.tensor_copy(out=cbd[0:64, 0, 0:16], in_=cm[0:64, 0, :]), ["cm"], ["cbd"])
                V(lambda e: e.tensor_copy(out=cbd[64:128, 0, 16:32], in_=cm[64:128, 0, :]), ["cm"], ["cbd"])
                V(lambda e: e.tensor_scalar(out=cbd[0:64, 1, 0:16], in0=cm[0:64, 1, :], scalar1=-1.0, scalar2=None, op0=ALU.mult), ["cm"], ["cbd"])
                V(lambda e: e.tensor_scalar(out=cbd[64:128, 1, 16:32], in0=cm[64:128, 1, :], scalar1=-1.0, scalar2=None, op0=ALU.mult), ["cm"], ["cbd"])
                V(lambda e: e.tensor_scalar(out=cbd[0:64, 2, 0:16], in0=cm[0:64, 0, :], scalar1=-1.0, scalar2=None, op0=ALU.mult), ["cm"], ["cbd"])
                V(lambda e: e.tensor_scalar(out=cbd[64:128, 2, 16:32], in0=cm[64:128, 0, :], scalar1=-1.0, scalar2=None, op0=ALU.mult), ["cm"], ["cbd"])
                V(lambda e: e.tensor_scalar(out=m_[0][:], in0=tv[:], scalar1=col(THT), scalar2=None, op0=ALU.mult), ["tv"] + S, ["m0"])
                V(lambda e: e.tensor_scalar(out=m_[1][:], in0=m_[0][:], scalar1=MAGIC, scalar2=MAGIC, op0=ALU.add, op1=ALU.subtract), ["m0"], ["m1"])
                V(lambda e: e.tensor_tensor(out=m_[1][:], in0=m_[0][:], in1=m_[1][:], op=ALU.subtract), ["m0", "m1"], ["m1"])
                P.op("act", lambda e: e.activation(out=sn[:], in_=m_[1][:], func=AF.Sin, scale=TWO_PI), r=["m1"], w=["sn"])
                V(lambda e: e.tensor_scalar(out=m_[0][:], in0=m_[0][:], scalar1=0.25, scalar2=None, op0=ALU.add), ["m0"], ["m0"])
                V(lambda e: e.tensor_scalar(out=m_[1][:], in0=m_[0][:], scalar1=MAGIC, scalar2=MAGIC, op0=ALU.add, op1=ALU.subtract), ["m0"], ["m1"])
                V(lambda e: e.tensor_tensor(out=m_[1][:], in0=m_[0][:], in1=m_[1][:], op=ALU.subtract), ["m0", "m1"], ["m1"])
                P.op("act", lambda e: e.activation(out=cs[:], in_=m_[1][:], func=AF.Sin, scale=TWO_PI), r=["m1"], w=["cs"])
                if d == 1:
                    V(lambda e: e.tensor_copy(out=m_[0][:], in_=rev_ap(cs[:], BLK)), ["cs"], ["m0"])
                    V(lambda e: e.tensor_copy(out=cs[:], in_=m_[0][:]), ["m0"], ["cs"])
                    V(lambda e: e.tensor_copy(out=m_[0][:], in_=rev_ap(sn[:], BLK)), ["sn"], ["m0"])
                    V(lambda e: e.tensor_copy(out=sn[:], in_=m_[0][:]), ["m0"], ["sn"])
                V(lambda e: e.tensor_scalar(out=col(RT), in0=col(THT), scalar1=float(BLK), scalar2=None, op0=ALU.mult), S, S)
                V(lambda e: e.tensor_scalar(out=col(RN), in0=col(RT), scalar1=MAGIC, scalar2=MAGIC, op0=ALU.add, op1=ALU.subtract), S, S)
                V(lambda e: e.tensor_tensor(out=col(FR), in0=col(RT), in1=col(RN), op=ALU.subtract), S, S)
                P.op("act", lambda e: e.activation(out=col(RS), in_=col(FR), func=AF.Sin, scale=TWO_PI), r=S, w=S)
                V(lambda e: e.tensor_scalar(out=col(TC), in0=col(RT), scalar1=0.25, scalar2=None, op0=ALU.add), S, S)
                V(lambda e: e.tensor_scalar(out=col(RN), in0=col(TC), scalar1=MAGIC, scalar2=MAGIC, op0=ALU.add, op1=ALU.subtract), S, S)
                V(lambda e: e.tensor_tensor(out=col(FR), in0=col(TC), in1=col(RN), op=ALU.subtract), S, S)
                P.op("act", lambda e: e.activation(out=col(RC), in_=col(FR), func=AF.Sin, scale=TWO_PI), r=S, w=S)
                V(lambda e: e.tensor_scalar(out=col(NRS), in0=col(RS), scalar1=-1.0, scalar2=None, op0=ALU.mult), S, S)
                order = list(range(NB)) if d == 0 else list(range(NB - 1, -1, -1))
                rho_bc = col(RHO).to_broadcast([128, BLK])
                units = [(oi, bi, b) for oi, bi in enumerate(order) for b in range(2)]

                def stage1(oi, bi, b):
                    nonlocal nub
                    t0 = bi * BLK
                    uj = nub % 3
                    nub += 1
                    ubt = ub[uj]
                    un = "ub%d" % uj
                    P.dma("sp", ubt[:], u_d[gp, :, b, t0:t0 + BLK], r=[], w=[un], key=un)
                    bre, bim = bri[b]
                    for s4 in range(NSUB):
                        c_ = slice(s4 * 512, (s4 + 1) * 512)
                        pr, pi = ps_ri[s4 % 2]
                        prn, pin = "ps_r%d" % (s4 % 2), "ps_i%d" % (s4 % 2)
                        P.op("pe", lambda e: e.matmul(pr[:], lb16[:, 0, :], ubt[:, c_], start=True, stop=True), r=[("lb16", 0), un], w=[prn])
                        P.op("pe", lambda e: e.matmul(pi[:], lb16[:, 1, :], ubt[:, c_], start=True, stop=True), r=[("lb16", 1), un], w=[pin])
                        P.op("act", lambda e: e.activation(out=bre[:, c_], in_=pr[:], func=AF.Copy), r=[prn], w=[("bre%d" % b, s4)])
                        P.op("act", lambda e: e.activation(out=bim[:, c_], in_=pi[:], func=AF.Copy), r=[pin], w=[("bim%d" % b, s4)])
                    return ubt, un

                def stageB(oi, bi, b):
                    bre, bim = bri[b]
                    wre, wim = wri[b]
                    brk, bik = keys("bre%d" % b, NSUB), keys("bim%d" % b, NSUB)
                    V(lambda e: e.tensor_tensor(out=m_[0][:], in0=bre[:], in1=cs[:], op=ALU.mult), brk + ["cs"], ["m0"])
                    V(lambda e: e.tensor_tensor(out=m_[1][:], in0=bim[:], in1=sn[:], op=ALU.mult), bik + ["sn"], ["m1"])
                    V(lambda e: e.tensor_tensor(out=wre[:], in0=m_[0][:], in1=m_[1][:], op=ALU.add), ["m0", "m1"], ["wre%d" % b])
                    V(lambda e: e.tensor_tensor(out=m_[2][:], in0=bim[:], in1=cs[:], op=ALU.mult), bik + ["cs"], ["m2"])
                    V(lambda e: e.tensor_tensor(out=m_[3][:], in0=bre[:], in1=sn[:], op=ALU.mult), brk + ["sn"], ["m3"])
                    V(lambda e: e.tensor_tensor(out=wim[:], in0=m_[2][:], in1=m_[3][:], op=ALU.subtract), ["m2", "m3"], ["wim%d" % b])

                def stage2(oi, bi, b):
                    wre, wim = wri[b]
                    xr, xi = xri[b]
                    if oi > 0:
                        cr_, ci_ = carry[:, 2 * b:2 * b + 1], carry[:, 2 * b + 1:2 * b + 2]
                        ir_, ii_ = carry[:, 4 + 2 * b:5 + 2 * b], carry[:, 5 + 2 * b:6 + 2 * b]
                        V(lambda e: e.tensor_scalar(out=col(T1), in0=cr_, scalar1=col(RC), scalar2=None, op0=ALU.mult), ["carry"] + S, S)
                        V(lambda e: e.scalar_tensor_tensor(out=ir_, in0=ci_, scalar=col(NRS), in1=col(T1), op0=ALU.mult, op1=ALU.add), ["carry"] + S, ["carry"])
                        V(lambda e: e.tensor_scalar(out=col(T2), in0=ci_, scalar1=col(RC), scalar2=None, op0=ALU.mult), ["carry"] + S, S)
                        V(lambda e: e.scalar_tensor_tensor(out=ii_, in0=cr_, scalar=col(RS), in1=col(T2), op0=ALU.mult, op1=ALU.add), ["carry"] + S, ["carry"])
                    for (wt, wn, xt_, xn, ci2) in ((wre, "wre%d" % b, xr, "xr%d" % b, 0), (wim, "wim%d" % b, xi, "xi%d" % b, 1)):
                        init = 0.0 if oi == 0 else carry[:, 4 + 2 * b + ci2:5 + 2 * b + ci2]
                        if d == 0:
                            dat, out_ = wt[:], xt_[:]
                            last = xt_[:, BLK - 1:BLK]
                        else:
                            dat, out_ = rev_ap(wt[:], BLK), rev_ap(xt_[:], BLK)
                            last = xt_[:, 0:1]
                        V(lambda e: e.tensor_tensor_scan(out=out_, data0=rho_bc, data1=dat, initial=init, op0=ALU.mult, op1=ALU.add),
                          [wn, "carry"] + S, [xn])
                        V(lambda e: e.tensor_copy(out=carry[:, 2 * b + ci2:2 * b + ci2 + 1], in_=last), [xn], ["carry"])

                def stage3(oi, bi, b, ubt, un):
                    nonlocal nys
                    t0 = bi * BLK
                    xr, xi = xri[b]
                    xrn, xin_ = "xr%d" % b, "xi%d" % b
                    q4 = dm[b]
                    qn = ["dm%d_%d" % (b, i) for i in range(4)]
                    V(lambda e: e.tensor_tensor(out=q4[0][:], in0=xr[:], in1=cs[:], op=ALU.mult), [xrn, "cs"], [qn[0]])
                    V(lambda e: e.tensor_tensor(out=q4[1][:], in0=xi[:], in1=sn[:], op=ALU.mult), [xin_, "sn"], [qn[1]])
                    V(lambda e: e.tensor_tensor(out=q4[2][:], in0=xr[:], in1=sn[:], op=ALU.mult), [xrn, "sn"], [qn[2]])
                    V(lambda e: e.tensor_tensor(out=q4[3][:], in0=xi[:], in1=cs[:], op=ALU.mult), [xin_, "cs"], [qn[3]])
                    yj = nys % 2
                    nys += 1
                    yst = ystage[yj]
                    yn = "ystage%d" % yj
                    for s4 in range(NSUB):
                        c_ = slice(s4 * 512, (s4 + 1) * 512)
                        py = ps_y[s4 % 2]
                        pyn = "ps_y%d" % (s4 % 2)
                        P.op("pe", lambda e: e.matmul(py[:], cbd[:, 0, :], q4[0][:, c_], start=True, stop=False), r=["cbd", qn[0]], w=[pyn])
                        P.op("pe", lambda e: e.matmul(py[:], cbd[:, 2, :], q4[1][:, c_], start=False, stop=False), r=["cbd", qn[1]], w=[pyn])
                        P.op("pe", lambda e: e.matmul(py[:], cbd[:, 1, :], q4[2][:, c_], start=False, stop=False), r=["cbd", qn[2]], w=[pyn])
                        P.op("pe", lambda e: e.matmul(py[:], cbd[:, 1, :], q4[3][:, c_], start=False, stop=(d == 1)), r=["cbd", qn[3]], w=[pyn])
                        if d == 0:
                            P.op("pe", lambda e: e.matmul(py[:], ddiag[:], ubt[:, c_], start=False, stop=True), r=["ddiag", un], w=[pyn])
                        P.op("act", lambda e: e.activation(out=yst[:, c_], in_=py[:], func=AF.Copy), r=[pyn], w=[(yn, s4)])
                    if d == 0:
                        P.dma("act", ys0_d[gp, :, b, t0:t0 + BLK], yst[:], r=keys(yn, NSUB), w=[("ys0", gp, b, bi)], key=yn)
                    else:
                        P.dma("sp", yprev[:], ys0_d[gp, :, b, t0:t0 + BLK], r=[("ys0", gp, b, bi)], w=["yprev"], key="yprev")
                        V(lambda e: e.tensor_tensor(out=yst[:], in0=yst[:], in1=yprev[:], op=ALU.add), keys(yn, NSUB) + ["yprev"], keys(yn, NSUB))
                        P.dma("act", ys_d[gp, :, b, t0:t0 + BLK], yst[:], r=keys(yn, NSUB), w=[], key=yn, final=True)

                nu = len(units)
                ubs = {0: stage1(*units[0])}
                if nu > 1:
                    ubs[1] = stage1(*units[1])
                stageB(*units[0])
                for k, u_ in enumerate(units):
                    if k + 2 < nu:
                        ubs[k + 2] = stage1(*units[k + 2])
                    if k + 1 < nu:
                        stageB(*units[k + 1])
                    stage2(*u_)
                    stage3(*u_, *ubs.pop(k))
        P.barrier()
        P.cur = P.st


def s5_inputs(u, lam_re, lam_im, log_dt, b_re, b_im, c_re, c_im, dsk):
    ident = np.eye(128, dtype=np.float32)
    ims = []
    for c in range(NCORES):
        g0 = 4 * c
        uu = None if u is None else np.ascontiguousarray(u[:, :, 64 * c:64 * c + 64].reshape(B, L, 2, 32).transpose(2, 3, 0, 1))
        par = np.zeros((2, 2, 128, 4), np.float32)
        bmat = np.zeros((2, 128, 2, 16), np.float32)
        cmat = np.zeros((2, 2, 128, 2, 16), np.float32)
        for gp in range(2):
            for g2 in range(2):
                g = g0 + 2 * gp + g2
                ps = slice(64 * g2, 64 * g2 + 64)
                bmat[gp, ps, 0] = b_re[g]
                bmat[gp, ps, 1] = b_im[g]
                for d in range(2):
                    par[gp, d, ps, 0] = lam_re[d, g]
                    par[gp, d, ps, 1] = lam_im[d, g]
                    par[gp, d, ps, 2] = log_dt[d, g]
                    cmat[gp, d, ps, 0] = c_re[d, g].T
                    cmat[gp, d, ps, 1] = c_im[d, g].T
        ims.append({"u": uu, "par": par, "bmat": bmat, "cmat": cmat,
                    "dsk": np.ascontiguousarray(dsk[64 * c:64 * c + 64].reshape(2, 32, 1)), "ident": ident})
    return ims


def s5_outputs(res):
    outs = []
    for c in range(NCORES):
        ys = np.asarray(res[c]["ys"])
        outs.append(ys.transpose(0, 3, 4, 1, 2).reshape(2, B, L, 64))
    return np.concatenate(outs, axis=-1)


def fview(ap, dims):
    return AP(ap.tensor, ap.offset, [list(ap.ap[0])] + [list(d) for d in dims])


NFFT = 2 * L


def hyena_tables():
    j = np.arange(128)
    ang = 2.0 * np.pi * np.outer(j, j) / 128.0
    Wr, Wi = np.cos(ang), -np.sin(ang)
    W1s = np.concatenate([np.concatenate([Wr[:64], Wi[:64]], 1), np.concatenate([-Wi[:64], Wr[:64]], 1)], 0)
    W1k = np.concatenate([Wr, Wi], 1)
    Wcr, Wci = np.cos(ang), np.sin(ang)
    W1i = np.stack([np.concatenate([Wcr, Wci], 1), np.concatenate([-Wci, Wcr], 1)], 1)
    tl = np.arange(128)[:, None, None]
    fl = np.arange(128)[None, :, None]
    fh = np.arange(128)[None, None, :]
    a = 2.0 * np.pi * ((tl * (fl + 128 * fh)) % NFFT) / NFFT
    Ef = np.stack([np.cos(a), -np.sin(a)], 2)
    ta = np.arange(64)[None, None, :]
    g = 2.0 * np.pi * ((tl * (fl + 128 * ta)) % NFFT) / NFFT
    Gr, Gi = np.cos(g), np.sin(g)
    Ei = np.stack([np.concatenate([Gr, Gi], 2), np.concatenate([-Gi, Gr], 2)], 2)
    bf = lambda x: np.ascontiguousarray(x.astype(np.float32).astype(ml_dtypes.bfloat16))
    return {"W1s": bf(W1s), "W1k": bf(W1k), "W1i": bf(W1i), "Ef": bf(Ef), "Ei": bf(Ei)}


def emit_PBHY(P, io):
    nc = P.nc
    din = lambda name, shape, dt=F32: io[name]
    hyp_t = io["hyp_t"]
    HB = io["hyp_base"]
    shw_d = din("shw", [1, 3 * 4 * 64])
    w1_d = din("hw1", [33, 64])
    w2_d = din("hw2", [64, 64])
    w3_d = din("hw3", [64, 2, 128])
    mlpv_d = din("mlpv", [64, 3])
    hbias_d = din("hbias", [128, 1])
    ndelta_d = din("ndelta", [128, 1])
    zT_d = din("zT", [2, 33, L])
    tn_d = din("tn", [2, 1, L])
    W1s_d = din("W1s", [128, 256], BF16)
    W1k_d = din("W1k", [128, 256], BF16)
    W1i_d = din("W1i", [128, 2, 256], BF16)
    Ef_d = din("Ef", [128, 128, 2, 128], BF16)
    Ei_d = din("Ei", [128, 128, 2, 128], BF16)
    yhy_t = io["yhy_t"]
    YB = io["yhy_base"]
    kscr_d = io["kscr"]
    bscr_d = io["bscr"]
    kscr_t = kscr_d.tensor
    with ExitStack() as st:
        P.cur = st
        V = lambda fn, r, w: P.op("dve", fn, r=r, w=w)
        G = lambda fn, r, w: P.op("pool", fn, r=r, w=w)
        A_ = lambda fn, r, w: P.op("act", fn, r=r, w=w)
        gates = [P.sb([128, 64, 128], BF16, "gate%d" % i) for i in range(3)]
        W1s = P.sb([128, 256], BF16, "W1s")
        W1k = P.sb([128, 256], BF16, "W1k")
        W1i = P.sb([128, 2, 256], BF16, "W1i")
        P.dma("sp", W1s[:], W1s_d, r=[], w=["W1s"], key="W1s")
        P.dma("sp", W1k[:], W1k_d, r=[], w=["W1k"], key="W1k")
        P.dma("sp", W1i[:], W1i_d, r=[], w=["W1i"], key="W1i")
        biasb = P.sb([128, 128], F32, "biasb")
        with ExitStack() as st2:
            w1 = P.sb([33, 64], F32, "hk_w1", st2)
            w2 = P.sb([64, 64], F32, "hk_w2", st2)
            w3 = P.sb([64, 2, 128], F32, "hk_w3", st2)
            mlpv = P.sb([64, 3], F32, "mlpv", st2)
            fsc = P.sb([64, 4], F32, "fsc", st2)
            hbias = P.sb([128, 1], F32, "hbias", st2)
            ndelta = P.sb([128, 1], F32, "ndelta", st2)
            for (t_, d_, n_) in ((w1, w1_d, "hk_w1"), (w2, w2_d, "hk_w2"), (w3, w3_d, "hk_w3"), (mlpv, mlpv_d, "mlpv"), (hbias, hbias_d, "hbias"),
                                 (ndelta, ndelta_d, "ndelta")):
                P.dma("sp", t_[:], d_, r=[], w=[n_], key=n_)
            V(lambda e: e.tensor_scalar(out=fsc[:, 0:1], in0=mlpv[:, 2:3], scalar1=1.0 / TWO_PI, scalar2=None, op0=ALU.mult), ["mlpv"], ["fsc"])
            V(lambda e: e.tensor_tensor(out=fsc[:, 1:2], in0=fsc[:, 0:1], in1=mlpv[:, 0:1], op=ALU.mult), ["mlpv", "fsc"], ["fsc"])
            V(lambda e: e.tensor_tensor(out=fsc[:, 2:3], in0=fsc[:, 0:1], in1=mlpv[:, 1:2], op=ALU.mult), ["mlpv", "fsc"], ["fsc"])
            hk32 = [P.sb([128, L], F32, "hk32_%d" % i, st2) for i in range(2)]
            hk16 = P.sb([128, L], BF16, "hk16", st2)
            zt = [P.sb([33, 512], F32, "zt%d" % i, st2) for i in range(2)]
            tnb = [P.sb([128, 512], F32, "tnb%d" % i, st2) for i in range(2)]
            tqs = [P.sb([64, 512], F32, "tqm%d" % i, st2) for i in range(2)]
            trs = [P.sb([64, 512], F32, "trm%d" % i, st2) for i in range(2)]
            h1s = [P.sb([64, 512], F32, "h1_%d" % i, st2) for i in range(2)]
            h2s = [P.sb([64, 512], F32, "h2_%d" % i, st2) for i in range(2)]
            decs = [P.sb([128, 512], F32, "dec%d" % i, st2) for i in range(2)]
            asum = P.sb([128, 32], F32, "asum", st2)
            nrm = P.sb([128, 4], F32, "nrm", st2)
            psm1s = [P.ps([64, 512], F32, "psm1_%d" % i, st2) for i in range(2)]
            psm2s = [P.ps([64, 512], F32, "psm2_%d" % i, st2) for i in range(2)]
            psm3s = [P.ps([128, 512], F32, "psm3_%d" % i, st2) for i in range(2)]

            def mlp_tile(i, sx):
                d, tt = i // 16, i % 16
                tq_, tr_, h1_, h2_, dec_, p1, p2, p3 = tqs[sx], trs[sx], h1s[sx], h2s[sx], decs[sx], psm1s[sx], psm2s[sx], psm3s[sx]
                sfx = "_%d" % sx
                z_ = zt[sx]
                zn = "zt%d" % sx
                tb_ = tnb[sx]
                tbn = "tnb%d" % sx
                cs_ = slice(tt * 512, (tt + 1) * 512)

                def sin_layer(ps, psn, bcol, hout, hn):
                    V(lambda e: e.tensor_scalar(out=tq_[:], in0=ps[:], scalar1=fsc[:, 0:1], scalar2=fsc[:, bcol:bcol + 1], op0=ALU.mult, op1=ALU.add),
                      [psn, "fsc"], ["tqm" + sfx])
                    V(lambda e: e.tensor_scalar(out=tr_[:], in0=tq_[:], scalar1=MAGIC, scalar2=MAGIC, op0=ALU.add, op1=ALU.subtract),
                      ["tqm" + sfx], ["trm" + sfx])
                    V(lambda e: e.tensor_tensor(out=tr_[:], in0=tq_[:], in1=tr_[:], op=ALU.subtract), ["tqm" + sfx, "trm" + sfx], ["trm" + sfx])
                    A_(lambda e: e.activation(out=hout[:], in_=tr_[:], func=AF.Sin, scale=TWO_PI), ["trm" + sfx], [hn])

                P.dma("sp", z_[:], zT_d[d, :, cs_], r=[], w=[zn], key=zn)
                P.dma("sp", tb_[:], tn_d[d, :, cs_].partition_broadcast(128), r=[], w=[tbn], key=tbn)
                P.op("pe", lambda e: e.matmul(p1[:], w1[:], z_[:], start=True, stop=True), r=["hk_w1", zn], w=["psm1" + sfx])
                yield
                sin_layer(p1, "psm1" + sfx, 1, h1_, "h1" + sfx)
                yield
                P.op("pe", lambda e: e.matmul(p2[:], w2[:], h1_[:], start=True, stop=True), r=["hk_w2", "h1" + sfx], w=["psm2" + sfx])
                yield
                sin_layer(p2, "psm2" + sfx, 2, h2_, "h2" + sfx)
                yield
                P.op("pe", lambda e: e.matmul(p3[:], w3[:, d, :], h2_[:], start=True, stop=True), r=["hk_w3", "h2" + sfx], w=["psm3" + sfx])
                yield
                A_(lambda e: e.activation(out=dec_[:], in_=tb_[:], func=AF.Exp, scale=ndelta[:, 0:1]), [tbn, "ndelta"], ["dec" + sfx])
                V(lambda e: e.tensor_tensor(out=hk32[d][:, cs_], in0=p3[:], in1=dec_[:], op=ALU.mult), ["psm3" + sfx, "dec" + sfx], [("hk32", d, tt)])
                V(lambda e: e.tensor_reduce(out=asum[:, i:i + 1], in_=hk32[d][:, cs_], axis=mybir.AxisListType.X, op=ALU.add,
                                            apply_absolute_value=True),
                  [("hk32", d, tt)], ["asum"])

            for i in range(0, 32, 2):
                gens = [mlp_tile(i, 0), mlp_tile(i + 1, 1)]
                live = list(gens)
                while live:
                    for g_ in list(live):
                        try:
                            next(g_)
                        except StopIteration:
                            live.remove(g_)
            V(lambda e: e.tensor_reduce(out=nrm[:, 0:1], in_=asum[:], axis=mybir.AxisListType.X, op=ALU.add), ["asum"], ["nrm"])
            V(lambda e: e.tensor_scalar(out=nrm[:, 0:1], in0=nrm[:, 0:1], scalar1=1e-6, scalar2=None, op0=ALU.add), ["nrm"], ["nrm"])
            V(lambda e: e.reciprocal(out=nrm[:, 1:2], in_=nrm[:, 0:1]), ["nrm"], ["nrm"])
            V(lambda e: e.scalar_tensor_tensor(out=nrm[:, 2:3], in0=hk32[1][:, 0:1], scalar=nrm[:, 1:2], in1=hbias[:], op0=ALU.mult, op1=ALU.add),
              ["nrm", "hbias", ("hk32", 1, 0)], ["nrm"])
            P.dma("sp", bscr_d.rearrange("o p -> p o"), nrm[:, 2:3], r=["nrm"], w=["bscr"], key="bscr")
            for d in range(2):
                for q in range(4):
                    cs_ = slice(q * 2048, (q + 1) * 2048)
                    eng = V
                    eng(lambda e, d=d, cs_=cs_: e.tensor_scalar(out=hk16[:, cs_], in0=hk32[d][:, cs_], scalar1=nrm[:, 1:2], scalar2=None, op0=ALU.mult),
                        ["nrm"] + [("hk32", d, tt) for tt in range(4 * q, 4 * q + 4)], [("hk16", q)])
                    P.dma("sp", kscr_d[d, :, cs_], hk16[:, cs_], r=[("hk16", q)], w=[("kscr", d)], key=("hk16", q))
            P.dma("sp", biasb[:], bscr_d.partition_broadcast(128), r=["bscr"], w=["biasb"], key="biasb")
            P.barrier()
        with ExitStack() as st3:
            shw = P.sb([128, 3 * 4 * 64], F32, "shw", st3)
            P.dma("sp", shw[:], shw_d.partition_broadcast(128), r=[], w=["shw"], key="shw")
            Z = P.sb([128, 64, 130], BF16, "Z", st3)
            c1 = P.sb([128, 64, 128], F32, "c1", st3)
            c2 = P.sb([128, 64, 128], F32, "c2", st3)
            for comp in range(3):
                ZK = [("Z", b_, p_) for b_ in range(2) for p_ in range(3)]
                V(lambda e: e.memset(Z[:, :, 0:1], 0.0), [], ["Zh"] + ZK)
                V(lambda e: e.memset(Z[:, :, 129:130], 0.0), [], ["Zh"] + ZK)
                for b in range(2):
                    off = HB + (comp * 64 * 2 + b) * L
                    p0 = 64 * b
                    src0 = AP(hyp_t, off, [[L, 1], [2 * L, 64], [1, 129]])
                    zq = "sp" if b == 0 else "act"
                    P.dma(zq, Z[p0:p0 + 1, :, 1:130], src0, r=["Zh"], w=[("Z", b, 0)], key=("Z", b, 0))
                    src1 = AP(hyp_t, off + 127, [[128, 62], [2 * L, 64], [1, 130]])
                    P.dma(zq, Z[p0 + 1:p0 + 63, :, 0:130], src1, r=["Zh"], w=[("Z", b, 1)], key=("Z", b, 1))
                    src2 = AP(hyp_t, off + 63 * 128 - 1, [[L, 1], [2 * L, 64], [1, 129]])
                    P.dma(zq, Z[p0 + 63:p0 + 64, :, 0:129], src2, r=["Zh"], w=[("Z", b, 2)], key=("Z", b, 2))
                wb = lambda tap: fview(shw[:, (comp * 4 + tap) * 64:(comp * 4 + tap) * 64 + 64], [[1, 64], [0, 128]])
                V(lambda e: e.tensor_tensor(out=c1[:], in0=Z[:, :, 0:128], in1=wb(0), op=ALU.mult), ZK + ["shw"], ["c1"])
                V(lambda e: e.tensor_tensor(out=c2[:], in0=Z[:, :, 1:129], in1=wb(1), op=ALU.mult), ZK + ["shw"], ["c2"])
                V(lambda e: e.tensor_tensor(out=c1[:], in0=c1[:], in1=c2[:], op=ALU.add), ["c1", "c2"], ["c1"])
                V(lambda e: e.tensor_tensor(out=c2[:], in0=Z[:, :, 2:130], in1=wb(2), op=ALU.mult), ZK + ["shw", "c1"], ["c2"])
                V(lambda e: e.tensor_tensor(out=c1[:], in0=c1[:], in1=c2[:], op=ALU.add), ["c1", "c2"], ["c1"])
                V(lambda e, comp=comp: e.tensor_tensor(out=gates[comp][:], in0=c1[:], in1=wb(3), op=ALU.add), ["c1", "shw"], [("gate", comp)])
            P.barrier()
        AB = P.sb([128, 128 * 3 * 64], BF16, "AB")
        A_sb = fview(AB[:], [[192, 128], [64, 3], [1, 64]])
        B_sb = fview(AB[:], [[128, 128], [64, 2], [1, 64]])
        YK = P.sb([128, 2 * 64 * 128], BF16, "YK")
        Y_sb = fview(YK[:], [[64 * 128, 2], [128, 64], [1, 128]])
        kt = fview(YK[:], [[128, 64], [1, 128]])
        KH = P.sb([128, 128, 2, 64], BF16, "KH")
        Ech = [P.sb([128, 8, 2, 128], BF16, "Ech%d" % i) for i in range(2)]
        mt = [P.sb([128, 4, 64], F32, "mtm%d" % i) for i in range(4)]
        s1 = P.sb([128, 8, 64], F32, "s1")
        s2 = P.sb([128, 8, 64], F32, "s2")
        psA = [P.ps([128, 256], F32, "psA%d" % i) for i in range(2)]
        ps3 = [P.ps([128, 4, 128], F32, "ps3_%d" % i) for i in range(2)]
        psI = [P.ps([128, 8, 64], F32, "psI%d" % i) for i in range(2)]
        nE = [0]

        def load_E(tab_d, g):
            j = nE[0] % 2
            nE[0] += 1
            P.dma("sp", Ech[j][:], tab_d[:, 8 * g:8 * g + 8], r=[], w=["Ech%d" % j], key="Ech%d" % j)
            return Ech[j], "Ech%d" % j

        def evacA(pa, pn, c):
            A_(lambda e: e.activation(out=AB[:, 384 * c + 128:384 * c + 384], in_=pa[:, 0:256], func=AF.Copy), [pn], ["AB"])
            V(lambda e: e.tensor_scalar(out=AB[:, 384 * c:384 * c + 128], in0=pa[:, 128:256], scalar1=-1.0, scalar2=None, op0=ALU.mult),
              [pn], ["AB"])

        def fwd_step1(lhs_fn, lhs_key, wtab, wname):
            for c in range(64):
                pa = psA[c % 2]
                pn = "psA%d" % (c % 2)
                P.op("pe", lambda e, pa=pa, c=c: e.matmul(pa[:], lhs_fn(c), wtab[:], start=True, stop=True), r=[lhs_key, wname], w=[pn])
                evacA(pa, pn, c)

        A4 = AB[:, :].rearrange("p (f s c) -> p f s c", s=3, c=64)
        B4 = AB[:, 0:128 * 2 * 64].rearrange("p (t r c) -> p t r c", r=2, c=64)
        Y4 = YK[:, :].rearrange("p (r c f) -> p r c f", r=2, c=64)
        K3 = YK[:, 0:64 * 128].rearrange("p (c t) -> p c t", c=64)

        def fwd_step3(evac):
            for g in range(16):
                Et, En = load_E(Ef_d, g)
                for q2 in range(2):
                    pg = ps3[(2 * g + q2) % 2]
                    pgn = "ps3_%d" % ((2 * g + q2) % 2)
                    for q in range(4):
                        j = q2 * 4 + q
                        fl = 8 * g + j
                        P.op("pe", lambda e, pg=pg, q=q, j=j, fl=fl, Et=Et: e.matmul(pg[:, q, :], Et[:, j, 0, :], fview(AB[:, fl + 128:fl + 129], [[128, 2], [384, 64]]), start=True, stop=False),
                             r=[En, "AB"], w=[pgn])
                        P.op("pe", lambda e, pg=pg, q=q, j=j, fl=fl, Et=Et: e.matmul(pg[:, q, :], Et[:, j, 1, :], fview(AB[:, fl:fl + 1], [[128, 2], [384, 64]]), start=False, stop=True),
                             r=[En, "AB"], w=[pgn])
                    evac(pg, pgn, 8 * g + 4 * q2)

        def evac_filter(pg, pgn, fl0):
            A_(lambda e: e.activation(out=KH[:, fl0:fl0 + 4, :, :], in_=pg[:, :, :].rearrange("p q (r c) -> p q r c", r=2), func=AF.Copy,
                                      scale=1.0 / NFFT),
               [pgn], ["KH"])

        def evac_mac(pg, pgn, fl0):
            Xr = pg[:, :, 0:64]
            Xi = pg[:, :, 64:128]
            Kr = KH[:, fl0:fl0 + 4, 0, :]
            Ki = KH[:, fl0:fl0 + 4, 1, :]
            V(lambda e: e.tensor_tensor(out=mt[0][:], in0=Xr, in1=Kr, op=ALU.mult), [pgn, "KH"], ["mtm0"])
            V(lambda e: e.tensor_tensor(out=mt[1][:], in0=Xi, in1=Ki, op=ALU.mult), [pgn, "KH"], ["mtm1"])
            V(lambda e: e.tensor_tensor(out=Y4[:, 0, :, fl0:fl0 + 4].rearrange("p c f -> p f c"), in0=mt[0][:], in1=mt[1][:], op=ALU.subtract),
              ["mtm0", "mtm1"], ["YK"])
            V(lambda e: e.tensor_tensor(out=mt[2][:], in0=Xr, in1=Ki, op=ALU.mult), [pgn, "KH"], ["mtm2"])
            V(lambda e: e.tensor_tensor(out=mt[3][:], in0=Xi, in1=Kr, op=ALU.mult), [pgn, "KH"], ["mtm3"])
            V(lambda e: e.tensor_tensor(out=Y4[:, 1, :, fl0:fl0 + 4].rearrange("p c f -> p f c"), in0=mt[2][:], in1=mt[3][:], op=ALU.add),
              ["mtm2", "mtm3"], ["YK"])

        def inverse(src, srcname, gate, gatename, dst, dstname, o):
            for c in range(64):
                pa = psA[c % 2]
                pn = "psA%d" % (c % 2)
                P.op("pe", lambda e, pa=pa, c=c: e.matmul(pa[:], Y4[:, 0, c, :], W1i[:, 0, :], start=True, stop=False), r=["YK", "W1i"], w=[pn])
                P.op("pe", lambda e, pa=pa, c=c: e.matmul(pa[:], Y4[:, 1, c, :], W1i[:, 1, :], start=False, stop=True), r=["YK", "W1i"], w=[pn])
                A_(lambda e, pa=pa, c=c: e.activation(out=AB[:, 256 * c:256 * c + 256], in_=pa[:, 0:256], func=AF.Copy), [pn], ["AB"])
            for g in range(16):
                Et, En = load_E(Ei_d, g)
                pg = psI[g % 2]
                pgn = "psI%d" % (g % 2)
                for j in range(8):
                    tb = 8 * g + j
                    P.op("pe", lambda e, pg=pg, j=j, tb=tb, Et=Et: e.matmul(pg[:, j, :], Et[:, j, 0, :], fview(AB[:, tb:tb + 1], [[256, 64]]), start=True, stop=False),
                         r=[En, "AB"], w=[pgn])
                    P.op("pe", lambda e, pg=pg, j=j, tb=tb, Et=Et: e.matmul(pg[:, j, :], Et[:, j, 1, :], fview(AB[:, 128 + tb:128 + tb + 1], [[256, 64]]), start=False, stop=True),
                         r=[En, "AB"], w=[pgn])
                tsl = slice(8 * g, 8 * g + 8)
                sv = src[:, :, tsl].rearrange("p c t -> p t c")
                gv = gate[:, :, tsl].rearrange("p c t -> p t c")
                dv = dst[:, :, tsl].rearrange("p c t -> p t c")
                bb = fview(biasb[:, 64 * o:64 * o + 64], [[0, 8], [1, 64]])
                V(lambda e, sv=sv, bb=bb: e.tensor_tensor(out=s1[:], in0=sv, in1=bb, op=ALU.mult), [(srcname, g), "biasb"], ["s1"])
                V(lambda e, pg=pg: e.tensor_tensor(out=s2[:], in0=pg[:], in1=s1[:], op=ALU.add), [pgn, "s1"], ["s2"])
                V(lambda e, gv=gv, dv=dv: e.tensor_tensor(out=dv, in0=s2[:], in1=gv, op=ALU.mult), ["s2", (gatename, g)], [(dstname, g)])

        gk = lambda name: [(name, g) for g in range(16)]
        for comp, nm in ((0, "gout"), (1, "gmid"), (2, "vv")):
            for g in range(16):
                P.lastw[(nm, g)] = P.lastw.get(("gate", comp))
                P.readers[(nm, g)] = []
        for o in range(2):
            for dd in range(2):
                srck = AP(kscr_t, (dd * 128 + o * 64) * L, [[128, 64], [L, 64], [1, 128]])
                P.dma("sp", K3[64 * dd:64 * dd + 64, :, :], srck, r=[("kscr", dd)], w=["YK"], key=("kt", dd))
            fwd_step1(lambda c: K3[:, c, :], "YK", W1k, "W1k")
            fwd_step3(evac_filter)
            if o == 0:
                src, srcname, gate, gatename, dst, dstname = gates[2], "vv", gates[1], "gmid", gates[2], "vv"
            else:
                src, srcname, gate, gatename, dst, dstname = gates[2], "vv", gates[0], "gout", gates[1], "gmid"
            for c in range(64):
                pa = psA[c % 2]
                pn = "psA%d" % (c % 2)
                P.op("pe", lambda e, pa=pa, c=c: e.matmul(pa[:], src[:, c, :], W1s[:], start=True, stop=True), r=gk(srcname) + ["W1s"], w=[pn])
                evacA(pa, pn, c)
            fwd_step3(evac_mac)
            inverse(src, srcname, gate, gatename, dst, dstname, o)
        outst = gates[1]
        for b in range(2):
            dsto = AP(yhy_t, YB + b * L, [[128, 64], [2 * L, 64], [1, 128]])
            P.dma("act", dsto, outst[64 * b:64 * b + 64, :, :], r=gk("gmid"), w=[], key=("yout", b), final=True)
        P.barrier()
        P.cur = P.st


HY_DELTAS = np.abs(np.linspace(math.log(1e-2) / 1.5, math.log(1e-2) / 0.3, 512, dtype=np.float32)).astype(np.float32)


def hyena_pos_tables():
    f32 = np.float32
    t_norm = np.linspace(0.0, 1.0, L, dtype=f32)
    bands = np.linspace(1e-4, 15.0, 16, dtype=f32)[None, :]
    ang = f32(2.0 * math.pi / L) * np.arange(L, dtype=f32)[:, None] * bands
    z = np.concatenate([t_norm[:, None], np.cos(ang), -np.sin(ang)], axis=-1).astype(f32)
    idx = (L - np.arange(L)) % L
    zT = np.ascontiguousarray(np.stack([z.T, z[idx].T], 0))
    tn = np.ascontiguousarray(np.stack([t_norm[None, :], t_norm[idx][None, :]], 0))
    return zT, tn


_HY_CONST = {}


def hyena_inputs(p_hy, short_w, short_b, w1, b1, w2, b2, w3, freq, bias):
    if not _HY_CONST:
        _HY_CONST.update(hyena_tables())
        zT, tn = hyena_pos_tables()
        _HY_CONST["zT"] = zT
        _HY_CONST["tn"] = tn
    ims = []
    for c in range(NCORES):
        ch = slice(64 * c, 64 * c + 64)
        hyp = None if p_hy is None else np.ascontiguousarray(np.stack([p_hy[:, :, comp * 512 + 64 * c:comp * 512 + 64 * c + 64] for comp in range(3)], 0).transpose(0, 3, 1, 2))
        shw = np.zeros((3, 4, 64), np.float32)
        for comp in range(3):
            shw[comp, 0:3] = short_w[:, comp * 512 + 64 * c:comp * 512 + 64 * c + 64]
            shw[comp, 3] = short_b[comp * 512 + 64 * c:comp * 512 + 64 * c + 64]
        hw3 = np.zeros((64, 2, 128), np.float32)
        for o in range(2):
            for d in range(2):
                hw3[:, d, o * 64:(o + 1) * 64] = w3[:, o * 1024 + d * 512 + 64 * c:o * 1024 + d * 512 + 64 * c + 64]
        d = {"hyp": hyp, "shw": shw.reshape(1, -1), "hw1": np.ascontiguousarray(w1), "hw2": np.ascontiguousarray(w2), "hw3": hw3,
             "mlpv": np.ascontiguousarray(np.stack([b1, b2, freq], 1)),
             "hbias": np.ascontiguousarray(bias[:, ch].reshape(128, 1)),
             "ndelta": np.ascontiguousarray(-np.tile(HY_DELTAS[ch], 2).reshape(128, 1))}
        d.update(_HY_CONST)
        ims.append(d)
    return ims


def hyena_outputs(res):
    return np.concatenate([np.asarray(res[c]["yhy"]).transpose(1, 2, 0) for c in range(NCORES)], axis=-1)


I32 = mybir.dt.int32


def _msel_table():
    t = np.arange(L)
    m = np.zeros((4, 4, L), np.float32)
    for q, win in enumerate((2, 4, 8, 16)):
        h = win // 2
        cnt = (np.minimum(t + h, L) - np.maximum(t - h, 0)).astype(np.float32)
        m[q, q] = (1.0 / cnt).astype(np.float32)
    return m


def emit_regather(P, p_all, idxp_d, pm):
    with ExitStack() as st:
        P.cur = st
        idx = P.sb([128, 16], I32, "idxp")
        P.dma("sp", idx[:], idxp_d, r=[], w=["idxp"], key="idxp")
        gb = [P.sb([128, 2048], BF16, "gb%d" % i) for i in range(4)]
        for j in range(16):
            rg, blk = j // 4, j % 4
            g_ = gb[j % 4]
            gn = "gb%d" % (j % 4)
            P.op("pool", lambda g, g_=g_, j=j: g.indirect_dma_start(out=g_[:], out_offset=None, in_=p_all,
                                                                   in_offset=bass.IndirectOffsetOnAxis(ap=idx[:, j:j + 1], axis=0)),
                 r=["idxp", "p_all"], w=[gn], dma=gn)
            P.dma("act", pm[128 * rg:128 * rg + 128, 2048 * blk:2048 * blk + 2048], g_[:], r=[gn], w=["pm"], key=gn + "o")
        P.barrier()
        P.cur = P.st


def build_fused():
    nc = bass.Bass("TRN2", target_bir_lowering=False)
    ext = lambda name, shape, dt=F32: nc.dram_tensor(name, list(shape), dt, kind="ExternalInput").ap()
    itn = lambda name, shape, dt: nc.dram_tensor(name, list(shape), dt, kind="Internal").ap()
    xin = ext("xT", [D, NT])
    memT = ext("memT", [D, NMEM])
    xout = nc.dram_tensor("xoT", [D, NT], F32, kind="ExternalOutput").ap()
    idxp = [ext("idxp_e", [128, 16], I32), ext("idxp_o", [128, 16], I32)]
    idxy = [ext("idxy_e", [128, 32], I32), ext("idxy_o", [128, 32], I32)]
    xbuf = itn("xbuf", [D, NT], F32)
    p_loc = itn("p_loc", [2048, NT], BF16)
    p_all = itn("p_all", [NCORES * 2048, NT], BF16)
    pm = itn("pm", [512, L], BF16)
    y_loc = itn("y_loc", [4096, 512], BF16)
    y_all = itn("y_all", [NCORES * 4096, 512], BF16)
    ys0 = itn("ys0", [2, 32, 2, L], BF16)
    hal_loc = itn("hal_loc", [2048, 16], BF16)
    hal_all = itn("hal_all", [NCORES * 2048, 16], BF16)
    yT_loc = itn("yT_loc", [D, NT], BF16)
    idxh_d = ext("idxh", [128, 32], I32)
    hmask_d = ext("hmask", [128, 2])
    invc_d = ext("invc", [4, NT])
    kscr = itn("kscr", [2, 128, L], BF16)
    bscr = itn("bscr", [1, 128], F32)
    yl = y_loc.rearrange("(r a) t -> r (a t)", a=16)
    W = []
    for i in range(DEPTH):
        odd = i % 2 == 1
        d = {"w_in": ext("w_in%d" % i, [128, 8, 2048]), "g_in": ext("g_in%d" % i, [128, 8]),
             "wout": ext("wout%d" % i, [128, 8, 1024]), "wq": ext("wq%d" % i, [128, 8, 1024]), "wk": ext("wk%d" % i, [128, 8, 1024]),
             "wv": ext("wv%d" % i, [128, 8, 1024]), "wo": ext("wo%d" % i, [128, 8, 1024]), "gains": ext("gains%d" % i, [128, 4, 8]),
             "w1": ext("w1_%d" % i, [128, 8, 4096]), "w2": ext("w2_%d" % i, [128, 32, 1024]),
             "g0": ext("g0_%d" % i, [128, 8]), "g1": ext("g1_%d" % i, [128, 8])}
        if odd:
            d.update({"wglu": ext("wglu%d" % i, [128, 4, 512]),
                      "par": ext("par%d" % i, [2, 2, 128, 4]), "bmat": ext("bmat%d" % i, [2, 128, 2, 16]),
                      "cmat": ext("cmat%d" % i, [2, 2, 128, 2, 16]), "dsk": ext("dsk%d" % i, [2, 32, 1]),
                      "shw": ext("shw%d" % i, [1, 3 * 4 * 64]), "hw1": ext("hw1_%d" % i, [33, 64]), "hw2": ext("hw2_%d" % i, [64, 64]),
                      "hw3": ext("hw3_%d" % i, [64, 2, 128]), "mlpv": ext("mlpv%d" % i, [64, 3]), "hbias": ext("hbias%d" % i, [128, 1])})
        else:
            d.update({"poolw4": ext("poolw4_%d" % i, [128, 4, 128]), "pscale4": ext("pscale4_%d" % i, [128, 4]),
                      "convw4": ext("convw4_%d" % i, [128, 4, 3])})
        W.append(d)
    ident = ext("ident", [128, 128])
    hyc = {"ndelta": ext("ndelta", [128, 1]), "zT": ext("zT", [2, 33, L]), "tn": ext("tn", [2, 1, L]),
           "W1s": ext("W1s", [128, 256], BF16), "W1k": ext("W1k", [128, 256], BF16), "W1i": ext("W1i", [128, 2, 256], BF16),
           "Ef": ext("Ef", [128, 128, 2, 128], BF16), "Ei": ext("Ei", [128, 128, 2, 128], BF16)}
    with ExitStack() as st:
        P = Prog(nc, st)
        for i in range(DEPTH):
            odd = i % 2 == 1
            w = W[i]
            if not odd:
                emit_PA(P, {"xT": xin if i == 0 else xbuf, "w": w["w_in"], "g": w["g_in"], "pT": p_loc, "hal": hal_loc})
                P.allgather(hal_loc, hal_all, r=[], w=["hal_all"])
                emit_PBE_tok(P, {"p_loc": p_loc, "hal_all": hal_all, "yT": yT_loc, "idxh": idxh_d, "hmask": hmask_d, "invc": invc_d,
                                 "poolw4": w["poolw4"], "pscale4": w["pscale4"], "convw4": w["convw4"]})
            else:
                emit_PA(P, {"xT": xin if i == 0 else xbuf, "w": w["w_in"], "g": w["g_in"], "pT": p_loc})
                P.allgather(p_loc, p_all, r=[], w=["p_all"])
                emit_regather(P, p_all, idxp[1], pm)
                emit_PBS5(P, {"u": pm[0:128].rearrange("(g i b) t -> g i b t", g=2, b=2), "par": w["par"], "bmat": w["bmat"],
                              "cmat": w["cmat"], "dsk": w["dsk"], "ident": ident,
                              "ys": yl[0:128].rearrange("(g i b) t -> g i b t", g=2, b=2), "ys0": ys0})
                hio = {"hyp_t": pm.tensor, "hyp_base": 128 * L, "yhy_t": y_loc.tensor, "yhy_base": 128 * L, "kscr": kscr, "bscr": bscr,
                       "shw": w["shw"], "hw1": w["hw1"], "hw2": w["hw2"], "hw3": w["hw3"], "mlpv": w["mlpv"], "hbias": w["hbias"]}
                hio.update(hyc)
                emit_PBHY(P, hio)
            if odd:
                P.allgather(y_loc, y_all, r=[], w=["y_all"])
            pio = {"xT": xin if i == 0 else xbuf, "xoT": xbuf, "y_all": y_all if odd else None, "idx_y": idxy[1] if odd else None,
                   "yT": None if odd else yT_loc, "memT": memT,
                   "wout": w["wout"], "wq": w["wq"], "wk": w["wk"], "wv": w["wv"], "wo": w["wo"], "gains": w["gains"]}
            if odd:
                pio["wglu"] = w["wglu"]
            emit_PC1(P, pio, odd)
            emit_PC2(P, {"xT": xbuf, "w1": w["w1"], "w2": w["w2"], "g0": w["g0"], "g1": w["g1"],
                         "xoT": xout if i == DEPTH - 1 else xbuf})
        P.finish()
    return nc


def _idx_tables(c):
    kb, kq = c // 4, c % 4
    p = np.arange(128)
    idxp_e = np.zeros((128, 16), np.int32)
    idxp_o = np.zeros((128, 16), np.int32)
    for rg in range(4):
        for blk in range(4):
            j = rg * 4 + blk
            if rg == 0:
                idxp_e[:, j] = (4 * (c % 2) + blk) * 2048 + 128 * (c // 2) + p
            else:
                idxp_e[:, j] = (4 * (p // 64) + blk) * 2048 + 512 * rg + 64 * c + (p % 64)
            chl = 64 * rg + p // 2
            b = p % 2
            chan = np.where(chl < 64, 64 * c + chl, 512 + ((chl - 64) // 64) * 512 + 64 * c + ((chl - 64) % 64))
            idxp_o[:, j] = (4 * b + blk) * 2048 + chan
    idxy_e = np.zeros((128, 32), np.int32)
    idxy_o = np.zeros((128, 32), np.int32)
    for k in range(8):
        m = 128 * k + p
        for t in range(4):
            blk16 = kq * 4 + t
            if k < 4:
                src_o, row_o = m // 64, 2 * (m % 64) + kb
                src_e, row_e = 2 * (m // 128) + kb, m % 128
            else:
                mm = m - 512
                src_o, row_o = mm // 64, 128 + 2 * (mm % 64) + kb
                src_e, row_e = mm // 64, 128 + kb * 64 + (mm % 64)
            idxy_o[:, 4 * k + t] = src_o * 4096 + row_o * 16 + blk16
            idxy_e[:, 4 * k + t] = src_e * 4096 + row_e * 16 + blk16
    return idxp_e, idxp_o, idxy_e, idxy_o


def kernel(x, mem, norm_mix, norm_xattn, norm_mem, norm_mlp, xa_wq, xa_wk, xa_wv, xa_wo, mlp_w1, mlp_w2, ev_w_in, ev_pool_w,
           ev_pool_scale, ev_conv_w, ev_w_out, od_w_in, od_s5_lambda_re, od_s5_lambda_im, od_s5_log_dt, od_s5_b_re, od_s5_b_im,
           od_s5_c_re, od_s5_c_im, od_s5_d, od_s5_w_glu, od_hy_short_w, od_hy_short_b, od_hy_w1, od_hy_b1, od_hy_w2, od_hy_b2,
           od_hy_w3, od_hy_freq, od_hy_bias, od_w_out):
    f = lambda a: np.ascontiguousarray(np.asarray(a, dtype=np.float32))
    x = f(x).reshape(NTOK, D)
    mem = f(mem)
    nc = prog("fused", build_fused)
    tabs = hyena_tables()
    zT, tn = hyena_pos_tables()
    ims = [dict() for _ in range(NCORES)]
    shared = {"ident": np.eye(128, dtype=np.float32), "zT": zT, "tn": tn}
    shared.update(tabs)
    for i in range(DEPTH):
        j = i // 2
        odd = i % 2 == 1
        shared["w_in%d" % i] = wlay(f(od_w_in[j] if odd else ev_w_in[j]))
        shared["g_in%d" % i] = glay(f(norm_mix[i, 0]))
        shared["wout%d" % i] = wlay(f(od_w_out[j] if odd else ev_w_out[j]))
        for nm, arr in (("wq", xa_wq), ("wk", xa_wk), ("wv", xa_wv), ("wo", xa_wo)):
            shared["%s%d" % (nm, i)] = wlay(f(arr[i]))
        shared["gains%d" % i] = np.ascontiguousarray(np.stack([glay(f(norm_mix[i, 1])), glay(f(norm_xattn[i, 0])), glay(f(norm_xattn[i, 1])),
                                                               glay(f(norm_mem[i]))], axis=1))
        shared["w1_%d" % i] = wlay(f(mlp_w1[i]))
        shared["w2_%d" % i] = wlay(f(mlp_w2[i]))
        shared["g0_%d" % i] = glay(f(norm_mlp[i, 0]))
        shared["g1_%d" % i] = glay(f(norm_mlp[i, 1]))
        if odd:
            shared["wglu%d" % i] = wlay(f(od_s5_w_glu[j]))
            s5i = s5_inputs(None, f(od_s5_lambda_re[j]),
                            f(od_s5_lambda_im[j]), f(od_s5_log_dt[j]), f(od_s5_b_re[j]), f(od_s5_b_im[j]), f(od_s5_c_re[j]), f(od_s5_c_im[j]),
                            f(od_s5_d[j]))
            hyi = hyena_inputs(None, f(od_hy_short_w[j]), f(od_hy_short_b[j]), f(od_hy_w1[j]), f(od_hy_b1[j]), f(od_hy_w2[j]),
                               f(od_hy_b2[j]), f(od_hy_w3[j]), f(od_hy_freq[j]), f(od_hy_bias[j]))
            for c in range(NCORES):
                for nm in ("par", "bmat", "cmat", "dsk"):
                    ims[c]["%s%d" % (nm, i)] = s5i[c][nm]
                for nm in ("shw", "mlpv", "hbias"):
                    ims[c]["%s%d" % (nm, i)] = hyi[c][nm]
                ims[c]["hw3_%d" % i] = hyi[c]["hw3"]
                ims[c]["hw1_%d" % i] = hyi[c]["hw1"]
                ims[c]["hw2_%d" % i] = hyi[c]["hw2"]
                ims[c]["ndelta"] = hyi[c]["ndelta"]
        else:
            shared["poolw4_%d" % i] = np.ascontiguousarray(f(ev_pool_w[j]).transpose(1, 0, 2))
            shared["pscale4_%d" % i] = np.ascontiguousarray(f(ev_pool_scale[j]).reshape(4, 128).T)
            shared["convw4_%d" % i] = np.ascontiguousarray(f(ev_conv_w[j]).reshape(3, 4, 128).transpose(2, 1, 0))
    for c in range(NCORES):
        ie, io_, ye, yo = _idx_tables(c)
        kq = c % 4
        p_ = np.arange(128)
        idxh = np.zeros((128, 32), np.int32)
        cl, cr = (c - 1 if kq > 0 else c), (c + 1 if kq < 3 else c)
        for ch in range(16):
            idxh[:, ch] = cl * 2048 + 128 * ch + p_
            idxh[:, 16 + ch] = cr * 2048 + 128 * ch + p_
        hmask = np.zeros((128, 2), np.float32)
        hmask[:, 0] = 1.0 if kq > 0 else 0.0
        hmask[:, 1] = 1.0 if kq < 3 else 0.0
        tt = kq * NT + np.arange(NT)
        invc = np.zeros((4, NT), np.float32)
        for q_, win in enumerate((2, 4, 8, 16)):
            hh = win // 2
            invc[q_] = (1.0 / (np.minimum(tt + hh, L) - np.maximum(tt - hh, 0)).astype(np.float32)).astype(np.float32)
        ims[c].update({"xT": _tok_T(x, c), "memT": np.ascontiguousarray(mem[c // (NCORES // B)].T),
                       "idxp_e": ie, "idxp_o": io_, "idxy_e": ye, "idxy_o": yo, "idxh": idxh, "hmask": hmask, "invc": invc})
        ims[c].update(shared)
    res = run(nc, ims)
    out = np.concatenate([np.asarray(r["xoT"]).T for r in res], axis=0)
    return np.ascontiguousarray(out.reshape(B, L, D).astype(np.float32))


def _tok_T(a, c):
    return np.ascontiguousarray(a[c * NT:(c + 1) * NT].T)
```
